# Optimizing a Trainium2 kernel written in Bass

```python
import math
import jax, jax.numpy as jnp
from jax import lax
import numpy as np

D_MODEL = 1024
BATCH = 4
SEQ = 8192
DEPTH = 2

CHUNK = 64
HEAD_DIM = 64
N_HEADS_FOX = 8
N_HEADS_CHUNK = 8
WIDTH_FOX = N_HEADS_FOX * HEAD_DIM
WIDTH_CHUNK = N_HEADS_CHUNK * HEAD_DIM
LEFT_CHUNKS = 8
BAND_CHUNKS = LEFT_CHUNKS + 1
REL_CLIP = 128
Q_BLOCK = 128
D_FF = 2816
CONV_WIDTH = 3
LN_EPS = 1e-5
N_MOD = 6
PROJ_SIZES = (WIDTH_FOX, WIDTH_FOX, WIDTH_FOX, N_HEADS_FOX,
              WIDTH_CHUNK, WIDTH_CHUNK, WIDTH_CHUNK, D_MODEL, D_MODEL)
PROJ_COLS = sum(PROJ_SIZES)

kernel_name = "fox_chunkattn_gated_hybrid_deepnorm_adaln"


def _layer_norm(x, gain=None, bias=None):
    xf = x.astype(jnp.float32)
    mu = jnp.mean(xf, axis=-1, keepdims=True)
    var = jnp.mean(jnp.square(xf - mu), axis=-1, keepdims=True)
    y = (xf - mu) * lax.rsqrt(var + LN_EPS)
    if gain is not None:
        y = y * gain.astype(jnp.float32) + bias.astype(jnp.float32)
    return y.astype(x.dtype)


def _forgetting_attention(q, k, v, f_logit):
    b, s, h, dh = q.shape
    n_blk = s // Q_BLOCK
    log_f = jax.nn.log_sigmoid(f_logit.astype(jnp.float32))
    cum = jnp.cumsum(log_f, axis=1).transpose(0, 2, 1)
    scale = 1.0 / math.sqrt(dh)
    k_pos = jnp.arange(s)
    q_blocks = q.reshape(b, n_blk, Q_BLOCK, h, dh).transpose(1, 0, 2, 3, 4)
    cum_blocks = cum.reshape(b, h, n_blk, Q_BLOCK).transpose(2, 0, 1, 3)
    neg = jnp.finfo(jnp.float32).min

    def block(args):
        qb, cq, i = args
        q_pos = i * Q_BLOCK + jnp.arange(Q_BLOCK)
        logits = jnp.einsum('bqhd,bkhd->bhqk', qb, k).astype(jnp.float32) * scale
        logits = logits + cq[:, :, :, None] - cum[:, :, None, :]
        mask = k_pos[None, :] <= q_pos[:, None]
        logits = jnp.where(mask[None, None], logits, neg)
        p = jax.nn.softmax(logits, axis=-1).astype(v.dtype)
        return jnp.einsum('bhqk,bkhd->bqhd', p, v)

    out = lax.map(block, (q_blocks, cum_blocks, jnp.arange(n_blk)))
    return out.transpose(1, 0, 2, 3, 4).reshape(b, s, h * dh)


def _chunk_band_attention(q, k, v, rel_table):
    b, s, h, dh = q.shape
    n_c = s // CHUNK
    band = BAND_CHUNKS * CHUNK
    qc = q.reshape(b, n_c, CHUNK, h, dh)
    pad = ((0, 0), (LEFT_CHUNKS, 0), (0, 0), (0, 0), (0, 0))
    kp = jnp.pad(k.reshape(b, n_c, CHUNK, h, dh), pad)
    vp = jnp.pad(v.reshape(b, n_c, CHUNK, h, dh), pad)
    band_idx = jnp.arange(n_c)[:, None] + jnp.arange(BAND_CHUNKS)[None, :]
    kb = kp[:, band_idx].reshape(b, n_c, band, h, dh)
    vb = vp[:, band_idx].reshape(b, n_c, band, h, dh)
    valid = jnp.repeat(band_idx >= LEFT_CHUNKS, CHUNK, axis=1)
    q_off = LEFT_CHUNKS * CHUNK + np.arange(CHUNK)
    rel = np.clip(q_off[:, None] - np.arange(band)[None, :], -REL_CLIP, REL_CLIP) + REL_CLIP
    bias = rel_table[:, rel].astype(jnp.float32)
    scale = 1.0 / math.sqrt(dh)
    logits = jnp.einsum('bcqhd,bckhd->bchqk', qc, kb).astype(jnp.float32) * scale
    logits = logits + bias[None, None]
    logits = jnp.where(valid[None, :, None, None, :], logits, jnp.finfo(jnp.float32).min)
    p = jax.nn.softmax(logits, axis=-1).astype(v.dtype)
    out = jnp.einsum('bchqk,bckhd->bcqhd', p, vb)
    return out.reshape(b, s, h * dh)


def _causal_depthwise_conv(u, w, bias):
    s = u.shape[1]
    up = jnp.pad(u, ((0, 0), (CONV_WIDTH - 1, 0), (0, 0)))
    y = bias
    for j in range(CONV_WIDTH):
        y = y + w[j] * up[:, j:j + s]
    return y


def setup_inputs(seed: int = 0) -> dict:
    key = jax.random.key(seed)
    ks = jax.random.split(key, 20)
    beta = (8.0 * DEPTH) ** -0.25
    f32 = jnp.float32
    nrm = lambda k, shape: jax.random.normal(k, shape, f32)
    col_scale = np.concatenate([
        np.ones(2 * WIDTH_FOX), beta * np.ones(WIDTH_FOX), 0.5 * np.ones(N_HEADS_FOX),
        np.ones(2 * WIDTH_CHUNK), beta * np.ones(WIDTH_CHUNK), np.ones(2 * D_MODEL)]).astype(np.float32)
    w_in = nrm(ks[2], (DEPTH, D_MODEL, PROJ_COLS)) * (D_MODEL ** -0.5) * jnp.asarray(col_scale)
    b_f = jnp.linspace(1.0, 5.0, N_HEADS_FOX, dtype=f32)[None, :] + 0.1 * nrm(ks[3], (DEPTH, N_HEADS_FOX))
    rel_bias = 0.5 * nrm(ks[4], (DEPTH, N_HEADS_CHUNK, 2 * REL_CLIP + 1))
    w_br_fox = nrm(ks[5], (DEPTH, WIDTH_FOX, D_MODEL)) * WIDTH_FOX ** -0.5
    w_br_chunk = nrm(ks[6], (DEPTH, WIDTH_CHUNK, D_MODEL)) * WIDTH_CHUNK ** -0.5
    w_out = nrm(ks[7], (DEPTH, D_MODEL, D_MODEL)) * (D_MODEL ** -0.5) * beta
    w_up = nrm(ks[8], (DEPTH, D_MODEL, 2 * D_FF)) * D_MODEL ** -0.5
    conv_w = 0.3 * nrm(ks[9], (DEPTH, CONV_WIDTH, 2 * D_FF)) + jnp.array([0.0, 0.0, 1.0], f32)[None, :, None]
    conv_b = 0.01 * nrm(ks[10], (DEPTH, 2 * D_FF))
    w_down = nrm(ks[11], (DEPTH, D_FF, D_MODEL)) * (D_FF ** -0.5) * beta
    w_ada = 0.2 * nrm(ks[12], (DEPTH, D_MODEL, N_MOD * D_MODEL)) * D_MODEL ** -0.5
    b_ada = 0.01 * nrm(ks[13], (DEPTH, N_MOD * D_MODEL))
    ln1_g = 1.0 + 0.05 * nrm(ks[14], (DEPTH, D_MODEL))
    ln1_b = 0.01 * nrm(ks[15], (DEPTH, D_MODEL))
    ln2_g = 1.0 + 0.05 * nrm(ks[16], (DEPTH, D_MODEL))
    ln2_b = 0.01 * nrm(ks[17], (DEPTH, D_MODEL))
    x = nrm(ks[0], (BATCH, SEQ, D_MODEL))
    c = nrm(ks[1], (BATCH, D_MODEL))
    return {"x": x, "c": c, "w_in": w_in, "b_f": b_f, "rel_bias": rel_bias,
            "w_br_fox": w_br_fox, "w_br_chunk": w_br_chunk, "w_out": w_out,
            "w_up": w_up, "conv_w": conv_w, "conv_b": conv_b, "w_down": w_down,
            "w_ada": w_ada, "b_ada": b_ada, "ln1_g": ln1_g, "ln1_b": ln1_b,
            "ln2_g": ln2_g, "ln2_b": ln2_b}


def reference(x, c, w_in, b_f, rel_bias, w_br_fox, w_br_chunk, w_out, w_up, conv_w,
              conv_b, w_down, w_ada, b_ada, ln1_g, ln1_b, ln2_g, ln2_b):
    alpha = (2.0 * DEPTH) ** 0.25
    b, s, _ = x.shape
    split_points = list(np.cumsum(PROJ_SIZES)[:-1])
    cond = jax.nn.silu(c)
    for l in range(DEPTH):
        mod = cond @ w_ada[l] + b_ada[l]
        sh1, sc1, g1, sh2, sc2, g2 = jnp.split(mod[:, None, :], N_MOD, axis=-1)

        h = _layer_norm(x) * (1.0 + sc1) + sh1
        proj = h @ w_in[l]
        q_a, k_a, v_a, f_a, q_c, k_c, v_c, gate_a, gate_c = jnp.split(proj, split_points, axis=-1)
        heads_a = lambda t: t.reshape(b, s, N_HEADS_FOX, HEAD_DIM)
        heads_c = lambda t: t.reshape(b, s, N_HEADS_CHUNK, HEAD_DIM)
        o_a = _forgetting_attention(heads_a(q_a), heads_a(k_a), heads_a(v_a), f_a + b_f[l])
        o_c = _chunk_band_attention(heads_c(q_c), heads_c(k_c), heads_c(v_c), rel_bias[l])
        merged = (jax.nn.sigmoid(gate_a) * (o_a @ w_br_fox[l])
                  + jax.nn.sigmoid(gate_c) * (o_c @ w_br_chunk[l]))
        mix = merged @ w_out[l]
        x = _layer_norm(alpha * x + (1.0 + g1) * mix, ln1_g[l], ln1_b[l])

        h = _layer_norm(x) * (1.0 + sc2) + sh2
        u = _causal_depthwise_conv(h @ w_up[l], conv_w[l], conv_b[l])
        a, val = jnp.split(u, 2, axis=-1)
        y = (jax.nn.silu(a) * val) @ w_down[l]
        x = _layer_norm(alpha * x + (1.0 + g2) * y, ln2_g[l], ln2_b[l])
    return x
```

```python
import contextlib
import os
import numpy as np
import ml_dtypes
import concourse.bass as bass
import concourse.mybir as mybir
from concourse.bass_utils import run_bass_kernel_spmd

F32 = mybir.dt.float32
BF16 = mybir.dt.bfloat16
ALU = mybir.AluOpType
AF = mybir.ActivationFunctionType

D = 1024
S = 8192
HALF = 4096
DFF = 2816
NFC = 44
EPS = 1e-5
ALPHA = float(4.0 ** 0.25)
NL = 2
PAIRS = [[0, 1], [2, 3], [4, 5], [6, 7]]
TW = 1408
G4 = 256


class Prog:
    ENGS = ("pe", "act", "dve", "pool", "sp")

    def __init__(self, nc, es):
        self.nc = nc
        self.es = es
        self.engobj = {"pe": nc.tensor, "act": nc.scalar, "dve": nc.vector, "pool": nc.gpsimd, "sp": nc.sync}
        self.ops = {e: [] for e in self.ENGS}
        self.esem = {e: es.enter_context(nc.semaphore("sem_" + e)) for e in ("pe", "act", "dve", "pool")}
        self.ecount = {e: 0 for e in self.esem}
        self.chans = {}
        self.res = {}
        self.waited = {e: {} for e in self.ENGS}
        self.gidx = {e: 0 for e in self.ENGS}
        self.phase_base = {e: 0 for e in self.ENGS}
        self.sigval_at_phase_start = {e: 0 for e in self.esem}
        self.pid = None
        self.cumdur = {e: 0 for e in self.ENGS}
        self.trace = {e: [] for e in self.ENGS}

    def chan(self, key):
        if key not in self.chans:
            self.chans[key] = [self.es.enter_context(self.nc.semaphore("c%d" % len(self.chans))), 0]
        return self.chans[key]

    def _r(self, name):
        if name not in self.res:
            self.res[name] = {"w": [], "r": [], "pr": []}
        return self.res[name]

    def op(self, eng, fn, reads=(), writes=(), multi=(), chan=None, inc=16, osz=512, dur=None):
        deps = []
        for n in reads:
            r = self._r(n)
            for d in r["w"]:
                if d["eng"] == eng and d["chan"] is None:
                    if eng == "pe":
                        continue
                    d = dict(d, raw=True, orig=d)
                deps.append(d)
        for n in writes:
            r = self._r(n)
            deps += r["w"] + r["r"] + r["pr"]
        for n in multi:
            r = self._r(n)
            deps += r["r"] + r["pr"]
        if dur is None:
            dur = osz
        self.cumdur[eng] += dur
        rec = {"fn": fn, "eng": eng, "idx": self.gidx[eng], "chan": None, "deps": [], "sig": False,
               "osz": osz, "cum": self.cumdur[eng]}
        self.gidx[eng] += 1
        need = {}
        for d in deps:
            if d["chan"] is not None:
                key = ("c", d["chan"][0])
                val = self.chans[d["chan"][0]][1]
                need[key] = max(need.get(key, 0), val)
            else:
                if d["eng"] == eng:
                    if not d.get("raw"):
                        continue
                    d = d["orig"]
                key = ("e", d["eng"])
                if key not in need or need[key]["idx"] < d["idx"]:
                    need[key] = d
        rec["need"] = need
        if chan is not None:
            c = self.chan(chan)
            c[1] += inc
            rec["chan"] = (chan, c[1], inc)
        self.ops[eng].append(rec)
        me = rec
        for n in reads:
            self._r(n)["r"].append(me)
        for n in writes:
            r = self._r(n)
            r["w"] = [me]
            r["r"] = []
            r["pr"] = []
        for n in multi:
            r = self._r(n)
            if r["r"]:
                r["pr"] = r["r"]
                r["r"] = []
                r["w"] = [me]
            else:
                r["w"].append(me)
        return rec

    def emit_phase(self, final_wait=False):
        for e in self.ENGS:
            for rec in self.ops[e]:
                for key, d in rec["need"].items():
                    if key[0] == "e":
                        if d["idx"] >= self.phase_base[d["eng"]]:
                            d["sig"] = True
        for e in self.esem:
            for rec in reversed(self.ops[e]):
                if rec["chan"] is None:
                    rec["sig"] = True
                    break
        sigval = {}
        for e in self.esem:
            cnt = self.ecount[e]
            for rec in self.ops[e]:
                if rec["chan"] is not None:
                    rec["sig"] = False
                if rec["sig"]:
                    cnt += 1
                    rec["sigval"] = cnt
            nxt = cnt
            for rec in reversed(self.ops[e]):
                if rec["sig"]:
                    nxt = rec["sigval"]
                rec["cover"] = nxt
            sigval[e] = cnt
        prog = self

        def run(e):
            eng_ops = prog.ops[e]

            def body(eng):
                w = prog.waited[e]
                if e == "pool" and prog.pid is None:
                    prog.pid = eng.partition_id()
                    rank = prog.pid % 2
                    prog.c0 = eng.snap(rank * HALF)
                    prog.c1 = eng.snap(rank * (HALF - 2))
                for rec in eng_ops:
                    for key, d in rec["need"].items():
                        if key[0] == "c":
                            sem = prog.chans[key[1]][0]
                            val = d
                        else:
                            sem = prog.esem[key[1]]
                            if d["idx"] >= prog.phase_base[key[1]]:
                                val = d["cover"]
                            else:
                                val = prog.sigval_at_phase_start[key[1]]
                        if w.get(key, 0) >= val:
                            continue
                        w[key] = val
                        eng.wait_ge(sem, val)
                        prog.trace[e].append(("wait", key, val))
                    ins = rec["fn"](eng)
                    if rec["chan"] is not None:
                        ins.then_inc(prog.chans[rec["chan"][0]][0], rec["chan"][2])
                        prog.trace[e].append(("inc", ("c", rec["chan"][0]), rec["chan"][2]))
                    elif rec["sig"]:
                        ins.then_inc(prog.esem[e], 1)
                        prog.trace[e].append(("inc", ("e", e), 1))
                if final_wait and e in ("sp", "pool"):
                    for key, (sem, cnt) in prog.chans.items():
                        if cnt > 0 and w.get(("c", key), 0) < cnt:
                            eng.wait_ge(sem, cnt)
                            prog.trace[e].append(("wait", ("c", key), cnt))
                prog.trace[e].append(("barrier", None, 0))
            return body

        with self.nc.Block() as block:
            block.tensor(run("pe"))
            block.scalar(run("act"))
            block.vector(run("dve"))
            block.gpsimd(run("pool"))
            block.sync(run("sp"))
        for e in self.esem:
            self.ecount[e] = sigval[e]
            self.sigval_at_phase_start[e] = sigval[e]
        for e in self.ENGS:
            self.phase_base[e] = self.gidx[e]
            self.ops[e] = []


_NC_CACHE = {}


def build_program(debug=None, n_layers=NL, stop_after=None):
    nc = bass.Bass("TRN2", target_bir_lowering=False)
    es = contextlib.ExitStack()
    P = Prog(nc, es)

    def din(name, shape, dt=F32):
        return nc.dram_tensor(name, list(shape), dt, kind="ExternalInput")

    xfull = din("xfull", [S, D])
    xmine = din("xmine", [HALF + 2, D])
    hv_d = din("hv", [128, 1])
    ccol_d = din("ccol", [128, 8])
    wqk_d = din("wqk", [NL, D, 1024])
    wv_d = din("wv", [NL, D, 512])
    wf_d = din("wf", [NL, D, 4])
    bfbc_d = din("bfbc", [NL, 128, 4])
    wg_d = din("wg", [NL, D, 2048])
    ttab_d = din("ttab", [NL, 4, 128, TW])
    vis_d = din("vis", [128, TW])
    wbr_d = din("wbr", [NL, D, D])
    wout_d = din("wout", [NL, D, D])
    wup_d = din("wup", [NL, D, 2 * DFF])
    cw_d = din("cw", [NL, 128, NFC, 3])
    cb_d = din("cb", [NL, 128, NFC])
    wdn_d = din("wdn", [NL, DFF, D])
    wada_d = din("wada", [NL, D, 6 * D])
    badac_d = din("badac", [NL, 128, 48])
    badar_d = din("badar", [NL, 2, D])
    lnp_d = din("lnp", [NL, 4, 128, D])
    ident_d = din("ident", [128, 128], F32)
    mask_d = din("mask01", [128, 128], BF16)
    tri_d = din("tri", [128, 128])
    tri64s_d = din("tri64s", [64, 64])
    tri64i_d = din("tri64i", [64, 64])
    E_d = din("Emat", [64, S], BF16)
    out_d = nc.dram_tensor("out", [HALF, D], F32, kind="ExternalOutput")

    qaT = nc.dram_tensor("qaT", [256, S], BF16)
    kaT = nc.dram_tensor("kaT", [256, S], BF16)
    qcT = nc.dram_tensor("qcT", [256, S], BF16)
    kcT = nc.dram_tensor("kcT", [256, S], BF16)
    va_d = nc.dram_tensor("va", [S, 260], BF16)
    vc_d = nc.dram_tensor("vc", [S, 260], BF16)
    gref_d = nc.dram_tensor("gref", [4, 3, S], BF16)
    osend = nc.dram_tensor("osend", [512, S], BF16)
    og = nc.dram_tensor("og", [1024, S], BF16)
    x1buf = nc.dram_tensor("x1buf", [HALF + 2, D], F32)
    xs = nc.dram_tensor("xs", [HALF, D], F32)
    xg = nc.dram_tensor("xg", [S, D], F32)
    gbc_d = nc.dram_tensor("gbc", [NL, 2, 128, D], F32)
    dbg = {}

    uniq = [0]

    def sb(stack, name, shape, dt):
        uniq[0] += 1
        return stack.enter_context(nc.sbuf_tensor("s%d_%s" % (uniq[0], name), list(shape), dt))

    ps = [es.enter_context(nc.psum_tensor("ps%d" % i, [128, 512], F32)) for i in range(8)]
    psb = [p[:].bitcast(BF16) for p in ps]
    ident = sb(es, "ident", [128, 128], F32)
    mask01 = sb(es, "mask01", [128, 128], BF16)
    tri = sb(es, "tri", [128, 128], F32)
    tri64s = sb(es, "tri64s", [64, 64], F32)
    tri64i = sb(es, "tri64i", [64, 64], F32)
    ones32 = sb(es, "ones32", [128, 128], F32)
    hv = sb(es, "hv", [128, 1], F32)
    modc = sb(es, "modc", [128, NL, 48], F32)
    G_sb = sb(es, "G_sb", [128, 4, 64], F32)

    def ld(dst_ap, src_ap, res, eng="sp", reads=()):
        P.op(eng, lambda e: e.dma_start(out=dst_ap, in_=src_ap), reads=reads, writes=[res], chan=res)

    ld(ident[:], ident_d.ap(), "ident")
    ld(mask01[:], mask_d.ap(), "mask01")
    ld(tri[:], tri_d.ap(), "tri")
    ld(tri64s[:], tri64s_d.ap(), "tri64s")
    ld(tri64i[:], tri64i_d.ap(), "tri64i")
    ld(hv[:], hv_d.ap(), "hv")
    P.op("dve", lambda e: e.memset(ones32[:], 1.0), writes=["ones32"])

    rr = [0]

    def cast_eng():
        rr[0] += 1
        return ("pool", "dve", "act")[rr[0] % 3]

    def cast_op(eng, dst, src, reads, writes=(), multi=()):
        if eng == "act":
            P.op("act", lambda e: e.activation(out=dst, in_=src, func=AF.Copy), reads=reads, writes=writes, multi=multi)
        else:
            P.op(eng, lambda e: e.tensor_copy(dst, src), reads=reads, writes=writes, multi=multi)

    def load_weight_bf16(stage, dst, dst_res, src3, K, N):
        piece = stage[0].shape[1]
        i = 0
        for k in range(K):
            for n0 in range(0, N, piece):
                n1 = min(N, n0 + piece)
                s = i % 2
                i += 1
                sres = "wstage%d" % s
                st = stage[s]
                ld(st[:, 0:n1 - n0], src3[:, k, n0:n1], sres)
                cast_op(cast_eng(), dst[:, k, n0:n1], st[:, 0:n1 - n0], reads=[sres], multi=[dst_res])

    with contextlib.ExitStack() as ph:
        ccol = sb(ph, "ccol", [128, 8], F32)
        condT = sb(ph, "condT", [128, 8], F32)
        crep = sb(ph, "crep", [128, 8, 128], F32)
        wst = [sb(ph, "wadast%d" % i, [128, 8, 1024], F32) for i in range(2)]
        bcol = sb(ph, "bcol", [128, 48], F32)
        brow = sb(ph, "brow", [1, 2, D], F32)
        gtile = sb(ph, "gtile", [128, D], F32)
        ld(ccol[:], ccol_d.ap(), "ccol")
        P.op("act", lambda e: e.activation(out=condT[:], in_=ccol[:], func=AF.Silu), reads=["ccol"], writes=["condT"], osz=8)
        for k in range(8):
            P.op("dve", lambda e, k=k: e.tensor_scalar(crep[:, k, :], ones32[:], condT[:, k:k + 1], None, ALU.mult),
                 reads=["ones32", "condT"], multi=["crep"], osz=128)
        for l in range(n_layers):
            ld(bcol[:], badac_d.ap()[l], "bcol")
            ld(brow[:], badar_d.ap()[l:l + 1], "brow")
            w3 = wada_d.ap()[l].rearrange("(k p) n -> p k n", p=128)
            for pc in range(6):
                s = pc % 2
                ld(wst[s][:], w3[:, :, pc * 1024:(pc + 1) * 1024], "wadast%d" % s)
                for j in range(8):
                    col = pc * 8 + j
                    for k in range(8):
                        P.op("pe", lambda e, s=s, j=j, k=k, col=col: e.matmul(
                            ps[0][:, col:col + 1], lhsT=wst[s][:, k, j * 128:(j + 1) * 128], rhs=condT[:, k:k + 1],
                            start=(k == 0), stop=(k == 7)),
                            reads=["wadast%d" % s, "condT"], multi=["ps0"])
                if pc in (2, 5):
                    gi = 0 if pc == 2 else 1
                    for hf in range(2):
                        bank = 1 + hf
                        for k in range(8):
                            P.op("pe", lambda e, s=s, k=k, hf=hf, bank=bank: e.matmul(
                                ps[bank][:, :], lhsT=crep[:, k, :], rhs=wst[s][:, k, hf * 512:(hf + 1) * 512],
                                start=(k == 0), stop=False),
                                reads=["wadast%d" % s, "crep"], multi=["ps%d" % bank])
                        P.op("pe", lambda e, gi=gi, hf=hf, bank=bank: e.matmul(
                            ps[bank][:, :], lhsT=ones32[0:1, :], rhs=brow[0:1, gi, hf * 512:(hf + 1) * 512],
                            start=False, stop=True), reads=["ones32", "brow"], multi=["ps%d" % bank])
                        P.op("dve", lambda e, hf=hf, bank=bank: e.tensor_scalar(
                            gtile[:, hf * 512:(hf + 1) * 512], ps[bank][:, :], 1.0, None, ALU.add),
                            reads=["ps%d" % bank], multi=["gtile"])
                    P.op("pool", lambda e, l=l, gi=gi: e.dma_start(out=gbc_d.ap()[l, gi], in_=gtile[:]),
                         reads=["gtile"], multi=["d_gbc"], chan="gtile")
            P.op("dve", lambda e, l=l: e.tensor_tensor(modc[:, l, :], ps[0][:, 0:48], bcol[:], ALU.add),
                 reads=["ps0", "bcol"], multi=["modc"], osz=48)
            for c0 in (8, 32):
                P.op("dve", lambda e, l=l, c0=c0: e.tensor_scalar(
                    modc[:, l, c0:c0 + 8], modc[:, l, c0:c0 + 8], 1.0, None, ALU.add), reads=["modc"], multi=["modc"], osz=8)
        P.emit_phase()

    def ln_stats(xap, n, st, mv, res_in, tag):
        for hf in range(2):
            P.op("dve", lambda e, hf=hf: e.bn_stats(st[0:n, 2 * hf:2 * hf + 2, :], xap[:, hf * 512:(hf + 1) * 512]),
                 reads=[res_in], multi=[tag + "st"], osz=6, dur=512)
        P.op("dve", lambda e: e.bn_aggr(mv[0:n, 0:2], st[0:n, :, :]), reads=[tag + "st"], multi=[tag + "mv"], osz=2)
        P.op("act", lambda e: e.activation(out=mv[0:n, 2:3], in_=mv[0:n, 1:2], func=AF.Ln, bias=EPS),
             reads=[tag + "mv"], multi=[tag + "mv"], osz=1)
        P.op("act", lambda e: e.activation(out=mv[0:n, 2:3], in_=mv[0:n, 2:3], func=AF.Exp, scale=-0.5),
             reads=[tag + "mv"], multi=[tag + "mv"], osz=1)
        P.op("dve", lambda e: e.tensor_scalar(mv[0:n, 3:4], mv[0:n, 0:1], mv[0:n, 2:3], -1.0, ALU.mult, ALU.mult),
             reads=[tag + "mv"], multi=[tag + "mv"], osz=1)

    def ln_to_hT(xin_tile, xin_res, ntok, xn, xn_res, hT, hT_res, st, mv, l, sc_col, sh_col, tbanks, tag):
        ntt = (ntok + 127) // 128
        for tt in range(ntt):
            n = min(128, ntok - tt * 128)
            ln_stats(xin_tile[0:n, tt, :], n, st, mv, xin_res, tag)
            P.op("act", lambda e, tt=tt, n=n: e.activation(out=xn[0:n, tt, :], in_=xin_tile[0:n, tt, :], func=AF.Identity,
                                                           scale=mv[0:n, 2:3], bias=mv[0:n, 3:4]),
                 reads=[xin_res, tag + "mv"], multi=[xn_res])
        for k in range(8):
            bank = tbanks[k % len(tbanks)]
            off = 0
            pres = "ps%d" % bank
            for tt in range(ntt):
                n = min(128, ntok - tt * 128)
                P.op("pe", lambda e, tt=tt, n=n, k=k, bank=bank, off=off: e.transpose(
                    ps[bank][:, off + tt * 128: off + tt * 128 + n], xn[0:n, tt, k * 128:(k + 1) * 128], ident[0:n, 0:n]),
                    reads=[xn_res, "ident"], multi=[pres])
            P.op("act", lambda e, k=k, bank=bank, off=off: e.activation(
                out=hT[:, k, 0:ntok], in_=ps[bank][:, off:off + ntok], func=AF.Identity,
                scale=modc[:, l, sc_col + k:sc_col + k + 1], bias=modc[:, l, sh_col + k:sh_col + k + 1]),
                reads=[pres, "modc"], multi=[hT_res], osz=ntok)

    def final_ln(z, n, st, mv, lng, lnb, zres, tag):
        ln_stats(z, n, st, mv, zres, tag)
        P.op("act", lambda e: e.activation(out=z, in_=z, func=AF.Identity, scale=mv[0:n, 2:3], bias=mv[0:n, 3:4]),
             reads=[tag + "mv", zres], writes=[zres])
        P.op("dve", lambda e: e.tensor_tensor(z, z, lng[0:n, :], ALU.mult), reads=["lnp", zres], writes=[zres])
        P.op("dve", lambda e: e.tensor_tensor(z, z, lnb[0:n, :], ALU.add), reads=["lnp", zres], writes=[zres])

    for l in range(n_layers if stop_after != ("p0", 0) else 0):
        xsrc_full = xfull.ap() if l == 0 else xg.ap()
        xsrc_full_res = "d_xfull" if l == 0 else "d_xg"

        with contextlib.ExitStack() as ph:
            stage = [sb(ph, "wstage%d" % i, [128, 2048], F32) for i in range(2)]
            wqk = sb(ph, "wqk", [128, 8, 1024], BF16)
            wv = sb(ph, "wv", [128, 8, 512], BF16)
            wf = sb(ph, "wf", [128, 8, 4], BF16)
            bfbc = sb(ph, "bfbc", [128, 4], F32)
            xin = [sb(ph, "xin%d" % i, [128, 4, D], F32) for i in range(2)]
            xn = [sb(ph, "xn%d" % i, [128, 4, D], F32) for i in range(2)]
            hT = [sb(ph, "hT%d" % i, [128, 8, 512], BF16) for i in range(2)]
            qko = [sb(ph, "qko%d" % i, [128, 512], BF16) for i in range(4)]
            vout = [sb(ph, "vout%d" % i, [128, 8, 65], BF16) for i in range(2)]
            fsb = sb(ph, "fsb", [128, 4, 64], F32)
            st = sb(ph, "st", [128, 4, 3], F32)
            mv = sb(ph, "mv", [128, 4], F32)
            Emat = sb(ph, "Emat", [64, S], BF16)
            tot = sb(ph, "tot", [64, 1], F32)
            totrep = sb(ph, "totrep", [64, 128], F32)
            r0 = sb(ph, "r0", [64, 4], F32)
            vals = sb(ph, "vals", [64, 3], BF16)
            v32 = sb(ph, "v32", [64, 1], F32)
            grow = [sb(ph, "grow%d" % i, [3, 512], BF16) for i in range(2)]

            load_weight_bf16(stage, wqk, "wqk", wqk_d.ap()[l].rearrange("(k p) n -> p k n", p=128), 8, 1024)
            load_weight_bf16(stage, wv, "wv", wv_d.ap()[l].rearrange("(k p) n -> p k n", p=128), 8, 512)
            load_weight_bf16(stage, wf, "wf", wf_d.ap()[l].rearrange("(k p) n -> p k n", p=128), 8, 4)
            ld(bfbc[:], bfbc_d.ap()[l], "bfbc")
            ld(Emat[:], E_d.ap(), "Emat")
            for i in range(2):
                P.op("pool", lambda e, i=i: e.memset(vout[i][:], 1.0), writes=["vout%d" % i])

            dsts = [qaT, qaT, kaT, kaT, qcT, qcT, kcT, kcT]
            for g in range(int(os.environ.get('K_P1_GROUPS', '16'))):
                s = g % 2
                tok0 = g * 512
                srow = tok0 if l == 0 else ((g % 8) * 2 + g // 8) * 512
                ld(xin[s][:], xsrc_full[srow:srow + 512, :].rearrange("(t p) d -> p t d", p=128), "xin%d" % s,
                   reads=[xsrc_full_res])
                if int(os.environ.get('K_P1_STEPS', '9')) >= 2:
                    ln_to_hT(xin[s], "xin%d" % s, 512, xn[s], "xn%d" % s, hT[s], "hT%d" % s, st, mv, l, 8, 0, [0, 1], "p1")
                STEPS = int(os.environ.get('K_P1_STEPS', '9'))
                for m in range(8 if STEPS >= 3 else 0):
                    bank = 2 + m % 3
                    for k in range(8):
                        P.op("pe", lambda e, m=m, k=k, bank=bank, s=s: e.matmul(
                            ps[bank][:, :], lhsT=wqk[:, k, m * 128:(m + 1) * 128], rhs=hT[s][:, k, :],
                            start=(k == 0), stop=(k == 7)), reads=["wqk", "hT%d" % s], multi=["ps%d" % bank])
                    qs = (g * 8 + m) % 4
                    scale = 0.125 if m in (0, 1, 4, 5) else 1.0
                    if m % 2 == 0:
                        P.op("act", lambda e, bank=bank, qs=qs, scale=scale: e.activation(
                            out=qko[qs][:], in_=ps[bank][:, :], func=AF.Copy, scale=scale),
                            reads=["ps%d" % bank], writes=["qko%d" % qs])
                    else:
                        P.op("dve", lambda e, bank=bank, qs=qs, scale=scale: e.tensor_scalar(
                            qko[qs][:], ps[bank][:, :], scale, None, ALU.mult),
                            reads=["ps%d" % bank], writes=["qko%d" % qs])
                    dst = dsts[m]
                    r0_ = (m % 2) * 128
                    P.op("pool", lambda e, dst=dst, r0_=r0_, qs=qs, tok0=tok0: e.dma_start(
                        out=dst.ap()[r0_:r0_ + 128, tok0:tok0 + 512], in_=qko[qs][:]),
                        reads=["qko%d" % qs], multi=["d_" + dst.name], chan="qko%d" % qs)
                for tt in range(4 if STEPS >= 4 else 0):
                    bank = 6 + tt % 2
                    for k in range(8):
                        P.op("pe", lambda e, tt=tt, k=k, bank=bank, s=s: e.matmul(
                            ps[bank][:, :], lhsT=hT[s][:, k, tt * 128:(tt + 1) * 128], rhs=wv[:, k, :],
                            start=(k == 0), stop=(k == 7)), reads=["wv", "hT%d" % s], multi=["ps%d" % bank])
                    vs = (g * 4 + tt) % 2
                    P.op("dve", lambda e, bank=bank, vs=vs: e.tensor_copy(
                        vout[vs][:, :, 1:65], ps[bank][:, :].rearrange("p (h c) -> p h c", c=64)),
                        reads=["ps%d" % bank], multi=["vout%d" % vs])
                    t0_ = tok0 + tt * 128
                    P.op("pool", lambda e, vs=vs, t0_=t0_: e.dma_start(
                        out=va_d.ap()[t0_:t0_ + 128, :], in_=vout[vs][:, 0:4, :].rearrange("p h c -> p (h c)")),
                        reads=["vout%d" % vs], multi=["d_va"], chan="vout%d" % vs)
                    P.op("pool", lambda e, vs=vs, t0_=t0_: e.dma_start(
                        out=vc_d.ap()[t0_:t0_ + 128, :], in_=vout[vs][:, 4:8, :].rearrange("p h c -> p (h c)")),
                        reads=["vout%d" % vs], multi=["d_vc"], chan="vout%d" % vs)
                for tt in range(4 if STEPS >= 5 else 0):
                    for k in range(8):
                        P.op("pe", lambda e, tt=tt, k=k, s=s: e.matmul(
                            ps[5][:, tt * 4: tt * 4 + 4], lhsT=hT[s][:, k, tt * 128:(tt + 1) * 128],
                            rhs=wf[:, k, :], start=(k == 0), stop=(k == 7)),
                            reads=["wf", "hT%d" % s], multi=["ps5"])
                for tt in range(4 if STEPS >= 5 else 0):
                    blk = g * 4 + tt
                    P.op("dve", lambda e, tt=tt, blk=blk: e.tensor_tensor(
                        fsb[:, :, blk], ps[5][:, tt * 4: tt * 4 + 4], bfbc[:], ALU.add),
                        reads=["ps5", "bfbc"], multi=["fsb"], osz=4)
            P.op("act", lambda e: e.activation(out=fsb[:], in_=fsb[:], func=AF.Exp, scale=-1.0), reads=["fsb"], writes=["fsb"], osz=255)
            P.op("act", lambda e: e.activation(out=fsb[:], in_=fsb[:], func=AF.Ln, bias=1.0), reads=["fsb"], writes=["fsb"], osz=255)
            for h in range(int(os.environ.get('K_P1_TAIL', '4'))):
                P.op("pe", lambda e, h=h: e.matmul(ps[2][0:64, 0:1], lhsT=fsb[:, h, :], rhs=ones32[:, 0:1], start=True, stop=True),
                     reads=["fsb", "ones32"], writes=["ps2"])
                P.op("dve", lambda e: e.tensor_copy(tot[:], ps[2][0:64, 0:1]), reads=["ps2"], writes=["tot"], osz=1)
                P.op("dve", lambda e: e.tensor_scalar(totrep[:], ones32[0:64, :], tot[:, 0:1], None, ALU.mult),
                     reads=["tot", "ones32"], writes=["totrep"], osz=128)
                P.op("pe", lambda e, h=h: e.matmul(ps[3][:, 0:64], lhsT=tri[:], rhs=fsb[:, h, :], start=True, stop=False),
                     reads=["fsb", "tri"], writes=["ps3"])
                P.op("pe", lambda e: e.matmul(ps[3][:, 0:64], lhsT=totrep[:], rhs=tri64s[:], start=False, stop=True),
                     reads=["totrep", "tri64s"], multi=["ps3"])
                P.op("dve", lambda e, h=h: e.tensor_copy(G_sb[:, h, :], ps[3][:, 0:64]), reads=["ps3"], multi=["G_sb"], osz=64)
                P.op("pe", lambda e: e.matmul(ps[4][0:64, 0:1], lhsT=tri64i[:], rhs=tot[:], start=True, stop=True),
                     reads=["tot", "tri64i"], writes=["ps4"])
                P.op("dve", lambda e: e.tensor_scalar(r0[:, 0:1], ps[4][0:64, 0:1], -1.0, None, ALU.mult),
                     reads=["ps4"], writes=["r0"], osz=1)
                for i in range(3):
                    P.op("dve", lambda e, i=i: e.tensor_copy(vals[:, i:i + 1], r0[:, i:i + 1]), reads=["r0"], multi=["vals"], osz=1)
                    if i < 2:
                        P.op("dve", lambda e, i=i: e.tensor_copy(v32[:], vals[:, i:i + 1]), reads=["vals"], writes=["v32"], osz=1)
                        P.op("dve", lambda e, i=i: e.tensor_tensor(r0[:, i + 1:i + 2], r0[:, i:i + 1], v32[:], ALU.subtract),
                             reads=["v32", "r0"], multi=["r0"], osz=1)
                for j in range(16):
                    bank = 5 + j % 2
                    gs = j % 2
                    P.op("pe", lambda e, j=j, bank=bank: e.matmul(
                        ps[bank][0:3, :], lhsT=vals[:, :], rhs=Emat[:, j * 512:(j + 1) * 512], start=True, stop=True),
                        reads=["vals", "Emat"], writes=["ps%d" % bank])
                    P.op("act", lambda e, bank=bank, gs=gs: e.activation(out=grow[gs][:], in_=ps[bank][0:3, :], func=AF.Copy),
                         reads=["ps%d" % bank], writes=["grow%d" % gs])
                    P.op("pool", lambda e, h=h, j=j, gs=gs: e.dma_start(
                        out=gref_d.ap()[h, :, j * 512:(j + 1) * 512], in_=grow[gs][:]),
                        reads=["grow%d" % gs], multi=["d_gref"], chan="grow%d" % gs)
            P.emit_phase()
        if stop_after == ("p1", l):
            break

        with contextlib.ExitStack() as ph:
            kT = [sb(ph, "kT%d" % i, [128, S], BF16) for i in range(2)]
            qT = [sb(ph, "qT%d" % i, [128, S], BF16) for i in range(2)]
            vx = [sb(ph, "vx%d" % i, [128, 64, 260], BF16) for i in range(2)]
            pT = [sb(ph, "pT%d" % i, [128, 512], BF16) for i in range(4)]
            pf = [sb(ph, "pf%d" % i, [128, 512], F32) for i in range(2)]
            osb = [sb(ph, "osb%d" % i, [96, 512], BF16) for i in range(2)]
            otmp = sb(ph, "otmp", [96, 512], F32)
            rl = sb(ph, "rl", [128, 512], F32)
            expT = sb(ph, "expT", [128, TW], F32)
            vis = sb(ph, "vis", [128, TW], F32)

            ld(vis[:], vis_d.ap(), "vis")
            for i in range(2):
                P.op("pool", lambda e, i=i: e.memset(kT[i][64:96, :], 1.0), writes=["kT%d" % i])
            for i_, (vsrc, vsr) in enumerate(((va_d, "d_va"), (vc_d, "d_vc"))):
                v3 = vsrc.ap().rearrange("(b p) c -> p b c", p=128)
                for q_ in range(4):
                    P.op("sp", lambda e, i_=i_, v3=v3, q_=q_: e.dma_start(out=vx[i_][:, q_ * 16:(q_ + 1) * 16, :], in_=v3[:, q_ * 16:(q_ + 1) * 16, :]),
                         reads=[vsr], multi=["vx%d" % i_], chan="vx%d" % i_)

            def finish_group(acc_bank, h_row0, g, gi):
                bcb = 6 + gi % 2
                o = gi % 2
                P.op("dve", lambda e: e.reciprocal(rl[0:1, :], ps[acc_bank][0:1, :]),
                     reads=["ps%d" % acc_bank], writes=["rl"])
                P.op("pe", lambda e: e.matmul(ps[bcb][0:96, :], lhsT=ones32[0:1, 0:96], rhs=rl[0:1, :], start=True, stop=True),
                     reads=["rl", "ones32"], writes=["ps%d" % bcb])
                P.op("dve", lambda e: e.tensor_copy(otmp[:], ps[acc_bank][0:96, :]),
                     reads=["ps%d" % acc_bank], writes=["otmp"])
                P.op("dve", lambda e: e.tensor_tensor(osb[o][:], otmp[:], ps[bcb][0:96, :], ALU.mult),
                     reads=["otmp", "ps%d" % bcb], writes=["osb%d" % o])
                P.op("pool", lambda e: e.dma_start(out=osend.ap()[h_row0:h_row0 + 64, g * 512:(g + 1) * 512], in_=osb[o][1:65, :]),
                     reads=["osb%d" % o], multi=["d_osend"], chan="osb%d" % o)

            def do_head(typ, h, tile_i, gi):
                if True:
                    hs = (typ * 4 + h) % 2
                    ksrc, qsrc = (kaT, qaT) if typ == 0 else (kcT, qcT)
                    kres, qres = "kT%d" % hs, "qT%d" % hs
                    P.op("sp", lambda e, hs=hs, ksrc=ksrc, h=h: e.dma_start(out=kT[hs][0:64, :], in_=ksrc.ap()[h * 64:(h + 1) * 64, :]),
                         reads=["d_" + ksrc.name], multi=[kres], chan=kres + "a")
                    P.op("sp", lambda e, hs=hs, qsrc=qsrc, h=h: e.dma_start(out=qT[hs][0:64, :], in_=qsrc.ap()[h * 64:(h + 1) * 64, :]),
                         reads=["d_" + qsrc.name], multi=[qres], chan=qres + "a")
                    if typ == 0:
                        P.op("sp", lambda e, hs=hs, h=h: e.dma_start(out=qT[hs][64:67, :], in_=gref_d.ap()[h]),
                             reads=["d_gref"], multi=[qres], chan=qres + "b")
                    else:
                        P.op("sp", lambda e, h=h: e.dma_start(out=expT[:], in_=ttab_d.ap()[l, h]), writes=["expT"], chan="expT")
                        P.op("act", lambda e: e.activation(out=expT[:], in_=expT[:], func=AF.Exp), writes=["expT"])
                        P.op("dve", lambda e: e.tensor_tensor(expT[:], expT[:], vis[:], ALU.mult), reads=["vis"], writes=["expT"])
                    KR = 67 if typ == 0 else 64
                    vxt = vx[typ]
                    vres = "vx%d" % typ
                    tiles = []
                    for g in range(16):
                        if typ == 0:
                            blks = list(range(4 * g + 4))
                            for j in blks:
                                d = j - 4 * g
                                c_lo = 0 if d < 0 else d * 128
                                tiles.append((g, j, c_lo, 512, j == 0, j == blks[-1], d))
                        else:
                            rs = [r for r in (0, -1, -2, -3, -4, 1, 2, 3) if 0 <= 4 * g + r <= 63]
                            for ri, r in enumerate(rs):
                                c_lo = 128 * max(0, r)
                                c_hi = 128 * (min(3, r + 4) + 1)
                                tiles.append((g, 4 * g + r, c_lo, c_hi, ri == 0, ri == len(rs) - 1, r))
                    nt = len(tiles)
                    LOOK = 3

                    def emit_S(ti):
                        g, j, c_lo, c_hi, first, last, d = tiles[ti]
                        sbank = (tile_i + ti) % 4
                        P.op("pe", lambda e: e.matmul(
                            ps[sbank][:, c_lo:c_hi], lhsT=kT[hs][0:KR, j * 128:(j + 1) * 128],
                            rhs=qT[hs][0:KR, g * 512 + c_lo:g * 512 + c_hi], start=True, stop=True),
                            reads=[kres, qres], writes=["ps%d" % sbank])

                    def emit_rest(ti):
                        g, j, c_lo, c_hi, first, last, d = tiles[ti]
                        sbank = (tile_i + ti) % 4
                        pslot = (tile_i + ti) % 4
                        acc = 4 + (gi + g) % 2
                        w = c_hi - c_lo
                        if typ == 0:
                            P.op("act", lambda e: e.activation(
                                out=pT[pslot][:, c_lo:c_hi], in_=ps[sbank][:, c_lo:c_hi], func=AF.Exp,
                                bias=G_sb[:, h, j:j + 1], scale=1.0),
                                reads=["ps%d" % sbank, "G_sb"], writes=["pT%d" % pslot])
                            if d >= 0:
                                P.op("dve", lambda e: e.tensor_tensor(
                                    pT[pslot][:, c_lo:c_lo + 128], pT[pslot][:, c_lo:c_lo + 128], mask01[:], ALU.mult),
                                    reads=["mask01", "pT%d" % pslot], writes=["pT%d" % pslot], osz=128)
                        else:
                            fs = (tile_i + ti) % 2
                            P.op("act", lambda e: e.activation(
                                out=pf[fs][:, c_lo:c_hi], in_=ps[sbank][:, c_lo:c_hi], func=AF.Exp),
                                reads=["ps%d" % sbank], writes=["pf%d" % fs])
                            t0c = c_lo - 128 * d + 384
                            P.op("dve", lambda e: e.tensor_tensor(
                                pT[pslot][:, c_lo:c_hi], pf[fs][:, c_lo:c_hi], expT[:, t0c:t0c + w], ALU.mult),
                                reads=["pf%d" % fs, "expT"], writes=["pT%d" % pslot])
                        P.op("pe", lambda e: e.matmul(
                            ps[acc][0:65, c_lo:c_hi], lhsT=vxt[:, j, h * 65:(h + 1) * 65], rhs=pT[pslot][:, c_lo:c_hi],
                            start=first, stop=last), reads=[vres, "pT%d" % pslot],
                            **({"writes": ["ps%d" % acc]} if first else {"multi": ["ps%d" % acc]}))
                        if last and os.environ.get('K_P2_FIN', '1') == '1':
                            finish_group(acc, typ * 256 + h * 64, g, gi + g)

                    for ti in range(min(LOOK, nt)):
                        emit_S(ti)
                    for ti in range(nt):
                        if ti + LOOK < nt:
                            emit_S(ti + LOOK)
                        emit_rest(ti)
                    return nt

            gi = 0
            tile_i = 0
            for typ in range(2):
                for h in range(4):
                    if typ * 4 + h >= int(os.environ.get('K_P2_HEADS', '8')):
                        continue
                    tile_i += do_head(typ, h, tile_i, gi)
                    gi += 16
            if os.environ.get('K_P2_CC', '1') == '1':
              for j_ in range(4):
                P.op("pool", lambda e, j_=j_: e.collective_compute(
                    "AllGather", ALU.bypass, replica_groups=PAIRS,
                    ins=[osend.ap()[j_ * 128:(j_ + 1) * 128, :]], outs=[og.ap()[j_ * 256:(j_ + 1) * 256, :]]),
                    reads=["d_osend"], multi=["d_og"], chan="cc_og", inc=1)
            P.emit_phase()
        if stop_after == ("p2", l):
            break

        with contextlib.ExitStack() as ph:
            stage = [sb(ph, "wstage%d" % i, [128, 2048], F32) for i in range(2)]
            wg = sb(ph, "wg", [128, 8, 2048], BF16)
            wbr = sb(ph, "wbr", [128, 8, D], BF16)
            wout = sb(ph, "wout", [128, 8, D], BF16)
            xin = [sb(ph, "xin%d" % i, [128, 4, D], F32) for i in range(2)]
            xn = [sb(ph, "xn%d" % i, [128, 4, D], F32) for i in range(1)]
            hT = [sb(ph, "hT%d" % i, [128, 8, 512], BF16) for i in range(1)]
            oT = [sb(ph, "oT%d" % i, [128, 8, 512], BF16) for i in range(2)]
            mT = sb(ph, "mT", [128, 8, 512], BF16)
            sa = [sb(ph, "sa%d" % i, [128, 512], F32) for i in range(2)]
            sc = [sb(ph, "sc%d" % i, [128, 512], F32) for i in range(2)]
            z = [sb(ph, "z%d" % i, [128, D], F32) for i in range(2)]
            gb1 = sb(ph, "gb1", [128, D], F32)
            lng = sb(ph, "lng", [128, D], F32)
            lnb = sb(ph, "lnb", [128, D], F32)
            st = sb(ph, "st", [128, 4, 3], F32)
            mv = sb(ph, "mv", [128, 4], F32)

            load_weight_bf16(stage, wg, "wg", wg_d.ap()[l].rearrange("(k p) n -> p k n", p=128), 8, 2048)
            load_weight_bf16(stage, wbr, "wbr", wbr_d.ap()[l].rearrange("(k p) n -> p k n", p=128), 8, D)
            load_weight_bf16(stage, wout, "wout", wout_d.ap()[l].rearrange("(k p) n -> p k n", p=128), 8, D)
            ld(gb1[:], gbc_d.ap()[l, 0], "gb1", reads=["d_gbc"])
            ld(lng[:], lnp_d.ap()[l, 0], "lnp")
            P.op("sp", lambda e: e.dma_start(out=lnb[:], in_=lnp_d.ap()[l, 1]), multi=["lnp"], chan="lnpb")

            FOXK = (0, 1, 2, 3)
            CHK = (4, 5, 6, 7)
            groups = [("halo", 2)] + [("main", gq) for gq in range(8)]
            def p3_group(gidx_, kind, gq):
                s = gidx_ % 2
                if kind == "halo":
                    ntok = 2
                    if l == 0:
                        xsrc = xmine.ap()[0:2, :]
                        xres = []
                    else:
                        xsrc = xg.ap()[14 * 512 + 510:14 * 512 + 512, :]
                        xres = ["d_xg"]
                    dst_row = 0
                else:
                    ntok = 512
                    if l == 0:
                        xsrc = xmine.ap()[2 + gq * 512: 2 + (gq + 1) * 512, :]
                        xres = []
                    else:
                        xsrc = xs.ap()[gq * 512:(gq + 1) * 512, :]
                        xres = ["d_xs"]
                    dst_row = 2 + gq * 512
                ntt = (ntok + 127) // 128
                if kind == "halo":
                    ld(xin[s][0:2, 0, :], xsrc, "xin%d" % s, reads=xres)
                else:
                    ld(xin[s][:], xsrc.rearrange("(t p) d -> p t d", p=128), "xin%d" % s, reads=xres)

                def og_load(e, s=s, kind=kind, gq=gq, ntok=ntok):
                    og3 = og.ap().rearrange("(k p) t -> p k t", p=128)
                    if kind == "halo":
                        src = og3[:, :, bass.ds(P.c1, 2)]
                    else:
                        src = og3[:, :, bass.ds(P.c0, HALF)][:, :, gq * 512:(gq + 1) * 512]
                    return e.dma_start(out=oT[s][:, :, 0:ntok], in_=src)
                P.op("pool", og_load, reads=["d_og"], writes=["oT%d" % s], chan="oT%d" % s)
                ln_to_hT(xin[s], "xin%d" % s, ntok, xn[0], "xn0", hT[0], "hT0", st, mv, l, 8, 0, [0], "p3")
                for fc in range(8):
                    b0 = 1 + (fc % 2) * 3
                    bga, bgc, bba = b0, b0 + 1, b0 + 2
                    bbc = 7
                    for k in range(8):
                        P.op("pe", lambda e, k=k, fc=fc, bga=bga: e.matmul(
                            ps[bga][:, 0:ntok], lhsT=wg[:, k, fc * 128:(fc + 1) * 128], rhs=hT[0][:, k, 0:ntok],
                            start=(k == 0), stop=(k == 7)), reads=["wg", "hT0"],
                            **({"writes": ["ps%d" % bga]} if k == 0 else {"multi": ["ps%d" % bga]}))
                    for k in range(8):
                        P.op("pe", lambda e, k=k, fc=fc, bgc=bgc: e.matmul(
                            ps[bgc][:, 0:ntok], lhsT=wg[:, k, D + fc * 128:D + (fc + 1) * 128], rhs=hT[0][:, k, 0:ntok],
                            start=(k == 0), stop=(k == 7)), reads=["wg", "hT0"],
                            **({"writes": ["ps%d" % bgc]} if k == 0 else {"multi": ["ps%d" % bgc]}))
                    for ki, kk in enumerate(FOXK):
                        P.op("pe", lambda e, kk=kk, ki=ki, fc=fc, bba=bba: e.matmul(
                            ps[bba][:, 0:ntok], lhsT=wbr[:, kk, fc * 128:(fc + 1) * 128], rhs=oT[s][:, kk, 0:ntok],
                            start=(ki == 0), stop=(ki == 3)), reads=["wbr", "oT%d" % s],
                            **({"writes": ["ps%d" % bba]} if ki == 0 else {"multi": ["ps%d" % bba]}))
                    for ki, kk in enumerate(CHK):
                        P.op("pe", lambda e, kk=kk, ki=ki, fc=fc: e.matmul(
                            ps[bbc][:, 0:ntok], lhsT=wbr[:, kk, fc * 128:(fc + 1) * 128], rhs=oT[s][:, kk, 0:ntok],
                            start=(ki == 0), stop=(ki == 3)), reads=["wbr", "oT%d" % s],
                            **({"writes": ["ps7"]} if ki == 0 else {"multi": ["ps7"]}))
                    q = fc % 2
                    P.op("act", lambda e, q=q, bga=bga: e.activation(out=sa[q][:, 0:ntok], in_=ps[bga][:, 0:ntok], func=AF.Sigmoid),
                         reads=["ps%d" % bga], writes=["sa%d" % q], osz=ntok)
                    P.op("act", lambda e, q=q, bgc=bgc: e.activation(out=sc[q][:, 0:ntok], in_=ps[bgc][:, 0:ntok], func=AF.Sigmoid),
                         reads=["ps%d" % bgc], writes=["sc%d" % q], osz=ntok)
                    P.op("dve", lambda e, q=q, bba=bba: e.tensor_tensor(sa[q][:, 0:ntok], sa[q][:, 0:ntok], ps[bba][:, 0:ntok], ALU.mult),
                         reads=["ps%d" % bba, "sa%d" % q], writes=["sa%d" % q], osz=ntok)
                    P.op("dve", lambda e, q=q: e.tensor_tensor(sc[q][:, 0:ntok], sc[q][:, 0:ntok], ps[bbc][:, 0:ntok], ALU.mult),
                         reads=["ps7", "sc%d" % q], writes=["sc%d" % q], osz=ntok)
                    P.op("dve", lambda e, q=q, fc=fc: e.tensor_tensor(mT[:, fc, 0:ntok], sa[q][:, 0:ntok], sc[q][:, 0:ntok], ALU.add),
                         reads=["sa%d" % q, "sc%d" % q], multi=["mT"], osz=ntok)
                for tt in range(ntt):
                    n = min(128, ntok - tt * 128)
                    zs = tt % 2
                    zt = z[zs]
                    for hf in range(2):
                        bank = 1 + (tt * 2 + hf) % 6
                        for k in range(8):
                            P.op("pe", lambda e, k=k, tt=tt, n=n, hf=hf, bank=bank: e.matmul(
                                ps[bank][0:n, :], lhsT=mT[:, k, tt * 128:tt * 128 + n], rhs=wout[:, k, hf * 512:(hf + 1) * 512],
                                start=(k == 0), stop=(k == 7)), reads=["wout", "mT"],
                                **({"writes": ["ps%d" % bank]} if k == 0 else {"multi": ["ps%d" % bank]}))
                        P.op("dve", lambda e, n=n, hf=hf, bank=bank, zt=zt: e.tensor_tensor(
                            zt[0:n, hf * 512:(hf + 1) * 512], ps[bank][0:n, :], gb1[0:n, hf * 512:(hf + 1) * 512], ALU.mult),
                            reads=["ps%d" % bank, "gb1"], multi=["z%d" % zs])
                    P.op("dve", lambda e, n=n, tt=tt, zt=zt, s=s: e.scalar_tensor_tensor(
                        zt[0:n, :], xin[s][0:n, tt, :], ALPHA, zt[0:n, :], ALU.mult, ALU.add),
                        reads=["xin%d" % s], multi=["z%d" % zs])
                    final_ln(zt[0:n, :], n, st, mv, lng, lnb, "z%d" % zs, "p3z")
                    r_ = dst_row + tt * 128
                    P.op("pool", lambda e, zt=zt, n=n, r_=r_: e.dma_start(out=x1buf.ap()[r_:r_ + n, :], in_=zt[0:n, :]),
                         reads=["z%d" % zs], multi=["d_x1"], chan="z%d" % zs)

            for gidx_, (kind, gq) in enumerate(groups):
                p3_group(gidx_, kind, gq)
            P.emit_phase()
        if stop_after == ("p3", l):
            break

        with contextlib.ExitStack() as ph:
            GT = G4
            NTT = (GT + 127) // 128
            stage = [sb(ph, "wstage%d" % i, [128, 512], F32) for i in range(2)]
            wup = sb(ph, "wup", [128, 8, 2 * DFF], BF16)
            wdn = sb(ph, "wdn", [128, 22, D], BF16)
            xin = [sb(ph, "xin%d" % i, [128, NTT, D], F32) for i in range(2)]
            xn = [sb(ph, "xn%d" % i, [128, NTT, D], F32) for i in range(1)]
            hT = [sb(ph, "hT%d" % i, [128, 8, GT], BF16) for i in range(1)]
            gT = sb(ph, "gT", [128, 22, GT], BF16)
            t0 = [sb(ph, "t0%d" % i, [128, GT], F32) for i in range(4)]
            sact = [sb(ph, "sact%d" % i, [128, GT], BF16) for i in range(2)]
            z = [sb(ph, "z%d" % i, [128, D], F32) for i in range(2)]
            gb2 = sb(ph, "gb2", [128, D], F32)
            lng = sb(ph, "lng", [128, D], F32)
            lnb = sb(ph, "lnb", [128, D], F32)
            carry = sb(ph, "carry", [128, NFC, 2], F32)
            cw = sb(ph, "cw", [128, NFC, 3], F32)
            cb = sb(ph, "cb", [128, NFC], F32)
            st = sb(ph, "st", [128, 4, 3], F32)
            mv = sb(ph, "mv", [128, 4], F32)

            load_weight_bf16(stage, wup, "wup", wup_d.ap()[l].rearrange("(k p) n -> p k n", p=128), 8, 2 * DFF)
            load_weight_bf16(stage, wdn, "wdn", wdn_d.ap()[l].rearrange("(k p) n -> p k n", p=128), 22, D)
            ld(gb2[:], gbc_d.ap()[l, 1], "gb2", reads=["d_gbc"])
            ld(lng[:], lnp_d.ap()[l, 2], "lnp")
            P.op("sp", lambda e: e.dma_start(out=lnb[:], in_=lnp_d.ap()[l, 3]), multi=["lnp"], chan="lnpb")
            ld(cw[:], cw_d.ap()[l], "cw")
            ld(cb[:], cb_d.ap()[l], "cb")

            groups = [(0, 2)]
            r_ = 2
            while r_ < HALF + 2:
                n_ = min(GT, HALF + 2 - r_)
                groups.append((r_, n_))
                r_ += n_
            def p4_group(gidx_, row0, ntok):
                halo = gidx_ == 0
                s = gidx_ % 2
                ntt = (ntok + 127) // 128
                if halo:
                    ld(xin[s][0:2, 0, :], x1buf.ap()[0:2, :], "xin%d" % s, reads=["d_x1"])
                else:
                    nfull = ntok // 128
                    ld(xin[s][:, 0:nfull, :], x1buf.ap()[row0:row0 + nfull * 128, :].rearrange("(t p) d -> p t d", p=128),
                       "xin%d" % s, reads=["d_x1"])
                    if ntok % 128:
                        rem = ntok % 128
                        P.op("sp", lambda e, s=s, nfull=nfull, rem=rem, row0=row0: e.dma_start(
                            out=xin[s][0:rem, nfull, :], in_=x1buf.ap()[row0 + nfull * 128: row0 + nfull * 128 + rem, :]),
                            reads=["d_x1"], multi=["xin%d" % s], chan="xin%db" % s)
                ln_to_hT(xin[s], "xin%d" % s, ntok, xn[0], "xn0", hT[0], "hT0", st, mv, l, 32, 24, [0], "p4")
                for j in range(22):
                    for half_i, f in enumerate((j, 22 + j)):
                        bank = 1 + (j * 2 + half_i) % 4
                        pres = "ps%d" % bank
                        for k in range(8):
                            P.op("pe", lambda e, k=k, f=f, bank=bank: e.matmul(
                                ps[bank][:, 0:ntok], lhsT=wup[:, k, f * 128:(f + 1) * 128], rhs=hT[0][:, k, 0:ntok],
                                start=(k == 0), stop=(k == 7)), reads=["wup", "hT0"],
                                **({"writes": [pres]} if k == 0 else {"multi": [pres]}))
                        u = ps[bank]
                        if halo:
                            P.op("dve", lambda e, f=f, u=u: e.tensor_scalar(carry[:, f, :], u[:, 0:2], hv[:, 0:1], None, ALU.mult),
                                 reads=[pres, "hv"], multi=["carry"], osz=2)
                            continue
                        ts = (j * 2 + half_i) % 4
                        tt_ = t0[ts]
                        tres = "t0%d" % ts
                        n = ntok
                        P.op("act", lambda e, f=f, u=u, tt_=tt_, n=n: e.activation(
                            out=tt_[:, 0:n], in_=u[:, 0:n], func=AF.Identity, scale=cw[:, f, 2:3], bias=cb[:, f:f + 1]),
                            reads=[pres, "cw", "cb"], writes=[tres], osz=n)
                        P.op("dve", lambda e, f=f, u=u, tt_=tt_, n=n: e.scalar_tensor_tensor(
                            tt_[:, 1:n], u[:, 0:n - 1], cw[:, f, 1:2], tt_[:, 1:n], ALU.mult, ALU.add),
                            reads=[pres, "cw", tres], writes=[tres], osz=n)
                        P.op("dve", lambda e, f=f, tt_=tt_: e.scalar_tensor_tensor(
                            tt_[:, 0:1], carry[:, f, 1:2], cw[:, f, 1:2], tt_[:, 0:1], ALU.mult, ALU.add),
                            reads=["carry", tres], writes=[tres], osz=1)
                        P.op("dve", lambda e, f=f, u=u, tt_=tt_, n=n: e.scalar_tensor_tensor(
                            tt_[:, 2:n], u[:, 0:n - 2], cw[:, f, 0:1], tt_[:, 2:n], ALU.mult, ALU.add),
                            reads=[pres, tres], writes=[tres], osz=n)
                        P.op("dve", lambda e, f=f, tt_=tt_: e.scalar_tensor_tensor(
                            tt_[:, 0:2], carry[:, f, 0:2], cw[:, f, 0:1], tt_[:, 0:2], ALU.mult, ALU.add),
                            reads=["carry", tres], writes=[tres], osz=2)
                        P.op("dve", lambda e, f=f, u=u, n=n: e.tensor_copy(carry[:, f, :], u[:, n - 2:n]),
                             reads=[pres], multi=["carry"], osz=2)
                        if half_i == 0:
                            q = j % 2
                            P.op("act", lambda e, q=q, tt_=tt_, n=n: e.activation(out=sact[q][:, 0:n], in_=tt_[:, 0:n], func=AF.Silu),
                                 reads=[tres], writes=["sact%d" % q], osz=n)
                        else:
                            q = j % 2
                            P.op("pool", lambda e, q=q, tt_=tt_, n=n, j=j: e.tensor_tensor(
                                gT[:, j, 0:n], sact[q][:, 0:n], tt_[:, 0:n], ALU.mult),
                                reads=[tres, "sact%d" % q], multi=["gT"], osz=n)
                if halo:
                    return
                for tt in range(ntt):
                    n = min(128, ntok - tt * 128)
                    zs = tt % 2
                    zt = z[zs]
                    for hf in range(2):
                        bank = 5 + (tt * 2 + hf) % 3
                        for j in range(22):
                            P.op("pe", lambda e, j=j, tt=tt, n=n, hf=hf, bank=bank: e.matmul(
                                ps[bank][0:n, :], lhsT=gT[:, j, tt * 128:tt * 128 + n], rhs=wdn[:, j, hf * 512:(hf + 1) * 512],
                                start=(j == 0), stop=(j == 21)), reads=["wdn", "gT"],
                                **({"writes": ["ps%d" % bank]} if j == 0 else {"multi": ["ps%d" % bank]}))
                        P.op("dve", lambda e, n=n, hf=hf, bank=bank, zt=zt: e.tensor_tensor(
                            zt[0:n, hf * 512:(hf + 1) * 512], ps[bank][0:n, :], gb2[0:n, hf * 512:(hf + 1) * 512], ALU.mult),
                            reads=["ps%d" % bank, "gb2"], multi=["z%d" % zs])
                    P.op("dve", lambda e, n=n, tt=tt, zt=zt, s=s: e.scalar_tensor_tensor(
                        zt[0:n, :], xin[s][0:n, tt, :], ALPHA, zt[0:n, :], ALU.mult, ALU.add),
                        reads=["xin%d" % s], multi=["z%d" % zs])
                    final_ln(zt[0:n, :], n, st, mv, lng, lnb, "z%d" % zs, "p4z")
                    rr_ = row0 - 2 + tt * 128
                    if l == n_layers - 1:
                        P.op("pool", lambda e, zt=zt, n=n, rr_=rr_: e.dma_start(out=out_d.ap()[rr_:rr_ + n, :], in_=zt[0:n, :]),
                             reads=["z%d" % zs], multi=["d_out"], chan="z%d" % zs)
                    else:
                        P.op("pool", lambda e, zt=zt, n=n, rr_=rr_: e.dma_start(out=xs.ap()[rr_:rr_ + n, :], in_=zt[0:n, :]),
                             reads=["z%d" % zs], multi=["d_xs"], chan="z%d" % zs)
            for gidx_, (row0, ntok) in enumerate(groups):
                p4_group(gidx_, row0, ntok)
            if l < n_layers - 1:
                for j_ in range(8):
                    P.op("pool", lambda e, j_=j_: e.collective_compute(
                        "AllGather", ALU.bypass, replica_groups=PAIRS,
                        ins=[xs.ap()[j_ * 512:(j_ + 1) * 512, :]], outs=[xg.ap()[j_ * 1024:(j_ + 1) * 1024, :]]),
                        reads=["d_xs"], multi=["d_xg"], chan="cc_xg", inc=1)
            P.emit_phase(final_wait=(l == n_layers - 1))

    if debug:
        with contextlib.ExitStack() as ph:
            for name in debug:
                src = {"qaT": qaT, "kaT": kaT, "qcT": qcT, "kcT": kcT, "va": va_d, "vc": vc_d, "gref": gref_d,
                       "osend": osend, "og": og, "x1buf": x1buf, "xs": xs, "xg": xg, "gbc": gbc_d}[name]
                o = nc.dram_tensor("dbg_" + name, list(src.shape), src.dtype, kind="ExternalOutput")
                dbg[name] = o
                P.op("sp", lambda e, o=o, src=src: e.dma_start(out=o.ap(), in_=src.ap()),
                     reads=["d_" + name, "d_" + src.name], multi=["dbgout"], chan="dbg_" + name)
            if "G_sb" in debug or True:
                o = nc.dram_tensor("dbg_G", [128, 256], F32, kind="ExternalOutput")
                P.op("sp", lambda e, o=o: e.dma_start(out=o.ap(), in_=G_sb[:].rearrange("p h b -> p (h b)")),
                     reads=["G_sb"], multi=["dbgout"], chan="dbg_G")
                o2 = nc.dram_tensor("dbg_modc", [128, NL * 48], F32, kind="ExternalOutput")
                P.op("sp", lambda e, o2=o2: e.dma_start(out=o2.ap(), in_=modc[:].rearrange("p l c -> p (l c)")),
                     reads=["modc"], multi=["dbgout"], chan="dbg_modc")
            P.emit_phase(final_wait=True)
    es.close()
    _NC_CACHE["trace"] = P.trace
    return nc


def _consts():
    bf = ml_dtypes.bfloat16
    p = np.arange(128)
    ident = np.eye(128, dtype=np.float32)
    tri = (p[:, None] <= p[None, :]).astype(np.float32)
    mask01 = tri.astype(bf)
    q = np.arange(64)
    tri64s = (q[:, None] < q[None, :]).astype(np.float32)
    tri64i = (q[:, None] <= q[None, :]).astype(np.float32)
    E = (np.arange(S)[None, :] // 128 == q[:, None]).astype(np.float32).astype(bf)
    m = np.arange(TW) - 384
    qc = np.floor_divide(m, 64)[None, :]
    kc = (p // 64)[:, None]
    vis = ((qc - kc >= 0) & (qc - kc <= 8)).astype(np.float32)
    relidx = np.clip(m[None, :] - p[:, None], -128, 128) + 128
    return dict(ident=ident, mask01=mask01, tri=tri, tri64s=tri64s, tri64i=tri64i, Emat=E, vis=vis), relidx


def prep_inputs(x, c, w_in, b_f, rel_bias, w_br_fox, w_br_chunk, w_out, w_up, conv_w, conv_b, w_down,
                w_ada, b_ada, ln1_g, ln1_b, ln2_g, ln2_b):
    f32 = np.float32
    A = lambda a: np.ascontiguousarray(np.asarray(a), dtype=f32)
    x, c, w_in, b_f, rel_bias = A(x), A(c), A(w_in), A(b_f), A(rel_bias)
    w_br_fox, w_br_chunk, w_out, w_up = A(w_br_fox), A(w_br_chunk), A(w_out), A(w_up)
    conv_w, conv_b, w_down, w_ada, b_ada = A(conv_w), A(conv_b), A(w_down), A(w_ada), A(b_ada)
    ln1_g, ln1_b, ln2_g, ln2_b = A(ln1_g), A(ln1_b), A(ln2_g), A(ln2_b)
    consts, relidx = _consts()
    shared = dict(consts)
    shared["wg"] = np.ascontiguousarray(w_in[:, :, 3080:5128])
    shared["wout"] = w_out
    shared["wup"] = w_up
    shared["wdn"] = w_down
    shared["wada"] = w_ada
    shared["cw"] = np.ascontiguousarray(conv_w.reshape(NL, 3, NFC, 128).transpose(0, 3, 2, 1))
    shared["cb"] = np.ascontiguousarray(conv_b.reshape(NL, NFC, 128).transpose(0, 2, 1))
    shared["badac"] = np.ascontiguousarray(b_ada.reshape(NL, 48, 128).transpose(0, 2, 1))
    shared["badar"] = np.ascontiguousarray(np.stack([b_ada[:, 2 * D:3 * D], b_ada[:, 5 * D:6 * D]], axis=1))
    lnp = np.stack([ln1_g, ln1_b, ln2_g, ln2_b], axis=1)
    shared["lnp"] = np.ascontiguousarray(np.broadcast_to(lnp[:, :, None, :], (NL, 4, 128, D)))
    brorder = [("f", 0), ("f", 2), ("f", 1), ("f", 3), ("c", 0), ("c", 2), ("c", 1), ("c", 3)]
    shared["wbr"] = np.ascontiguousarray(np.concatenate(
        [(w_br_fox if t == "f" else w_br_chunk)[:, i * 128:(i + 1) * 128, :] for t, i in brorder], axis=1))
    maps = []
    for core in range(8):
        b, r = core // 2, core % 2
        m = dict(shared)
        m["xfull"] = x[b]
        xm = np.zeros((HALF + 2, D), f32)
        xm[2:] = x[b, r * HALF:(r + 1) * HALF]
        if r == 1:
            xm[0:2] = x[b, HALF - 2:HALF]
        m["xmine"] = xm
        m["hv"] = np.full((128, 1), float(r), f32)
        m["ccol"] = np.ascontiguousarray(c[b].reshape(8, 128).T)
        cs = lambda base: slice(base + 256 * r, base + 256 * r + 256)
        m["wqk"] = np.ascontiguousarray(np.concatenate(
            [w_in[:, :, cs(0)], w_in[:, :, cs(512)], w_in[:, :, cs(1544)], w_in[:, :, cs(2056)]], axis=2))
        m["wv"] = np.ascontiguousarray(np.concatenate([w_in[:, :, cs(1024)], w_in[:, :, cs(2568)]], axis=2))
        m["wf"] = np.ascontiguousarray(w_in[:, :, 1536 + 4 * r:1536 + 4 * r + 4])
        m["bfbc"] = np.ascontiguousarray(np.broadcast_to(b_f[:, None, 4 * r:4 * r + 4], (NL, 128, 4)))
        m["ttab"] = np.ascontiguousarray(rel_bias[:, 4 * r:4 * r + 4][:, :, relidx])
        maps.append(m)
    return maps


_NC_CACHE = {}


def kernel(**inputs):
    maps = prep_inputs(**inputs)
    if "nc" not in _NC_CACHE:
        _NC_CACHE["nc"] = build_program()
    nc = _NC_CACHE["nc"]
    res = run_bass_kernel_spmd(nc, maps, core_ids=list(range(8)))
    out = np.empty((4, S, D), np.float32)
    for core in range(8):
        b, r = core // 2, core % 2
        out[b, r * HALF:(r + 1) * HALF] = np.asarray(res.results[core]["out"], dtype=np.float32)
    return out
```

```python
import contextlib
import os
import numpy as np
import ml_dtypes
import concourse.bass as bass
import concourse.mybir as mybir
from concourse.bass_utils import run_bass_kernel_spmd

F32 = mybir.dt.float32
BF16 = mybir.dt.bfloat16
ALU = mybir.AluOpType
AF = mybir.ActivationFunctionType

D = 1024
S = 8192
HALF = 4096
DFF = 2816
NFC = 44
EPS = 1e-5
ALPHA = float(4.0 ** 0.25)
NL = 2
PAIRS = [[0, 1], [2, 3], [4, 5], [6, 7]]
TW = 1408
G4 = 256


class Prog:
    ENGS = ("pe", "act", "dve", "pool", "sp")

    def __init__(self, nc, es):
        self.nc = nc
        self.es = es
        self.engobj = {"pe": nc.tensor, "act": nc.scalar, "dve": nc.vector, "pool": nc.gpsimd, "sp": nc.sync}
        self.ops = {e: [] for e in self.ENGS}
        self.esem = {e: es.enter_context(nc.semaphore("sem_" + e)) for e in ("pe", "act", "dve", "pool")}
        self.ecount = {e: 0 for e in self.esem}
        self.chans = {}
        self.res = {}
        self.waited = {e: {} for e in self.ENGS}
        self.gidx = {e: 0 for e in self.ENGS}
        self.phase_base = {e: 0 for e in self.ENGS}
        self.sigval_at_phase_start = {e: 0 for e in self.esem}
        self.pid = None
        self.cumdur = {e: 0 for e in self.ENGS}
        self.trace = {e: [] for e in self.ENGS}

    def chan(self, key):
        if key not in self.chans:
            self.chans[key] = [self.es.enter_context(self.nc.semaphore("c%d" % len(self.chans))), 0]
        return self.chans[key]

    def _r(self, name):
        if name not in self.res:
            self.res[name] = {"w": [], "r": [], "pr": []}
        return self.res[name]

    def op(self, eng, fn, reads=(), writes=(), multi=(), chan=None, inc=16, osz=512, dur=None):
        deps = []
        for n in reads:
            r = self._r(n)
            for d in r["w"]:
                if d["eng"] == eng and d["chan"] is None:
                    if eng == "pe" or d["osz"] >= 256 or (self.cumdur[eng] - d["cum"]) >= 256:
                        continue
                    d = dict(d, raw=True, orig=d)
                deps.append(d)
        for n in writes:
            r = self._r(n)
            deps += r["w"] + r["r"] + r["pr"]
        for n in multi:
            r = self._r(n)
            deps += r["r"] + r["pr"]
        if dur is None:
            dur = osz
        self.cumdur[eng] += dur
        rec = {"fn": fn, "eng": eng, "idx": self.gidx[eng], "chan": None, "deps": [], "sig": False,
               "osz": osz, "cum": self.cumdur[eng]}
        self.gidx[eng] += 1
        need = {}
        for d in deps:
            if d["chan"] is not None:
                key = ("c", d["chan"][0])
                val = self.chans[d["chan"][0]][1]
                need[key] = max(need.get(key, 0), val)
            else:
                if d["eng"] == eng:
                    if not d.get("raw"):
                        continue
                    d = d["orig"]
                key = ("e", d["eng"])
                if key not in need or need[key]["idx"] < d["idx"]:
                    need[key] = d
        rec["need"] = need
        if chan is not None:
            c = self.chan(chan)
            c[1] += inc
            rec["chan"] = (chan, c[1], inc)
        self.ops[eng].append(rec)
        me = rec
        for n in reads:
            self._r(n)["r"].append(me)
        for n in writes:
            r = self._r(n)
            r["w"] = [me]
            r["r"] = []
            r["pr"] = []
        for n in multi:
            r = self._r(n)
            if r["r"]:
                r["pr"] = r["r"]
                r["r"] = []
                r["w"] = [me]
            else:
                r["w"].append(me)
        return rec

    def emit_phase(self, final_wait=True):
        for e in self.ENGS:
            for rec in self.ops[e]:
                for key, d in rec["need"].items():
                    if key[0] == "e":
                        if d["idx"] >= self.phase_base[d["eng"]]:
                            d["sig"] = True
        for e in self.esem:
            for rec in reversed(self.ops[e]):
                if rec["chan"] is None:
                    rec["sig"] = True
                    break
        sigval = {}
        for e in self.esem:
            cnt = self.ecount[e]
            for rec in self.ops[e]:
                if rec["chan"] is not None:
                    rec["sig"] = False
                if rec["sig"]:
                    cnt += 1
                    rec["sigval"] = cnt
            nxt = cnt
            for rec in reversed(self.ops[e]):
                if rec["sig"]:
                    nxt = rec["sigval"]
                rec["cover"] = nxt
            sigval[e] = cnt
        prog = self

        def run(e):
            eng_ops = prog.ops[e]

            def body(eng):
                w = prog.waited[e]
                if e == "pool" and prog.pid is None:
                    prog.pid = eng.partition_id()
                    rank = prog.pid % 2
                    prog.c0 = eng.snap(rank * HALF)
                    prog.c1 = eng.snap(rank * (HALF - 2))
                for rec in eng_ops:
                    for key, d in rec["need"].items():
                        if key[0] == "c":
                            sem = prog.chans[key[1]][0]
                            val = d
                        else:
                            sem = prog.esem[key[1]]
                            if d["idx"] >= prog.phase_base[key[1]]:
                                val = d["cover"]
                            else:
                                val = prog.sigval_at_phase_start[key[1]]
                        if w.get(key, 0) >= val:
                            continue
                        w[key] = val
                        eng.wait_ge(sem, val)
                        prog.trace[e].append(("wait", key, val))
                    ins = rec["fn"](eng)
                    if rec["chan"] is not None:
                        ins.then_inc(prog.chans[rec["chan"][0]][0], rec["chan"][2])
                        prog.trace[e].append(("inc", ("c", rec["chan"][0]), rec["chan"][2]))
                    elif rec["sig"]:
                        ins.then_inc(prog.esem[e], 1)
                        prog.trace[e].append(("inc", ("e", e), 1))
                if final_wait and e in ("sp", "pool"):
                    for key, (sem, cnt) in prog.chans.items():
                        if cnt > 0 and w.get(("c", key), 0) < cnt:
                            eng.wait_ge(sem, cnt)
                            prog.trace[e].append(("wait", ("c", key), cnt))
                prog.trace[e].append(("barrier", None, 0))
            return body

        with self.nc.Block() as block:
            block.tensor(run("pe"))
            block.scalar(run("act"))
            block.vector(run("dve"))
            block.gpsimd(run("pool"))
            block.sync(run("sp"))
        for e in self.esem:
            self.ecount[e] = sigval[e]
            self.sigval_at_phase_start[e] = sigval[e]
        for e in self.ENGS:
            self.phase_base[e] = self.gidx[e]
            self.ops[e] = []


_NC_CACHE = {}


def build_program(debug=None, n_layers=NL, stop_after=None):
    nc = bass.Bass("TRN2", target_bir_lowering=False)
    es = contextlib.ExitStack()
    P = Prog(nc, es)

    def din(name, shape, dt=F32):
        return nc.dram_tensor(name, list(shape), dt, kind="ExternalInput")

    xfull = din("xfull", [S, D])
    xmine = din("xmine", [HALF + 2, D])
    hv_d = din("hv", [128, 1])
    ccol_d = din("ccol", [128, 8])
    wqk_d = din("wqk", [NL, D, 1024])
    wv_d = din("wv", [NL, D, 512])
    wf_d = din("wf", [NL, D, 4])
    bfbc_d = din("bfbc", [NL, 128, 4])
    wg_d = din("wg", [NL, D, 2048])
    ttab_d = din("ttab", [NL, 4, 128, TW])
    vis_d = din("vis", [128, TW])
    wbr_d = din("wbr", [NL, D, D])
    wout_d = din("wout", [NL, D, D])
    wup_d = din("wup", [NL, D, 2 * DFF])
    cw_d = din("cw", [NL, 128, NFC, 3])
    cb_d = din("cb", [NL, 128, NFC])
    wdn_d = din("wdn", [NL, DFF, D])
    wada_d = din("wada", [NL, D, 6 * D])
    badac_d = din("badac", [NL, 128, 48])
    badar_d = din("badar", [NL, 2, D])
    lnp_d = din("lnp", [NL, 4, 128, D])
    ident_d = din("ident", [128, 128], F32)
    mask_d = din("mask01", [128, 128], BF16)
    tri_d = din("tri", [128, 128])
    tri64s_d = din("tri64s", [64, 64])
    tri64i_d = din("tri64i", [64, 64])
    E_d = din("Emat", [64, S], BF16)
    out_d = nc.dram_tensor("out", [HALF, D], F32, kind="ExternalOutput")

    qaT = nc.dram_tensor("qaT", [256, S], BF16)
    kaT = nc.dram_tensor("kaT", [256, S], BF16)
    qcT = nc.dram_tensor("qcT", [256, S], BF16)
    kcT = nc.dram_tensor("kcT", [256, S], BF16)
    va_d = nc.dram_tensor("va", [S, 260], BF16)
    vc_d = nc.dram_tensor("vc", [S, 260], BF16)
    gref_d = nc.dram_tensor("gref", [4, 3, S], BF16)
    osend = nc.dram_tensor("osend", [512, S], BF16)
    og = nc.dram_tensor("og", [1024, S], BF16)
    x1buf = nc.dram_tensor("x1buf", [HALF + 2, D], F32)
    xs = nc.dram_tensor("xs", [HALF, D], F32)
    xg = nc.dram_tensor("xg", [S, D], F32)
    gbc_d = nc.dram_tensor("gbc", [NL, 2, 128, D], F32)
    dbg = {}

    uniq = [0]

    def sb(stack, name, shape, dt):
        uniq[0] += 1
        return stack.enter_context(nc.sbuf_tensor("s%d_%s" % (uniq[0], name), list(shape), dt))

    ps = [es.enter_context(nc.psum_tensor("ps%d" % i, [128, 512], F32)) for i in range(8)]
    psb = [p[:].bitcast(BF16) for p in ps]
    ident = sb(es, "ident", [128, 128], F32)
    mask01 = sb(es, "mask01", [128, 128], BF16)
    tri = sb(es, "tri", [128, 128], F32)
    tri64s = sb(es, "tri64s", [64, 64], F32)
    tri64i = sb(es, "tri64i", [64, 64], F32)
    ones32 = sb(es, "ones32", [128, 128], F32)
    hv = sb(es, "hv", [128, 1], F32)
    modc = sb(es, "modc", [128, NL, 48], F32)
    G_sb = sb(es, "G_sb", [128, 4, 64], F32)

    def ld(dst_ap, src_ap, res, eng="sp", reads=()):
        P.op(eng, lambda e: e.dma_start(out=dst_ap, in_=src_ap), reads=reads, writes=[res], chan=res)

    ld(ident[:], ident_d.ap(), "ident")
    ld(mask01[:], mask_d.ap(), "mask01")
    ld(tri[:], tri_d.ap(), "tri")
    ld(tri64s[:], tri64s_d.ap(), "tri64s")
    ld(tri64i[:], tri64i_d.ap(), "tri64i")
    ld(hv[:], hv_d.ap(), "hv")
    P.op("dve", lambda e: e.memset(ones32[:], 1.0), writes=["ones32"])

    rr = [0]

    def cast_eng():
        rr[0] += 1
        return ("pool", "dve", "act")[rr[0] % 3]

    def cast_op(eng, dst, src, reads, writes=(), multi=()):
        if eng == "act":
            P.op("act", lambda e: e.activation(out=dst, in_=src, func=AF.Copy), reads=reads, writes=writes, multi=multi)
        else:
            P.op(eng, lambda e: e.tensor_copy(dst, src), reads=reads, writes=writes, multi=multi)

    def load_weight_bf16(stage, dst, dst_res, src3, K, N):
        piece = stage[0].shape[1]
        i = 0
        for k in range(K):
            for n0 in range(0, N, piece):
                n1 = min(N, n0 + piece)
                s = i % 2
                i += 1
                sres = "wstage%d" % s
                st = stage[s]
                ld(st[:, 0:n1 - n0], src3[:, k, n0:n1], sres)
                cast_op(cast_eng(), dst[:, k, n0:n1], st[:, 0:n1 - n0], reads=[sres], multi=[dst_res])

    with contextlib.ExitStack() as ph:
        ccol = sb(ph, "ccol", [128, 8], F32)
        condT = sb(ph, "condT", [128, 8], F32)
        crep = sb(ph, "crep", [128, 8, 128], F32)
        wst = [sb(ph, "wadast%d" % i, [128, 8, 1024], F32) for i in range(2)]
        bcol = sb(ph, "bcol", [128, 48], F32)
        brow = sb(ph, "brow", [1, 2, D], F32)
        gtile = sb(ph, "gtile", [128, D], F32)
        ld(ccol[:], ccol_d.ap(), "ccol")
        P.op("act", lambda e: e.activation(out=condT[:], in_=ccol[:], func=AF.Silu), reads=["ccol"], writes=["condT"], osz=8)
        for k in range(8):
            P.op("dve", lambda e, k=k: e.tensor_scalar(crep[:, k, :], ones32[:], condT[:, k:k + 1], None, ALU.mult),
                 reads=["ones32", "condT"], multi=["crep"], osz=128)
        for l in range(n_layers):
            ld(bcol[:], badac_d.ap()[l], "bcol")
            ld(brow[:], badar_d.ap()[l:l + 1], "brow")
            w3 = wada_d.ap()[l].rearrange("(k p) n -> p k n", p=128)
            for pc in range(6):
                s = pc % 2
                ld(wst[s][:], w3[:, :, pc * 1024:(pc + 1) * 1024], "wadast%d" % s)
                for j in range(8):
                    col = pc * 8 + j
                    for k in range(8):
                        P.op("pe", lambda e, s=s, j=j, k=k, col=col: e.matmul(
                            ps[0][:, col:col + 1], lhsT=wst[s][:, k, j * 128:(j + 1) * 128], rhs=condT[:, k:k + 1],
                            start=(k == 0), stop=(k == 7)),
                            reads=["wadast%d" % s, "condT"], multi=["ps0"])
                if pc in (2, 5):
                    gi = 0 if pc == 2 else 1
                    for hf in range(2):
                        bank = 1 + hf
                        for k in range(8):
                            P.op("pe", lambda e, s=s, k=k, hf=hf, bank=bank: e.matmul(
                                ps[bank][:, :], lhsT=crep[:, k, :], rhs=wst[s][:, k, hf * 512:(hf + 1) * 512],
                                start=(k == 0), stop=False),
                                reads=["wadast%d" % s, "crep"], multi=["ps%d" % bank])
                        P.op("pe", lambda e, gi=gi, hf=hf, bank=bank: e.matmul(
                            ps[bank][:, :], lhsT=ones32[0:1, :], rhs=brow[0:1, gi, hf * 512:(hf + 1) * 512],
                            start=False, stop=True), reads=["ones32", "brow"], multi=["ps%d" % bank])
                        P.op("dve", lambda e, hf=hf, bank=bank: e.tensor_scalar(
                            gtile[:, hf * 512:(hf + 1) * 512], ps[bank][:, :], 1.0, None, ALU.add),
                            reads=["ps%d" % bank], multi=["gtile"])
                    P.op("pool", lambda e, l=l, gi=gi: e.dma_start(out=gbc_d.ap()[l, gi], in_=gtile[:]),
                         reads=["gtile"], multi=["d_gbc"], chan="gtile")
            P.op("dve", lambda e, l=l: e.tensor_tensor(modc[:, l, :], ps[0][:, 0:48], bcol[:], ALU.add),
                 reads=["ps0", "bcol"], multi=["modc"], osz=48)
            for c0 in (8, 32):
                P.op("dve", lambda e, l=l, c0=c0: e.tensor_scalar(
                    modc[:, l, c0:c0 + 8], modc[:, l, c0:c0 + 8], 1.0, None, ALU.add), reads=["modc"], multi=["modc"], osz=8)
        P.emit_phase()

    def ln_stats(xap, n, st, mv, res_in, tag):
        for hf in range(2):
            P.op("dve", lambda e, hf=hf: e.bn_stats(st[0:n, 2 * hf:2 * hf + 2, :], xap[:, hf * 512:(hf + 1) * 512]),
                 reads=[res_in], multi=[tag + "st"], osz=6, dur=512)
        P.op("dve", lambda e: e.bn_aggr(mv[0:n, 0:2], st[0:n, :, :]), reads=[tag + "st"], multi=[tag + "mv"], osz=2)
        P.op("act", lambda e: e.activation(out=mv[0:n, 2:3], in_=mv[0:n, 1:2], func=AF.Ln, bias=EPS),
             reads=[tag + "mv"], multi=[tag + "mv"], osz=1)
        P.op("act", lambda e: e.activation(out=mv[0:n, 2:3], in_=mv[0:n, 2:3], func=AF.Exp, scale=-0.5),
             reads=[tag + "mv"], multi=[tag + "mv"], osz=1)
        P.op("dve", lambda e: e.tensor_scalar(mv[0:n, 3:4], mv[0:n, 0:1], mv[0:n, 2:3], -1.0, ALU.mult, ALU.mult),
             reads=[tag + "mv"], multi=[tag + "mv"], osz=1)

    def ln_to_hT(xin_tile, xin_res, ntok, xn, xn_res, hT, hT_res, st, mv, l, sc_col, sh_col, tbanks, tag):
        ntt = (ntok + 127) // 128
        for tt in range(ntt):
            n = min(128, ntok - tt * 128)
            ln_stats(xin_tile[0:n, tt, :], n, st, mv, xin_res, tag)
            P.op("act", lambda e, tt=tt, n=n: e.activation(out=xn[0:n, tt, :], in_=xin_tile[0:n, tt, :], func=AF.Identity,
                                                           scale=mv[0:n, 2:3], bias=mv[0:n, 3:4]),
                 reads=[xin_res, tag + "mv"], multi=[xn_res])
        for k in range(8):
            bank = tbanks[k % len(tbanks)]
            off = 0
            pres = "ps%d" % bank
            for tt in range(ntt):
                n = min(128, ntok - tt * 128)
                P.op("pe", lambda e, tt=tt, n=n, k=k, bank=bank, off=off: e.transpose(
                    ps[bank][:, off + tt * 128: off + tt * 128 + n], xn[0:n, tt, k * 128:(k + 1) * 128], ident[0:n, 0:n]),
                    reads=[xn_res, "ident"], multi=[pres])
            P.op("act", lambda e, k=k, bank=bank, off=off: e.activation(
                out=hT[:, k, 0:ntok], in_=ps[bank][:, off:off + ntok], func=AF.Identity,
                scale=modc[:, l, sc_col + k:sc_col + k + 1], bias=modc[:, l, sh_col + k:sh_col + k + 1]),
                reads=[pres, "modc"], multi=[hT_res], osz=ntok)

    def final_ln(z, n, st, mv, lng, lnb, zres, tag):
        ln_stats(z, n, st, mv, zres, tag)
        P.op("act", lambda e: e.activation(out=z, in_=z, func=AF.Identity, scale=mv[0:n, 2:3], bias=mv[0:n, 3:4]),
             reads=[tag + "mv", zres], writes=[zres])
        P.op("dve", lambda e: e.tensor_tensor(z, z, lng[0:n, :], ALU.mult), reads=["lnp", zres], writes=[zres])
        P.op("dve", lambda e: e.tensor_tensor(z, z, lnb[0:n, :], ALU.add), reads=["lnp", zres], writes=[zres])

    for l in range(n_layers if stop_after != ("p0", 0) else 0):
        xsrc_full = xfull.ap() if l == 0 else xg.ap()
        xsrc_full_res = "d_xfull" if l == 0 else "d_xg"

        with contextlib.ExitStack() as ph:
            stage = [sb(ph, "wstage%d" % i, [128, 2048], F32) for i in range(2)]
            wqk = sb(ph, "wqk", [128, 8, 1024], BF16)
            wv = sb(ph, "wv", [128, 8, 512], BF16)
            wf = sb(ph, "wf", [128, 8, 4], BF16)
            bfbc = sb(ph, "bfbc", [128, 4], F32)
            xin = [sb(ph, "xin%d" % i, [128, 4, D], F32) for i in range(2)]
            xn = [sb(ph, "xn%d" % i, [128, 4, D], F32) for i in range(2)]
            hT = [sb(ph, "hT%d" % i, [128, 8, 512], BF16) for i in range(2)]
            qko = [sb(ph, "qko%d" % i, [128, 512], BF16) for i in range(4)]
            vout = [sb(ph, "vout%d" % i, [128, 8, 65], BF16) for i in range(2)]
            fsb = sb(ph, "fsb", [128, 4, 64], F32)
            st = sb(ph, "st", [128, 4, 3], F32)
            mv = sb(ph, "mv", [128, 4], F32)
            Emat = sb(ph, "Emat", [64, S], BF16)
            tot = sb(ph, "tot", [64, 1], F32)
            totrep = sb(ph, "totrep", [64, 128], F32)
            r0 = sb(ph, "r0", [64, 4], F32)
            vals = sb(ph, "vals", [64, 3], BF16)
            v32 = sb(ph, "v32", [64, 1], F32)
            grow = [sb(ph, "grow%d" % i, [3, 512], BF16) for i in range(2)]

            load_weight_bf16(stage, wqk, "wqk", wqk_d.ap()[l].rearrange("(k p) n -> p k n", p=128), 8, 1024)
            load_weight_bf16(stage, wv, "wv", wv_d.ap()[l].rearrange("(k p) n -> p k n", p=128), 8, 512)
            load_weight_bf16(stage, wf, "wf", wf_d.ap()[l].rearrange("(k p) n -> p k n", p=128), 8, 4)
            ld(bfbc[:], bfbc_d.ap()[l], "bfbc")
            ld(Emat[:], E_d.ap(), "Emat")
            for i in range(2):
                P.op("pool", lambda e, i=i: e.memset(vout[i][:], 1.0), writes=["vout%d" % i])

            dsts = [qaT, qaT, kaT, kaT, qcT, qcT, kcT, kcT]
            for g in range(int(os.environ.get('K_P1_GROUPS', '16'))):
                s = g % 2
                tok0 = g * 512
                srow = tok0 if l == 0 else ((g % 8) * 2 + g // 8) * 512
                ld(xin[s][:], xsrc_full[srow:srow + 512, :].rearrange("(t p) d -> p t d", p=128), "xin%d" % s,
                   reads=[xsrc_full_res])
                if int(os.environ.get('K_P1_STEPS', '9')) >= 2:
                    ln_to_hT(xin[s], "xin%d" % s, 512, xn[s], "xn%d" % s, hT[s], "hT%d" % s, st, mv, l, 8, 0, [0, 1], "p1")
                STEPS = int(os.environ.get('K_P1_STEPS', '9'))
                for m in range(8 if STEPS >= 3 else 0):
                    bank = 2 + m % 3
                    for k in range(8):
                        P.op("pe", lambda e, m=m, k=k, bank=bank, s=s: e.matmul(
                            ps[bank][:, :], lhsT=wqk[:, k, m * 128:(m + 1) * 128], rhs=hT[s][:, k, :],
                            start=(k == 0), stop=(k == 7)), reads=["wqk", "hT%d" % s], multi=["ps%d" % bank])
                    qs = (g * 8 + m) % 4
                    scale = 0.125 if m in (0, 1, 4, 5) else 1.0
                    if m % 2 == 0:
                        P.op("act", lambda e, bank=bank, qs=qs, scale=scale: e.activation(
                            out=qko[qs][:], in_=ps[bank][:, :], func=AF.Copy, scale=scale),
                            reads=["ps%d" % bank], writes=["qko%d" % qs])
                    else:
                        P.op("dve", lambda e, bank=bank, qs=qs, scale=scale: e.tensor_scalar(
                            qko[qs][:], ps[bank][:, :], scale, None, ALU.mult),
                            reads=["ps%d" % bank], writes=["qko%d" % qs])
                    dst = dsts[m]
                    r0_ = (m % 2) * 128
                    P.op("pool", lambda e, dst=dst, r0_=r0_, qs=qs, tok0=tok0: e.dma_start(
                        out=dst.ap()[r0_:r0_ + 128, tok0:tok0 + 512], in_=qko[qs][:]),
                        reads=["qko%d" % qs], multi=["d_" + dst.name], chan="qko%d" % qs)
                for tt in range(4 if STEPS >= 4 else 0):
                    bank = 6 + tt % 2
                    for k in range(8):
                        P.op("pe", lambda e, tt=tt, k=k, bank=bank, s=s: e.matmul(
                            ps[bank][:, :], lhsT=hT[s][:, k, tt * 128:(tt + 1) * 128], rhs=wv[:, k, :],
                            start=(k == 0), stop=(k == 7)), reads=["wv", "hT%d" % s], multi=["ps%d" % bank])
                    vs = (g * 4 + tt) % 2
                    P.op("dve", lambda e, bank=bank, vs=vs: e.tensor_copy(
                        vout[vs][:, :, 1:65], ps[bank][:, :].rearrange("p (h c) -> p h c", c=64)),
                        reads=["ps%d" % bank], multi=["vout%d" % vs])
                    t0_ = tok0 + tt * 128
                    P.op("pool", lambda e, vs=vs, t0_=t0_: e.dma_start(
                        out=va_d.ap()[t0_:t0_ + 128, :], in_=vout[vs][:, 0:4, :].rearrange("p h c -> p (h c)")),
                        reads=["vout%d" % vs], multi=["d_va"], chan="vout%d" % vs)
                    P.op("pool", lambda e, vs=vs, t0_=t0_: e.dma_start(
                        out=vc_d.ap()[t0_:t0_ + 128, :], in_=vout[vs][:, 4:8, :].rearrange("p h c -> p (h c)")),
                        reads=["vout%d" % vs], multi=["d_vc"], chan="vout%d" % vs)
                for tt in range(4 if STEPS >= 5 else 0):
                    for k in range(8):
                        P.op("pe", lambda e, tt=tt, k=k, s=s: e.matmul(
                            ps[5][:, tt * 4: tt * 4 + 4], lhsT=hT[s][:, k, tt * 128:(tt + 1) * 128],
                            rhs=wf[:, k, :], start=(k == 0), stop=(k == 7)),
                            reads=["wf", "hT%d" % s], multi=["ps5"])
                for tt in range(4 if STEPS >= 5 else 0):
                    blk = g * 4 + tt
                    P.op("dve", lambda e, tt=tt, blk=blk: e.tensor_tensor(
                        fsb[:, :, blk], ps[5][:, tt * 4: tt * 4 + 4], bfbc[:], ALU.add),
                        reads=["ps5", "bfbc"], multi=["fsb"], osz=4)
            P.op("act", lambda e: e.activation(out=fsb[:], in_=fsb[:], func=AF.Exp, scale=-1.0), reads=["fsb"], writes=["fsb"], osz=255)
            P.op("act", lambda e: e.activation(out=fsb[:], in_=fsb[:], func=AF.Ln, bias=1.0), reads=["fsb"], writes=["fsb"], osz=255)
            for h in range(int(os.environ.get('K_P1_TAIL', '4'))):
                P.op("pe", lambda e, h=h: e.matmul(ps[2][0:64, 0:1], lhsT=fsb[:, h, :], rhs=ones32[:, 0:1], start=True, stop=True),
                     reads=["fsb", "ones32"], writes=["ps2"])
                P.op("dve", lambda e: e.tensor_copy(tot[:], ps[2][0:64, 0:1]), reads=["ps2"], writes=["tot"], osz=1)
                P.op("dve", lambda e: e.tensor_scalar(totrep[:], ones32[0:64, :], tot[:, 0:1], None, ALU.mult),
                     reads=["tot", "ones32"], writes=["totrep"], osz=128)
                P.op("pe", lambda e, h=h: e.matmul(ps[3][:, 0:64], lhsT=tri[:], rhs=fsb[:, h, :], start=True, stop=False),
                     reads=["fsb", "tri"], writes=["ps3"])
                P.op("pe", lambda e: e.matmul(ps[3][:, 0:64], lhsT=totrep[:], rhs=tri64s[:], start=False, stop=True),
                     reads=["totrep", "tri64s"], multi=["ps3"])
                P.op("dve", lambda e, h=h: e.tensor_copy(G_sb[:, h, :], ps[3][:, 0:64]), reads=["ps3"], multi=["G_sb"], osz=64)
                P.op("pe", lambda e: e.matmul(ps[4][0:64, 0:1], lhsT=tri64i[:], rhs=tot[:], start=True, stop=True),
                     reads=["tot", "tri64i"], writes=["ps4"])
                P.op("dve", lambda e: e.tensor_scalar(r0[:, 0:1], ps[4][0:64, 0:1], -1.0, None, ALU.mult),
                     reads=["ps4"], writes=["r0"], osz=1)
                for i in range(3):
                    P.op("dve", lambda e, i=i: e.tensor_copy(vals[:, i:i + 1], r0[:, i:i + 1]), reads=["r0"], multi=["vals"], osz=1)
                    if i < 2:
                        P.op("dve", lambda e, i=i: e.tensor_copy(v32[:], vals[:, i:i + 1]), reads=["vals"], writes=["v32"], osz=1)
                        P.op("dve", lambda e, i=i: e.tensor_tensor(r0[:, i + 1:i + 2], r0[:, i:i + 1], v32[:], ALU.subtract),
                             reads=["v32", "r0"], multi=["r0"], osz=1)
                for j in range(16):
                    bank = 5 + j % 2
                    gs = j % 2
                    P.op("pe", lambda e, j=j, bank=bank: e.matmul(
                        ps[bank][0:3, :], lhsT=vals[:, :], rhs=Emat[:, j * 512:(j + 1) * 512], start=True, stop=True),
                        reads=["vals", "Emat"], writes=["ps%d" % bank])
                    P.op("act", lambda e, bank=bank, gs=gs: e.activation(out=grow[gs][:], in_=ps[bank][0:3, :], func=AF.Copy),
                         reads=["ps%d" % bank], writes=["grow%d" % gs])
                    P.op("pool", lambda e, h=h, j=j, gs=gs: e.dma_start(
                        out=gref_d.ap()[h, :, j * 512:(j + 1) * 512], in_=grow[gs][:]),
                        reads=["grow%d" % gs], multi=["d_gref"], chan="grow%d" % gs)
            P.emit_phase()
        if stop_after == ("p1", l):
            break

        with contextlib.ExitStack() as ph:
            kT = [sb(ph, "kT%d" % i, [128, S], BF16) for i in range(2)]
            qT = [sb(ph, "qT%d" % i, [128, S], BF16) for i in range(2)]
            vx = [sb(ph, "vx%d" % i, [128, 64, 260], BF16) for i in range(2)]
            pT = [sb(ph, "pT%d" % i, [128, 512], BF16) for i in range(4)]
            pf = [sb(ph, "pf%d" % i, [128, 512], F32) for i in range(2)]
            osb = [sb(ph, "osb%d" % i, [96, 512], BF16) for i in range(2)]
            otmp = sb(ph, "otmp", [96, 512], F32)
            rl = sb(ph, "rl", [128, 512], F32)
            expT = sb(ph, "expT", [128, TW], F32)
            vis = sb(ph, "vis", [128, TW], F32)

            ld(vis[:], vis_d.ap(), "vis")
            for i in range(2):
                P.op("pool", lambda e, i=i: e.memset(kT[i][64:96, :], 1.0), writes=["kT%d" % i])
            for i_, (vsrc, vsr) in enumerate(((va_d, "d_va"), (vc_d, "d_vc"))):
                v3 = vsrc.ap().rearrange("(b p) c -> p b c", p=128)
                for q_ in range(4):
                    P.op("sp", lambda e, i_=i_, v3=v3, q_=q_: e.dma_start(out=vx[i_][:, q_ * 16:(q_ + 1) * 16, :], in_=v3[:, q_ * 16:(q_ + 1) * 16, :]),
                         reads=[vsr], multi=["vx%d" % i_], chan="vx%d" % i_)

            def finish_group(acc_bank, h_row0, g, gi):
                bcb = 6 + gi % 2
                o = gi % 2
                P.op("dve", lambda e: e.reciprocal(rl[0:1, :], ps[acc_bank][0:1, :]),
                     reads=["ps%d" % acc_bank], writes=["rl"])
                P.op("pe", lambda e: e.matmul(ps[bcb][0:96, :], lhsT=ones32[0:1, 0:96], rhs=rl[0:1, :], start=True, stop=True),
                     reads=["rl", "ones32"], writes=["ps%d" % bcb])
                P.op("dve", lambda e: e.tensor_copy(otmp[:], ps[acc_bank][0:96, :]),
                     reads=["ps%d" % acc_bank], writes=["otmp"])
                P.op("dve", lambda e: e.tensor_tensor(osb[o][:], otmp[:], ps[bcb][0:96, :], ALU.mult),
                     reads=["otmp", "ps%d" % bcb], writes=["osb%d" % o])
                P.op("pool", lambda e: e.dma_start(out=osend.ap()[h_row0:h_row0 + 64, g * 512:(g + 1) * 512], in_=osb[o][1:65, :]),
                     reads=["osb%d" % o], multi=["d_osend"], chan="osb%d" % o)

            def do_head(typ, h, tile_i, gi):
                if True:
                    hs = (typ * 4 + h) % 2
                    ksrc, qsrc = (kaT, qaT) if typ == 0 else (kcT, qcT)
                    kres, qres = "kT%d" % hs, "qT%d" % hs
                    P.op("sp", lambda e, hs=hs, ksrc=ksrc, h=h: e.dma_start(out=kT[hs][0:64, :], in_=ksrc.ap()[h * 64:(h + 1) * 64, :]),
                         reads=["d_" + ksrc.name], multi=[kres], chan=kres + "a")
                    P.op("sp", lambda e, hs=hs, qsrc=qsrc, h=h: e.dma_start(out=qT[hs][0:64, :], in_=qsrc.ap()[h * 64:(h + 1) * 64, :]),
                         reads=["d_" + qsrc.name], multi=[qres], chan=qres + "a")
                    if typ == 0:
                        P.op("sp", lambda e, hs=hs, h=h: e.dma_start(out=qT[hs][64:67, :], in_=gref_d.ap()[h]),
                             reads=["d_gref"], multi=[qres], chan=qres + "b")
                    else:
                        P.op("sp", lambda e, h=h: e.dma_start(out=expT[:], in_=ttab_d.ap()[l, h]), writes=["expT"], chan="expT")
                        P.op("act", lambda e: e.activation(out=expT[:], in_=expT[:], func=AF.Exp), writes=["expT"])
                        P.op("dve", lambda e: e.tensor_tensor(expT[:], expT[:], vis[:], ALU.mult), reads=["vis"], writes=["expT"])
                    KR = 67 if typ == 0 else 64
                    vxt = vx[typ]
                    vres = "vx%d" % typ
                    tiles = []
                    for g in range(16):
                        if typ == 0:
                            blks = list(range(4 * g + 4))
                            for j in blks:
                                d = j - 4 * g
                                c_lo = 0 if d < 0 else d * 128
                                tiles.append((g, j, c_lo, 512, j == 0, j == blks[-1], d))
                        else:
                            rs = [r for r in (0, -1, -2, -3, -4, 1, 2, 3) if 0 <= 4 * g + r <= 63]
                            for ri, r in enumerate(rs):
                                c_lo = 128 * max(0, r)
                                c_hi = 128 * (min(3, r + 4) + 1)
                                tiles.append((g, 4 * g + r, c_lo, c_hi, ri == 0, ri == len(rs) - 1, r))
                    nt = len(tiles)
                    LOOK = 3

                    def emit_S(ti):
                        g, j, c_lo, c_hi, first, last, d = tiles[ti]
                        sbank = (tile_i + ti) % 4
                        P.op("pe", lambda e: e.matmul(
                            ps[sbank][:, c_lo:c_hi], lhsT=kT[hs][0:KR, j * 128:(j + 1) * 128],
                            rhs=qT[hs][0:KR, g * 512 + c_lo:g * 512 + c_hi], start=True, stop=True),
                            reads=[kres, qres], writes=["ps%d" % sbank])

                    def emit_rest(ti):
                        g, j, c_lo, c_hi, first, last, d = tiles[ti]
                        sbank = (tile_i + ti) % 4
                        pslot = (tile_i + ti) % 4
                        acc = 4 + (gi + g) % 2
                        w = c_hi - c_lo
                        if typ == 0:
                            P.op("act", lambda e: e.activation(
                                out=pT[pslot][:, c_lo:c_hi], in_=ps[sbank][:, c_lo:c_hi], func=AF.Exp,
                                bias=G_sb[:, h, j:j + 1], scale=1.0),
                                reads=["ps%d" % sbank, "G_sb"], writes=["pT%d" % pslot])
                            if d >= 0:
                                P.op("dve", lambda e: e.tensor_tensor(
                                    pT[pslot][:, c_lo:c_lo + 128], pT[pslot][:, c_lo:c_lo + 128], mask01[:], ALU.mult),
                                    reads=["mask01", "pT%d" % pslot], writes=["pT%d" % pslot], osz=128)
                        else:
                            fs = (tile_i + ti) % 2
                            P.op("act", lambda e: e.activation(
                                out=pf[fs][:, c_lo:c_hi], in_=ps[sbank][:, c_lo:c_hi], func=AF.Exp),
                                reads=["ps%d" % sbank], writes=["pf%d" % fs])
                            t0c = c_lo - 128 * d + 384
                            P.op("dve", lambda e: e.tensor_tensor(
                                pT[pslot][:, c_lo:c_hi], pf[fs][:, c_lo:c_hi], expT[:, t0c:t0c + w], ALU.mult),
                                reads=["pf%d" % fs, "expT"], writes=["pT%d" % pslot])
                        P.op("pe", lambda e: e.matmul(
                            ps[acc][0:65, c_lo:c_hi], lhsT=vxt[:, j, h * 65:(h + 1) * 65], rhs=pT[pslot][:, c_lo:c_hi],
                            start=first, stop=last), reads=[vres, "pT%d" % pslot],
                            **({"writes": ["ps%d" % acc]} if first else {"multi": ["ps%d" % acc]}))
                        if last and os.environ.get('K_P2_FIN', '1') == '1':
                            finish_group(acc, typ * 256 + h * 64, g, gi + g)

                    for ti in range(min(LOOK, nt)):
                        emit_S(ti)
                    for ti in range(nt):
                        if ti + LOOK < nt:
                            emit_S(ti + LOOK)
                        emit_rest(ti)
                    return nt

            gi = 0
            tile_i = 0
            for typ in range(2):
                for h in range(4):
                    if typ * 4 + h >= int(os.environ.get('K_P2_HEADS', '8')):
                        continue
                    tile_i += do_head(typ, h, tile_i, gi)
                    gi += 16
            if os.environ.get('K_P2_CC', '1') == '1':
              for j_ in range(4):
                P.op("pool", lambda e, j_=j_: e.collective_compute(
                    "AllGather", ALU.bypass, replica_groups=PAIRS,
                    ins=[osend.ap()[j_ * 128:(j_ + 1) * 128, :]], outs=[og.ap()[j_ * 256:(j_ + 1) * 256, :]]),
                    reads=["d_osend"], multi=["d_og"], chan="cc_og", inc=1)
            P.emit_phase()
        if stop_after == ("p2", l):
            break

        with contextlib.ExitStack() as ph:
            stage = [sb(ph, "wstage%d" % i, [128, 2048], F32) for i in range(2)]
            wg = sb(ph, "wg", [128, 8, 2048], BF16)
            wbr = sb(ph, "wbr", [128, 8, D], BF16)
            wout = sb(ph, "wout", [128, 8, D], BF16)
            xin = [sb(ph, "xin%d" % i, [128, 4, D], F32) for i in range(2)]
            xn = [sb(ph, "xn%d" % i, [128, 4, D], F32) for i in range(1)]
            hT = [sb(ph, "hT%d" % i, [128, 8, 512], BF16) for i in range(1)]
            oT = [sb(ph, "oT%d" % i, [128, 8, 512], BF16) for i in range(2)]
            mT = sb(ph, "mT", [128, 8, 512], BF16)
            sa = [sb(ph, "sa%d" % i, [128, 512], F32) for i in range(2)]
            sc = [sb(ph, "sc%d" % i, [128, 512], F32) for i in range(2)]
            z = [sb(ph, "z%d" % i, [128, D], F32) for i in range(2)]
            gb1 = sb(ph, "gb1", [128, D], F32)
            lng = sb(ph, "lng", [128, D], F32)
            lnb = sb(ph, "lnb", [128, D], F32)
            st = sb(ph, "st", [128, 4, 3], F32)
            mv = sb(ph, "mv", [128, 4], F32)

            load_weight_bf16(stage, wg, "wg", wg_d.ap()[l].rearrange("(k p) n -> p k n", p=128), 8, 2048)
            load_weight_bf16(stage, wbr, "wbr", wbr_d.ap()[l].rearrange("(k p) n -> p k n", p=128), 8, D)
            load_weight_bf16(stage, wout, "wout", wout_d.ap()[l].rearrange("(k p) n -> p k n", p=128), 8, D)
            ld(gb1[:], gbc_d.ap()[l, 0], "gb1", reads=["d_gbc"])
            ld(lng[:], lnp_d.ap()[l, 0], "lnp")
            P.op("sp", lambda e: e.dma_start(out=lnb[:], in_=lnp_d.ap()[l, 1]), multi=["lnp"], chan="lnpb")

            FOXK = (0, 1, 2, 3)
            CHK = (4, 5, 6, 7)
            groups = [("halo", 2)] + [("main", gq) for gq in range(8)]
            def p3_group(gidx_, kind, gq):
                s = gidx_ % 2
                if kind == "halo":
                    ntok = 2
                    if l == 0:
                        xsrc = xmine.ap()[0:2, :]
                        xres = []
                    else:
                        xsrc = xg.ap()[14 * 512 + 510:14 * 512 + 512, :]
                        xres = ["d_xg"]
                    dst_row = 0
                else:
                    ntok = 512
                    if l == 0:
                        xsrc = xmine.ap()[2 + gq * 512: 2 + (gq + 1) * 512, :]
                        xres = []
                    else:
                        xsrc = xs.ap()[gq * 512:(gq + 1) * 512, :]
                        xres = ["d_xs"]
                    dst_row = 2 + gq * 512
                ntt = (ntok + 127) // 128
                if kind == "halo":
                    ld(xin[s][0:2, 0, :], xsrc, "xin%d" % s, reads=xres)
                else:
                    ld(xin[s][:], xsrc.rearrange("(t p) d -> p t d", p=128), "xin%d" % s, reads=xres)

                def og_load(e, s=s, kind=kind, gq=gq, ntok=ntok):
                    og3 = og.ap().rearrange("(k p) t -> p k t", p=128)
                    if kind == "halo":
                        src = og3[:, :, bass.ds(P.c1, 2)]
                    else:
                        src = og3[:, :, bass.ds(P.c0, HALF)][:, :, gq * 512:(gq + 1) * 512]
                    return e.dma_start(out=oT[s][:, :, 0:ntok], in_=src)
                P.op("pool", og_load, reads=["d_og"], writes=["oT%d" % s], chan="oT%d" % s)
                ln_to_hT(xin[s], "xin%d" % s, ntok, xn[0], "xn0", hT[0], "hT0", st, mv, l, 8, 0, [0], "p3")
                for fc in range(8):
                    b0 = 1 + (fc % 2) * 3
                    bga, bgc, bba = b0, b0 + 1, b0 + 2
                    bbc = 7
                    for k in range(8):
                        P.op("pe", lambda e, k=k, fc=fc, bga=bga: e.matmul(
                            ps[bga][:, 0:ntok], lhsT=wg[:, k, fc * 128:(fc + 1) * 128], rhs=hT[0][:, k, 0:ntok],
                            start=(k == 0), stop=(k == 7)), reads=["wg", "hT0"],
                            **({"writes": ["ps%d" % bga]} if k == 0 else {"multi": ["ps%d" % bga]}))
                    for k in range(8):
                        P.op("pe", lambda e, k=k, fc=fc, bgc=bgc: e.matmul(
                            ps[bgc][:, 0:ntok], lhsT=wg[:, k, D + fc * 128:D + (fc + 1) * 128], rhs=hT[0][:, k, 0:ntok],
                            start=(k == 0), stop=(k == 7)), reads=["wg", "hT0"],
                            **({"writes": ["ps%d" % bgc]} if k == 0 else {"multi": ["ps%d" % bgc]}))
                    for ki, kk in enumerate(FOXK):
                        P.op("pe", lambda e, kk=kk, ki=ki, fc=fc, bba=bba: e.matmul(
                            ps[bba][:, 0:ntok], lhsT=wbr[:, kk, fc * 128:(fc + 1) * 128], rhs=oT[s][:, kk, 0:ntok],
                            start=(ki == 0), stop=(ki == 3)), reads=["wbr", "oT%d" % s],
                            **({"writes": ["ps%d" % bba]} if ki == 0 else {"multi": ["ps%d" % bba]}))
                    for ki, kk in enumerate(CHK):
                        P.op("pe", lambda e, kk=kk, ki=ki, fc=fc: e.matmul(
                            ps[bbc][:, 0:ntok], lhsT=wbr[:, kk, fc * 128:(fc + 1) * 128], rhs=oT[s][:, kk, 0:ntok],
                            start=(ki == 0), stop=(ki == 3)), reads=["wbr", "oT%d" % s],
                            **({"writes": ["ps7"]} if ki == 0 else {"multi": ["ps7"]}))
                    q = fc % 2
                    P.op("act", lambda e, q=q, bga=bga: e.activation(out=sa[q][:, 0:ntok], in_=ps[bga][:, 0:ntok], func=AF.Sigmoid),
                         reads=["ps%d" % bga], writes=["sa%d" % q], osz=ntok)
                    P.op("act", lambda e, q=q, bgc=bgc: e.activation(out=sc[q][:, 0:ntok], in_=ps[bgc][:, 0:ntok], func=AF.Sigmoid),
                         reads=["ps%d" % bgc], writes=["sc%d" % q], osz=ntok)
                    P.op("dve", lambda e, q=q, bba=bba: e.tensor_tensor(sa[q][:, 0:ntok], sa[q][:, 0:ntok], ps[bba][:, 0:ntok], ALU.mult),
                         reads=["ps%d" % bba, "sa%d" % q], writes=["sa%d" % q], osz=ntok)
                    P.op("dve", lambda e, q=q: e.tensor_tensor(sc[q][:, 0:ntok], sc[q][:, 0:ntok], ps[bbc][:, 0:ntok], ALU.mult),
                         reads=["ps7", "sc%d" % q], writes=["sc%d" % q], osz=ntok)
                    P.op("dve", lambda e, q=q, fc=fc: e.tensor_tensor(mT[:, fc, 0:ntok], sa[q][:, 0:ntok], sc[q][:, 0:ntok], ALU.add),
                         reads=["sa%d" % q, "sc%d" % q], multi=["mT"], osz=ntok)
                for tt in range(ntt):
                    n = min(128, ntok - tt * 128)
                    zs = tt % 2
                    zt = z[zs]
                    for hf in range(2):
                        bank = 1 + (tt * 2 + hf) % 6
                        for k in range(8):
                            P.op("pe", lambda e, k=k, tt=tt, n=n, hf=hf, bank=bank: e.matmul(
                                ps[bank][0:n, :], lhsT=mT[:, k, tt * 128:tt * 128 + n], rhs=wout[:, k, hf * 512:(hf + 1) * 512],
                                start=(k == 0), stop=(k == 7)), reads=["wout", "mT"],
                                **({"writes": ["ps%d" % bank]} if k == 0 else {"multi": ["ps%d" % bank]}))
                        P.op("dve", lambda e, n=n, hf=hf, bank=bank, zt=zt: e.tensor_tensor(
                            zt[0:n, hf * 512:(hf + 1) * 512], ps[bank][0:n, :], gb1[0:n, hf * 512:(hf + 1) * 512], ALU.mult),
                            reads=["ps%d" % bank, "gb1"], multi=["z%d" % zs])
                    P.op("dve", lambda e, n=n, tt=tt, zt=zt, s=s: e.scalar_tensor_tensor(
                        zt[0:n, :], xin[s][0:n, tt, :], ALPHA, zt[0:n, :], ALU.mult, ALU.add),
                        reads=["xin%d" % s], multi=["z%d" % zs])
                    final_ln(zt[0:n, :], n, st, mv, lng, lnb, "z%d" % zs, "p3z")
                    r_ = dst_row + tt * 128
                    P.op("pool", lambda e, zt=zt, n=n, r_=r_: e.dma_start(out=x1buf.ap()[r_:r_ + n, :], in_=zt[0:n, :]),
                         reads=["z%d" % zs], multi=["d_x1"], chan="z%d" % zs)

            for gidx_, (kind, gq) in enumerate(groups):
                p3_group(gidx_, kind, gq)
            P.emit_phase()
        if stop_after == ("p3", l):
            break

        with contextlib.ExitStack() as ph:
            GT = G4
            NTT = (GT + 127) // 128
            stage = [sb(ph, "wstage%d" % i, [128, 512], F32) for i in range(2)]
            wup = sb(ph, "wup", [128, 8, 2 * DFF], BF16)
            wdn = sb(ph, "wdn", [128, 22, D], BF16)
            xin = [sb(ph, "xin%d" % i, [128, NTT, D], F32) for i in range(2)]
            xn = [sb(ph, "xn%d" % i, [128, NTT, D], F32) for i in range(1)]
            hT = [sb(ph, "hT%d" % i, [128, 8, GT], BF16) for i in range(1)]
            gT = sb(ph, "gT", [128, 22, GT], BF16)
            t0 = [sb(ph, "t0%d" % i, [128, GT], F32) for i in range(4)]
            sact = [sb(ph, "sact%d" % i, [128, GT], BF16) for i in range(2)]
            z = [sb(ph, "z%d" % i, [128, D], F32) for i in range(2)]
            gb2 = sb(ph, "gb2", [128, D], F32)
            lng = sb(ph, "lng", [128, D], F32)
            lnb = sb(ph, "lnb", [128, D], F32)
            carry = sb(ph, "carry", [128, NFC, 2], F32)
            cw = sb(ph, "cw", [128, NFC, 3], F32)
            cb = sb(ph, "cb", [128, NFC], F32)
            st = sb(ph, "st", [128, 4, 3], F32)
            mv = sb(ph, "mv", [128, 4], F32)

            load_weight_bf16(stage, wup, "wup", wup_d.ap()[l].rearrange("(k p) n -> p k n", p=128), 8, 2 * DFF)
            load_weight_bf16(stage, wdn, "wdn", wdn_d.ap()[l].rearrange("(k p) n -> p k n", p=128), 22, D)
            ld(gb2[:], gbc_d.ap()[l, 1], "gb2", reads=["d_gbc"])
            ld(lng[:], lnp_d.ap()[l, 2], "lnp")
            P.op("sp", lambda e: e.dma_start(out=lnb[:], in_=lnp_d.ap()[l, 3]), multi=["lnp"], chan="lnpb")
            ld(cw[:], cw_d.ap()[l], "cw")
            ld(cb[:], cb_d.ap()[l], "cb")

            groups = [(0, 2)]
            r_ = 2
            while r_ < HALF + 2:
                n_ = min(GT, HALF + 2 - r_)
                groups.append((r_, n_))
                r_ += n_
            def p4_group(gidx_, row0, ntok):
                halo = gidx_ == 0
                s = gidx_ % 2
                ntt = (ntok + 127) // 128
                if halo:
                    ld(xin[s][0:2, 0, :], x1buf.ap()[0:2, :], "xin%d" % s, reads=["d_x1"])
                else:
                    nfull = ntok // 128
                    ld(xin[s][:, 0:nfull, :], x1buf.ap()[row0:row0 + nfull * 128, :].rearrange("(t p) d -> p t d", p=128),
                       "xin%d" % s, reads=["d_x1"])
                    if ntok % 128:
                        rem = ntok % 128
                        P.op("sp", lambda e, s=s, nfull=nfull, rem=rem, row0=row0: e.dma_start(
                            out=xin[s][0:rem, nfull, :], in_=x1buf.ap()[row0 + nfull * 128: row0 + nfull * 128 + rem, :]),
                            reads=["d_x1"], multi=["xin%d" % s], chan="xin%db" % s)
                ln_to_hT(xin[s], "xin%d" % s, ntok, xn[0], "xn0", hT[0], "hT0", st, mv, l, 32, 24, [0], "p4")
                for j in range(22):
                    for half_i, f in enumerate((j, 22 + j)):
                        bank = 1 + (j * 2 + half_i) % 4
                        pres = "ps%d" % bank
                        for k in range(8):
                            P.op("pe", lambda e, k=k, f=f, bank=bank: e.matmul(
                                ps[bank][:, 0:ntok], lhsT=wup[:, k, f * 128:(f + 1) * 128], rhs=hT[0][:, k, 0:ntok],
                                start=(k == 0), stop=(k == 7)), reads=["wup", "hT0"],
                                **({"writes": [pres]} if k == 0 else {"multi": [pres]}))
                        u = ps[bank]
                        if halo:
                            P.op("dve", lambda e, f=f, u=u: e.tensor_scalar(carry[:, f, :], u[:, 0:2], hv[:, 0:1], None, ALU.mult),
                                 reads=[pres, "hv"], multi=["carry"], osz=2)
                            continue
                        ts = (j * 2 + half_i) % 4
                        tt_ = t0[ts]
                        tres = "t0%d" % ts
                        n = ntok
                        P.op("act", lambda e, f=f, u=u, tt_=tt_, n=n: e.activation(
                            out=tt_[:, 0:n], in_=u[:, 0:n], func=AF.Identity, scale=cw[:, f, 2:3], bias=cb[:, f:f + 1]),
                            reads=[pres, "cw", "cb"], writes=[tres], osz=n)
                        P.op("dve", lambda e, f=f, u=u, tt_=tt_, n=n: e.scalar_tensor_tensor(
                            tt_[:, 1:n], u[:, 0:n - 1], cw[:, f, 1:2], tt_[:, 1:n], ALU.mult, ALU.add),
                            reads=[pres, "cw", tres], writes=[tres], osz=n)
                        P.op("dve", lambda e, f=f, tt_=tt_: e.scalar_tensor_tensor(
                            tt_[:, 0:1], carry[:, f, 1:2], cw[:, f, 1:2], tt_[:, 0:1], ALU.mult, ALU.add),
                            reads=["carry", tres], writes=[tres], osz=1)
                        P.op("dve", lambda e, f=f, u=u, tt_=tt_, n=n: e.scalar_tensor_tensor(
                            tt_[:, 2:n], u[:, 0:n - 2], cw[:, f, 0:1], tt_[:, 2:n], ALU.mult, ALU.add),
                            reads=[pres, tres], writes=[tres], osz=n)
                        P.op("dve", lambda e, f=f, tt_=tt_: e.scalar_tensor_tensor(
                            tt_[:, 0:2], carry[:, f, 0:2], cw[:, f, 0:1], tt_[:, 0:2], ALU.mult, ALU.add),
                            reads=["carry", tres], writes=[tres], osz=2)
                        P.op("dve", lambda e, f=f, u=u, n=n: e.tensor_copy(carry[:, f, :], u[:, n - 2:n]),
                             reads=[pres], multi=["carry"], osz=2)
                        if half_i == 0:
                            q = j % 2
                            P.op("act", lambda e, q=q, tt_=tt_, n=n: e.activation(out=sact[q][:, 0:n], in_=tt_[:, 0:n], func=AF.Silu),
                                 reads=[tres], writes=["sact%d" % q], osz=n)
                        else:
                            q = j % 2
                            P.op("pool", lambda e, q=q, tt_=tt_, n=n, j=j: e.tensor_tensor(
                                gT[:, j, 0:n], sact[q][:, 0:n], tt_[:, 0:n], ALU.mult),
                                reads=[tres, "sact%d" % q], multi=["gT"], osz=n)
                if halo:
                    return
                for tt in range(ntt):
                    n = min(128, ntok - tt * 128)
                    zs = tt % 2
                    zt = z[zs]
                    for hf in range(2):
                        bank = 5 + (tt * 2 + hf) % 3
                        for j in range(22):
                            P.op("pe", lambda e, j=j, tt=tt, n=n, hf=hf, bank=bank: e.matmul(
                                ps[bank][0:n, :], lhsT=gT[:, j, tt * 128:tt * 128 + n], rhs=wdn[:, j, hf * 512:(hf + 1) * 512],
                                start=(j == 0), stop=(j == 21)), reads=["wdn", "gT"],
                                **({"writes": ["ps%d" % bank]} if j == 0 else {"multi": ["ps%d" % bank]}))
                        P.op("dve", lambda e, n=n, hf=hf, bank=bank, zt=zt: e.tensor_tensor(
                            zt[0:n, hf * 512:(hf + 1) * 512], ps[bank][0:n, :], gb2[0:n, hf * 512:(hf + 1) * 512], ALU.mult),
                            reads=["ps%d" % bank, "gb2"], multi=["z%d" % zs])
                    P.op("dve", lambda e, n=n, tt=tt, zt=zt, s=s: e.scalar_tensor_tensor(
                        zt[0:n, :], xin[s][0:n, tt, :], ALPHA, zt[0:n, :], ALU.mult, ALU.add),
                        reads=["xin%d" % s], multi=["z%d" % zs])
                    final_ln(zt[0:n, :], n, st, mv, lng, lnb, "z%d" % zs, "p4z")
                    rr_ = row0 - 2 + tt * 128
                    if l == n_layers - 1:
                        P.op("pool", lambda e, zt=zt, n=n, rr_=rr_: e.dma_start(out=out_d.ap()[rr_:rr_ + n, :], in_=zt[0:n, :]),
                             reads=["z%d" % zs], multi=["d_out"], chan="z%d" % zs)
                    else:
                        P.op("pool", lambda e, zt=zt, n=n, rr_=rr_: e.dma_start(out=xs.ap()[rr_:rr_ + n, :], in_=zt[0:n, :]),
                             reads=["z%d" % zs], multi=["d_xs"], chan="z%d" % zs)
            for gidx_, (row0, ntok) in enumerate(groups):
                p4_group(gidx_, row0, ntok)
            if l < n_layers - 1:
                for j_ in range(8):
                    P.op("pool", lambda e, j_=j_: e.collective_compute(
                        "AllGather", ALU.bypass, replica_groups=PAIRS,
                        ins=[xs.ap()[j_ * 512:(j_ + 1) * 512, :]], outs=[xg.ap()[j_ * 1024:(j_ + 1) * 1024, :]]),
                        reads=["d_xs"], multi=["d_xg"], chan="cc_xg", inc=1)
            P.emit_phase(final_wait=True)

    if debug:
        with contextlib.ExitStack() as ph:
            for name in debug:
                src = {"qaT": qaT, "kaT": kaT, "qcT": qcT, "kcT": kcT, "va": va_d, "vc": vc_d, "gref": gref_d,
                       "osend": osend, "og": og, "x1buf": x1buf, "xs": xs, "xg": xg, "gbc": gbc_d}[name]
                o = nc.dram_tensor("dbg_" + name, list(src.shape), src.dtype, kind="ExternalOutput")
                dbg[name] = o
                P.op("sp", lambda e, o=o, src=src: e.dma_start(out=o.ap(), in_=src.ap()),
                     reads=["d_" + name, "d_" + src.name], multi=["dbgout"], chan="dbg_" + name)
            if "G_sb" in debug or True:
                o = nc.dram_tensor("dbg_G", [128, 256], F32, kind="ExternalOutput")
                P.op("sp", lambda e, o=o: e.dma_start(out=o.ap(), in_=G_sb[:].rearrange("p h b -> p (h b)")),
                     reads=["G_sb"], multi=["dbgout"], chan="dbg_G")
                o2 = nc.dram_tensor("dbg_modc", [128, NL * 48], F32, kind="ExternalOutput")
                P.op("sp", lambda e, o2=o2: e.dma_start(out=o2.ap(), in_=modc[:].rearrange("p l c -> p (l c)")),
                     reads=["modc"], multi=["dbgout"], chan="dbg_modc")
            P.emit_phase(final_wait=True)
    es.close()
    _NC_CACHE["trace"] = P.trace
    return nc


def _consts():
    bf = ml_dtypes.bfloat16
    p = np.arange(128)
    ident = np.eye(128, dtype=np.float32)
    tri = (p[:, None] <= p[None, :]).astype(np.float32)
    mask01 = tri.astype(bf)
    q = np.arange(64)
    tri64s = (q[:, None] < q[None, :]).astype(np.float32)
    tri64i = (q[:, None] <= q[None, :]).astype(np.float32)
    E = (np.arange(S)[None, :] // 128 == q[:, None]).astype(np.float32).astype(bf)
    m = np.arange(TW) - 384
    qc = np.floor_divide(m, 64)[None, :]
    kc = (p // 64)[:, None]
    vis = ((qc - kc >= 0) & (qc - kc <= 8)).astype(np.float32)
    relidx = np.clip(m[None, :] - p[:, None], -128, 128) + 128
    return dict(ident=ident, mask01=mask01, tri=tri, tri64s=tri64s, tri64i=tri64i, Emat=E, vis=vis), relidx


def prep_inputs(x, c, w_in, b_f, rel_bias, w_br_fox, w_br_chunk, w_out, w_up, conv_w, conv_b, w_down,
                w_ada, b_ada, ln1_g, ln1_b, ln2_g, ln2_b):
    f32 = np.float32
    A = lambda a: np.ascontiguousarray(np.asarray(a), dtype=f32)
    x, c, w_in, b_f, rel_bias = A(x), A(c), A(w_in), A(b_f), A(rel_bias)
    w_br_fox, w_br_chunk, w_out, w_up = A(w_br_fox), A(w_br_chunk), A(w_out), A(w_up)
    conv_w, conv_b, w_down, w_ada, b_ada = A(conv_w), A(conv_b), A(w_down), A(w_ada), A(b_ada)
    ln1_g, ln1_b, ln2_g, ln2_b = A(ln1_g), A(ln1_b), A(ln2_g), A(ln2_b)
    consts, relidx = _consts()
    shared = dict(consts)
    shared["wg"] = np.ascontiguousarray(w_in[:, :, 3080:5128])
    shared["wout"] = w_out
    shared["wup"] = w_up
    shared["wdn"] = w_down
    shared["wada"] = w_ada
    shared["cw"] = np.ascontiguousarray(conv_w.reshape(NL, 3, NFC, 128).transpose(0, 3, 2, 1))
    shared["cb"] = np.ascontiguousarray(conv_b.reshape(NL, NFC, 128).transpose(0, 2, 1))
    shared["badac"] = np.ascontiguousarray(b_ada.reshape(NL, 48, 128).transpose(0, 2, 1))
    shared["badar"] = np.ascontiguousarray(np.stack([b_ada[:, 2 * D:3 * D], b_ada[:, 5 * D:6 * D]], axis=1))
    lnp = np.stack([ln1_g, ln1_b, ln2_g, ln2_b], axis=1)
    shared["lnp"] = np.ascontiguousarray(np.broadcast_to(lnp[:, :, None, :], (NL, 4, 128, D)))
    brorder = [("f", 0), ("f", 2), ("f", 1), ("f", 3), ("c", 0), ("c", 2), ("c", 1), ("c", 3)]
    shared["wbr"] = np.ascontiguousarray(np.concatenate(
        [(w_br_fox if t == "f" else w_br_chunk)[:, i * 128:(i + 1) * 128, :] for t, i in brorder], axis=1))
    maps = []
    for core in range(8):
        b, r = core // 2, core % 2
        m = dict(shared)
        m["xfull"] = x[b]
        xm = np.zeros((HALF + 2, D), f32)
        xm[2:] = x[b, r * HALF:(r + 1) * HALF]
        if r == 1:
            xm[0:2] = x[b, HALF - 2:HALF]
        m["xmine"] = xm
        m["hv"] = np.full((128, 1), float(r), f32)
        m["ccol"] = np.ascontiguousarray(c[b].reshape(8, 128).T)
        cs = lambda base: slice(base + 256 * r, base + 256 * r + 256)
        m["wqk"] = np.ascontiguousarray(np.concatenate(
            [w_in[:, :, cs(0)], w_in[:, :, cs(512)], w_in[:, :, cs(1544)], w_in[:, :, cs(2056)]], axis=2))
        m["wv"] = np.ascontiguousarray(np.concatenate([w_in[:, :, cs(1024)], w_in[:, :, cs(2568)]], axis=2))
        m["wf"] = np.ascontiguousarray(w_in[:, :, 1536 + 4 * r:1536 + 4 * r + 4])
        m["bfbc"] = np.ascontiguousarray(np.broadcast_to(b_f[:, None, 4 * r:4 * r + 4], (NL, 128, 4)))
        m["ttab"] = np.ascontiguousarray(rel_bias[:, 4 * r:4 * r + 4][:, :, relidx])
        maps.append(m)
    return maps


_NC_CACHE = {}


def kernel(**inputs):
    maps = prep_inputs(**inputs)
    if "nc" not in _NC_CACHE:
        _NC_CACHE["nc"] = build_program()
    nc = _NC_CACHE["nc"]
    res = run_bass_kernel_spmd(nc, maps, core_ids=list(range(8)))
    out = np.empty((4, S, D), np.float32)
    for core in range(8):
        b, r = core // 2, core % 2
        out[b, r * HALF:(r + 1) * HALF] = np.asarray(res.results[core]["out"], dtype=np.float32)
    return out
```

```python
import contextlib
import os
import numpy as np
import ml_dtypes
import concourse.bass as bass
import concourse.mybir as mybir
from concourse.bass_utils import run_bass_kernel_spmd

F32 = mybir.dt.float32
BF16 = mybir.dt.bfloat16
ALU = mybir.AluOpType
AF = mybir.ActivationFunctionType

D = 1024
S = 8192
HALF = 4096
DFF = 2816
NFC = 44
EPS = 1e-5
ALPHA = float(4.0 ** 0.25)
NL = 2
PAIRS = [[0, 1], [2, 3], [4, 5], [6, 7]]
TW = 1408
G4 = 256


class Prog:
    ENGS = ("pe", "act", "dve", "pool", "sp")

    def __init__(self, nc, es):
        self.nc = nc
        self.es = es
        self.engobj = {"pe": nc.tensor, "act": nc.scalar, "dve": nc.vector, "pool": nc.gpsimd, "sp": nc.sync}
        self.ops = {e: [] for e in self.ENGS}
        self.esem = {e: es.enter_context(nc.semaphore("sem_" + e)) for e in ("pe", "act", "dve", "pool")}
        self.ecount = {e: 0 for e in self.esem}
        self.chans = {}
        self.res = {}
        self.waited = {e: {} for e in self.ENGS}
        self.gidx = {e: 0 for e in self.ENGS}
        self.phase_base = {e: 0 for e in self.ENGS}
        self.sigval_at_phase_start = {e: 0 for e in self.esem}
        self.pid = None
        self.cumdur = {e: 0 for e in self.ENGS}
        self.trace = {e: [] for e in self.ENGS}

    def chan(self, key):
        if key not in self.chans:
            self.chans[key] = [self.es.enter_context(self.nc.semaphore("c%d" % len(self.chans))), 0]
        return self.chans[key]

    def _r(self, name):
        if name not in self.res:
            self.res[name] = {"w": [], "r": [], "pr": []}
        return self.res[name]

    def op(self, eng, fn, reads=(), writes=(), multi=(), chan=None, inc=16, osz=512, dur=None):
        deps = []
        for n in reads:
            r = self._r(n)
            for d in r["w"]:
                if d["eng"] == eng and d["chan"] is None:
                    if eng == "pe" or d["osz"] >= 256 or (self.cumdur[eng] - d["cum"]) >= 256:
                        continue
                    d = dict(d, raw=True, orig=d)
                deps.append(d)
        for n in writes:
            r = self._r(n)
            deps += r["w"] + r["r"] + r["pr"]
        for n in multi:
            r = self._r(n)
            deps += r["r"] + r["pr"]
        if dur is None:
            dur = osz
        self.cumdur[eng] += dur
        rec = {"fn": fn, "eng": eng, "idx": self.gidx[eng], "chan": None, "deps": [], "sig": False,
               "osz": osz, "cum": self.cumdur[eng]}
        self.gidx[eng] += 1
        need = {}
        for d in deps:
            if d["chan"] is not None:
                key = ("c", d["chan"][0])
                val = self.chans[d["chan"][0]][1]
                need[key] = max(need.get(key, 0), val)
            else:
                if d["eng"] == eng:
                    if not d.get("raw"):
                        continue
                    d = d["orig"]
                key = ("e", d["eng"])
                if key not in need or need[key]["idx"] < d["idx"]:
                    need[key] = d
        rec["need"] = need
        if chan is not None:
            c = self.chan(chan)
            c[1] += inc
            rec["chan"] = (chan, c[1], inc)
        self.ops[eng].append(rec)
        me = rec
        for n in reads:
            self._r(n)["r"].append(me)
        for n in writes:
            r = self._r(n)
            r["w"] = [me]
            r["r"] = []
            r["pr"] = []
        for n in multi:
            r = self._r(n)
            if r["r"]:
                r["pr"] = r["r"]
                r["r"] = []
                r["w"] = [me]
            else:
                r["w"].append(me)
        return rec

    def emit_phase(self, final_wait=True):
        for e in self.ENGS:
            for rec in self.ops[e]:
                for key, d in rec["need"].items():
                    if key[0] == "e":
                        if d["idx"] >= self.phase_base[d["eng"]]:
                            d["sig"] = True
        for e in self.esem:
            for rec in reversed(self.ops[e]):
                if rec["chan"] is None:
                    rec["sig"] = True
                    break
        sigval = {}
        for e in self.esem:
            cnt = self.ecount[e]
            for rec in self.ops[e]:
                if rec["chan"] is not None:
                    rec["sig"] = False
                if rec["sig"]:
                    cnt += 1
                    rec["sigval"] = cnt
            nxt = cnt
            for rec in reversed(self.ops[e]):
                if rec["sig"]:
                    nxt = rec["sigval"]
                rec["cover"] = nxt
            sigval[e] = cnt
        prog = self

        def run(e):
            eng_ops = prog.ops[e]

            def body(eng):
                w = prog.waited[e]
                if e == "pool" and prog.pid is None:
                    prog.pid = eng.partition_id()
                    rank = prog.pid % 2
                    prog.c0 = eng.snap(rank * HALF)
                    prog.c1 = eng.snap(rank * (HALF - 2))
                for rec in eng_ops:
                    for key, d in rec["need"].items():
                        if key[0] == "c":
                            sem = prog.chans[key[1]][0]
                            val = d
                        else:
                            sem = prog.esem[key[1]]
                            if d["idx"] >= prog.phase_base[key[1]]:
                                val = d["cover"]
                            else:
                                val = prog.sigval_at_phase_start[key[1]]
                        if w.get(key, 0) >= val:
                            continue
                        w[key] = val
                        eng.wait_ge(sem, val)
                        prog.trace[e].append(("wait", key, val))
                    ins = rec["fn"](eng)
                    if rec["chan"] is not None:
                        ins.then_inc(prog.chans[rec["chan"][0]][0], rec["chan"][2])
                        prog.trace[e].append(("inc", ("c", rec["chan"][0]), rec["chan"][2]))
                    elif rec["sig"]:
                        ins.then_inc(prog.esem[e], 1)
                        prog.trace[e].append(("inc", ("e", e), 1))
                if final_wait and e in ("sp", "pool"):
                    for key, (sem, cnt) in prog.chans.items():
                        if cnt > 0 and w.get(("c", key), 0) < cnt:
                            eng.wait_ge(sem, cnt)
                            prog.trace[e].append(("wait", ("c", key), cnt))
                prog.trace[e].append(("barrier", None, 0))
            return body

        with self.nc.Block() as block:
            block.tensor(run("pe"))
            block.scalar(run("act"))
            block.vector(run("dve"))
            block.gpsimd(run("pool"))
            block.sync(run("sp"))
        for e in self.esem:
            self.ecount[e] = sigval[e]
            self.sigval_at_phase_start[e] = sigval[e]
        for e in self.ENGS:
            self.phase_base[e] = self.gidx[e]
            self.ops[e] = []


_NC_CACHE = {}


def build_program(debug=None, n_layers=NL, stop_after=None):
    nc = bass.Bass("TRN2", target_bir_lowering=False)
    es = contextlib.ExitStack()
    P = Prog(nc, es)

    def din(name, shape, dt=F32):
        return nc.dram_tensor(name, list(shape), dt, kind="ExternalInput")

    xfull = din("xfull", [S, D])
    xmine = din("xmine", [HALF + 2, D])
    hv_d = din("hv", [128, 1])
    ccol_d = din("ccol", [128, 8])
    wqk_d = din("wqk", [NL, D, 1024])
    wv_d = din("wv", [NL, D, 512])
    wf_d = din("wf", [NL, D, 4])
    bfbc_d = din("bfbc", [NL, 128, 4])
    wg_d = din("wg", [NL, D, 2048])
    ttab_d = din("ttab", [NL, 4, 128, TW])
    vis_d = din("vis", [128, TW])
    wbr_d = din("wbr", [NL, D, D])
    wout_d = din("wout", [NL, D, D])
    wup_d = din("wup", [NL, D, 2 * DFF])
    cw_d = din("cw", [NL, 128, NFC, 3])
    cb_d = din("cb", [NL, 128, NFC])
    wdn_d = din("wdn", [NL, DFF, D])
    wada_d = din("wada", [NL, D, 6 * D])
    badac_d = din("badac", [NL, 128, 48])
    badar_d = din("badar", [NL, 2, D])
    lnp_d = din("lnp", [NL, 4, 128, D])
    ident_d = din("ident", [128, 128], F32)
    mask_d = din("mask01", [128, 128], BF16)
    tri_d = din("tri", [128, 128])
    tri64s_d = din("tri64s", [64, 64])
    tri64i_d = din("tri64i", [64, 64])
    E_d = din("Emat", [64, S], BF16)
    out_d = nc.dram_tensor("out", [HALF, D], F32, kind="ExternalOutput")

    qaT = nc.dram_tensor("qaT", [256, S], BF16)
    kaT = nc.dram_tensor("kaT", [256, S], BF16)
    qcT = nc.dram_tensor("qcT", [256, S], BF16)
    kcT = nc.dram_tensor("kcT", [256, S], BF16)
    va_d = nc.dram_tensor("va", [S, 260], BF16)
    vc_d = nc.dram_tensor("vc", [S, 260], BF16)
    gref_d = nc.dram_tensor("gref", [4, 3, S], BF16)
    osend = nc.dram_tensor("osend", [512, S], BF16)
    og = nc.dram_tensor("og", [1024, S], BF16)
    x1buf = nc.dram_tensor("x1buf", [HALF + 2, D], F32)
    xs = nc.dram_tensor("xs", [HALF, D], F32)
    xg = nc.dram_tensor("xg", [S, D], F32)
    gbc_d = nc.dram_tensor("gbc", [NL, 2, 128, D], F32)
    dbg = {}

    uniq = [0]

    def sb(stack, name, shape, dt):
        uniq[0] += 1
        return stack.enter_context(nc.sbuf_tensor("s%d_%s" % (uniq[0], name), list(shape), dt))

    ps = [es.enter_context(nc.psum_tensor("ps%d" % i, [128, 512], F32)) for i in range(8)]
    psb = [p[:].bitcast(BF16) for p in ps]
    ident = sb(es, "ident", [128, 128], F32)
    mask01 = sb(es, "mask01", [128, 128], BF16)
    tri = sb(es, "tri", [128, 128], F32)
    tri64s = sb(es, "tri64s", [64, 64], F32)
    tri64i = sb(es, "tri64i", [64, 64], F32)
    ones32 = sb(es, "ones32", [128, 128], F32)
    hv = sb(es, "hv", [128, 1], F32)
    modc = sb(es, "modc", [128, NL, 48], F32)
    G_sb = sb(es, "G_sb", [128, 4, 64], F32)

    def ld(dst_ap, src_ap, res, eng="sp", reads=()):
        P.op(eng, lambda e: e.dma_start(out=dst_ap, in_=src_ap), reads=reads, writes=[res], chan=res)

    ld(ident[:], ident_d.ap(), "ident")
    ld(mask01[:], mask_d.ap(), "mask01")
    ld(tri[:], tri_d.ap(), "tri")
    ld(tri64s[:], tri64s_d.ap(), "tri64s")
    ld(tri64i[:], tri64i_d.ap(), "tri64i")
    ld(hv[:], hv_d.ap(), "hv")
    P.op("dve", lambda e: e.memset(ones32[:], 1.0), writes=["ones32"])

    rr = [0]

    def cast_eng():
        rr[0] += 1
        return ("dve", "act")[rr[0] % 2]

    def cast_op(eng, dst, src, reads, writes=(), multi=()):
        if eng == "act":
            P.op("act", lambda e: e.activation(out=dst, in_=src, func=AF.Copy), reads=reads, writes=writes, multi=multi)
        else:
            P.op(eng, lambda e: e.tensor_copy(dst, src), reads=reads, writes=writes, multi=multi)

    def load_weight_bf16(stage, dst, dst_res, src3, K, N):
        piece = stage[0].shape[1]
        i = 0
        for k in range(K):
            for n0 in range(0, N, piece):
                n1 = min(N, n0 + piece)
                s = i % 2
                i += 1
                sres = "wstage%d" % s
                st = stage[s]
                ld(st[:, 0:n1 - n0], src3[:, k, n0:n1], sres)
                cast_op(cast_eng(), dst[:, k, n0:n1], st[:, 0:n1 - n0], reads=[sres], multi=[dst_res])

    with contextlib.ExitStack() as ph:
        ccol = sb(ph, "ccol", [128, 8], F32)
        condT = sb(ph, "condT", [128, 8], F32)
        crep = sb(ph, "crep", [128, 8, 128], F32)
        wst = [sb(ph, "wadast%d" % i, [128, 8, 1024], F32) for i in range(2)]
        bcol = sb(ph, "bcol", [128, 48], F32)
        brow = sb(ph, "brow", [1, 2, D], F32)
        gtile = sb(ph, "gtile", [128, D], F32)
        ld(ccol[:], ccol_d.ap(), "ccol")
        P.op("act", lambda e: e.activation(out=condT[:], in_=ccol[:], func=AF.Silu), reads=["ccol"], writes=["condT"], osz=8)
        for k in range(8):
            P.op("dve", lambda e, k=k: e.tensor_scalar(crep[:, k, :], ones32[:], condT[:, k:k + 1], None, ALU.mult),
                 reads=["ones32", "condT"], multi=["crep"], osz=128)
        for l in range(n_layers):
            ld(bcol[:], badac_d.ap()[l], "bcol")
            ld(brow[:], badar_d.ap()[l:l + 1], "brow")
            w3 = wada_d.ap()[l].rearrange("(k p) n -> p k n", p=128)
            for pc in range(6):
                s = pc % 2
                ld(wst[s][:], w3[:, :, pc * 1024:(pc + 1) * 1024], "wadast%d" % s)
                for j in range(8):
                    col = pc * 8 + j
                    for k in range(8):
                        P.op("pe", lambda e, s=s, j=j, k=k, col=col: e.matmul(
                            ps[0][:, col:col + 1], lhsT=wst[s][:, k, j * 128:(j + 1) * 128], rhs=condT[:, k:k + 1],
                            start=(k == 0), stop=(k == 7)),
                            reads=["wadast%d" % s, "condT"], multi=["ps0"])
                if pc in (2, 5):
                    gi = 0 if pc == 2 else 1
                    for hf in range(2):
                        bank = 1 + hf
                        for k in range(8):
                            P.op("pe", lambda e, s=s, k=k, hf=hf, bank=bank: e.matmul(
                                ps[bank][:, :], lhsT=crep[:, k, :], rhs=wst[s][:, k, hf * 512:(hf + 1) * 512],
                                start=(k == 0), stop=False),
                                reads=["wadast%d" % s, "crep"], multi=["ps%d" % bank])
                        P.op("pe", lambda e, gi=gi, hf=hf, bank=bank: e.matmul(
                            ps[bank][:, :], lhsT=ones32[0:1, :], rhs=brow[0:1, gi, hf * 512:(hf + 1) * 512],
                            start=False, stop=True), reads=["ones32", "brow"], multi=["ps%d" % bank])
                        P.op("dve", lambda e, hf=hf, bank=bank: e.tensor_scalar(
                            gtile[:, hf * 512:(hf + 1) * 512], ps[bank][:, :], 1.0, None, ALU.add),
                            reads=["ps%d" % bank], multi=["gtile"])
                    P.op("pool", lambda e, l=l, gi=gi: e.dma_start(out=gbc_d.ap()[l, gi], in_=gtile[:]),
                         reads=["gtile"], multi=["d_gbc"], chan="gtile")
            P.op("dve", lambda e, l=l: e.tensor_tensor(modc[:, l, :], ps[0][:, 0:48], bcol[:], ALU.add),
                 reads=["ps0", "bcol"], multi=["modc"], osz=48)
            for c0 in (8, 32):
                P.op("dve", lambda e, l=l, c0=c0: e.tensor_scalar(
                    modc[:, l, c0:c0 + 8], modc[:, l, c0:c0 + 8], 1.0, None, ALU.add), reads=["modc"], multi=["modc"], osz=8)
        P.emit_phase()

    def ln_stats(xap, n, st, mv, res_in, tag):
        for hf in range(2):
            P.op("dve", lambda e, hf=hf: e.bn_stats(st[0:n, 2 * hf:2 * hf + 2, :], xap[:, hf * 512:(hf + 1) * 512]),
                 reads=[res_in], multi=[tag + "st"], osz=6, dur=512)
        P.op("dve", lambda e: e.bn_aggr(mv[0:n, 0:2], st[0:n, :, :]), reads=[tag + "st"], multi=[tag + "mv"], osz=2)
        P.op("act", lambda e: e.activation(out=mv[0:n, 2:3], in_=mv[0:n, 1:2], func=AF.Ln, bias=EPS),
             reads=[tag + "mv"], multi=[tag + "mv"], osz=1)
        P.op("act", lambda e: e.activation(out=mv[0:n, 2:3], in_=mv[0:n, 2:3], func=AF.Exp, scale=-0.5),
             reads=[tag + "mv"], multi=[tag + "mv"], osz=1)
        P.op("dve", lambda e: e.tensor_scalar(mv[0:n, 3:4], mv[0:n, 0:1], mv[0:n, 2:3], -1.0, ALU.mult, ALU.mult),
             reads=[tag + "mv"], multi=[tag + "mv"], osz=1)

    def ln_to_hT(xin_tile, xin_res, ntok, xn, xn_res, hT, hT_res, st, mv, l, sc_col, sh_col, tbanks, tag):
        ntt = (ntok + 127) // 128
        for tt in range(ntt):
            n = min(128, ntok - tt * 128)
            ln_stats(xin_tile[0:n, tt, :], n, st, mv, xin_res, tag)
            P.op("act", lambda e, tt=tt, n=n: e.activation(out=xn[0:n, tt, :], in_=xin_tile[0:n, tt, :], func=AF.Identity,
                                                           scale=mv[0:n, 2:3], bias=mv[0:n, 3:4]),
                 reads=[xin_res, tag + "mv"], multi=[xn_res])
        for k in range(8):
            bank = tbanks[k % len(tbanks)]
            off = 0
            pres = "ps%d" % bank
            for tt in range(ntt):
                n = min(128, ntok - tt * 128)
                P.op("pe", lambda e, tt=tt, n=n, k=k, bank=bank, off=off: e.transpose(
                    ps[bank][:, off + tt * 128: off + tt * 128 + n], xn[0:n, tt, k * 128:(k + 1) * 128], ident[0:n, 0:n]),
                    reads=[xn_res, "ident"], multi=[pres])
            P.op("act", lambda e, k=k, bank=bank, off=off: e.activation(
                out=hT[:, k, 0:ntok], in_=ps[bank][:, off:off + ntok], func=AF.Identity,
                scale=modc[:, l, sc_col + k:sc_col + k + 1], bias=modc[:, l, sh_col + k:sh_col + k + 1]),
                reads=[pres, "modc"], multi=[hT_res], osz=ntok)

    def final_ln(z, n, st, mv, lng, lnb, zres, tag):
        ln_stats(z, n, st, mv, zres, tag)
        P.op("act", lambda e: e.activation(out=z, in_=z, func=AF.Identity, scale=mv[0:n, 2:3], bias=mv[0:n, 3:4]),
             reads=[tag + "mv", zres], writes=[zres])
        P.op("dve", lambda e: e.tensor_tensor(z, z, lng[0:n, :], ALU.mult), reads=["lnp", zres], writes=[zres])
        P.op("dve", lambda e: e.tensor_tensor(z, z, lnb[0:n, :], ALU.add), reads=["lnp", zres], writes=[zres])

    for l in range(n_layers if stop_after != ("p0", 0) else 0):
        xsrc_full = xfull.ap() if l == 0 else xg.ap()
        xsrc_full_res = "d_xfull" if l == 0 else "d_xg"

        with contextlib.ExitStack() as ph:
            stage = [sb(ph, "wstage%d" % i, [128, 2048], F32) for i in range(2)]
            wqk = sb(ph, "wqk", [128, 8, 1024], BF16)
            wv = sb(ph, "wv", [128, 8, 512], BF16)
            wf = sb(ph, "wf", [128, 8, 4], BF16)
            bfbc = sb(ph, "bfbc", [128, 4], F32)
            xin = [sb(ph, "xin%d" % i, [128, 4, D], F32) for i in range(2)]
            xn = [sb(ph, "xn%d" % i, [128, 4, D], F32) for i in range(2)]
            hT = [sb(ph, "hT%d" % i, [128, 8, 512], BF16) for i in range(2)]
            qko = [sb(ph, "qko%d" % i, [128, 512], BF16) for i in range(4)]
            vout = [sb(ph, "vout%d" % i, [128, 8, 65], BF16) for i in range(2)]
            fsb = sb(ph, "fsb", [128, 4, 64], F32)
            st = sb(ph, "st", [128, 4, 3], F32)
            mv = sb(ph, "mv", [128, 4], F32)
            Emat = sb(ph, "Emat", [64, S], BF16)
            tot = sb(ph, "tot", [64, 1], F32)
            totrep = sb(ph, "totrep", [64, 128], F32)
            r0 = sb(ph, "r0", [64, 4], F32)
            vals = sb(ph, "vals", [64, 3], BF16)
            v32 = sb(ph, "v32", [64, 1], F32)
            grow = [sb(ph, "grow%d" % i, [3, 512], BF16) for i in range(2)]

            load_weight_bf16(stage, wqk, "wqk", wqk_d.ap()[l].rearrange("(k p) n -> p k n", p=128), 8, 1024)
            load_weight_bf16(stage, wv, "wv", wv_d.ap()[l].rearrange("(k p) n -> p k n", p=128), 8, 512)
            load_weight_bf16(stage, wf, "wf", wf_d.ap()[l].rearrange("(k p) n -> p k n", p=128), 8, 4)
            ld(bfbc[:], bfbc_d.ap()[l], "bfbc")
            ld(Emat[:], E_d.ap(), "Emat")
            for i in range(2):
                P.op("pool", lambda e, i=i: e.memset(vout[i][:], 1.0), writes=["vout%d" % i])

            dsts = [qaT, qaT, kaT, kaT, qcT, qcT, kcT, kcT]
            for g in range(int(os.environ.get('K_P1_GROUPS', '16'))):
                s = g % 2
                tok0 = g * 512
                srow = tok0 if l == 0 else ((g % 8) * 2 + g // 8) * 512
                ld(xin[s][:], xsrc_full[srow:srow + 512, :].rearrange("(t p) d -> p t d", p=128), "xin%d" % s,
                   reads=[xsrc_full_res])
                if int(os.environ.get('K_P1_STEPS', '9')) >= 2:
                    ln_to_hT(xin[s], "xin%d" % s, 512, xn[s], "xn%d" % s, hT[s], "hT%d" % s, st, mv, l, 8, 0, [0, 1], "p1")
                STEPS = int(os.environ.get('K_P1_STEPS', '9'))
                for m in range(8 if STEPS >= 3 else 0):
                    bank = 2 + m % 3
                    for k in range(8):
                        P.op("pe", lambda e, m=m, k=k, bank=bank, s=s: e.matmul(
                            ps[bank][:, :], lhsT=wqk[:, k, m * 128:(m + 1) * 128], rhs=hT[s][:, k, :],
                            start=(k == 0), stop=(k == 7)), reads=["wqk", "hT%d" % s], multi=["ps%d" % bank])
                    qs = (g * 8 + m) % 4
                    scale = 0.125 if m in (0, 1, 4, 5) else 1.0
                    if m % 2 == 0:
                        P.op("act", lambda e, bank=bank, qs=qs, scale=scale: e.activation(
                            out=qko[qs][:], in_=ps[bank][:, :], func=AF.Copy, scale=scale),
                            reads=["ps%d" % bank], writes=["qko%d" % qs])
                    else:
                        P.op("dve", lambda e, bank=bank, qs=qs, scale=scale: e.tensor_scalar(
                            qko[qs][:], ps[bank][:, :], scale, None, ALU.mult),
                            reads=["ps%d" % bank], writes=["qko%d" % qs])
                    dst = dsts[m]
                    r0_ = (m % 2) * 128
                    P.op("pool", lambda e, dst=dst, r0_=r0_, qs=qs, tok0=tok0: e.dma_start(
                        out=dst.ap()[r0_:r0_ + 128, tok0:tok0 + 512], in_=qko[qs][:]),
                        reads=["qko%d" % qs], multi=["d_" + dst.name], chan="qko%d" % qs)
                for tt in range(4 if STEPS >= 4 else 0):
                    bank = 6 + tt % 2
                    for k in range(8):
                        P.op("pe", lambda e, tt=tt, k=k, bank=bank, s=s: e.matmul(
                            ps[bank][:, :], lhsT=hT[s][:, k, tt * 128:(tt + 1) * 128], rhs=wv[:, k, :],
                            start=(k == 0), stop=(k == 7)), reads=["wv", "hT%d" % s], multi=["ps%d" % bank])
                    vs = (g * 4 + tt) % 2
                    P.op("dve", lambda e, bank=bank, vs=vs: e.tensor_copy(
                        vout[vs][:, :, 1:65], ps[bank][:, :].rearrange("p (h c) -> p h c", c=64)),
                        reads=["ps%d" % bank], multi=["vout%d" % vs])
                    t0_ = tok0 + tt * 128
                    P.op("pool", lambda e, vs=vs, t0_=t0_: e.dma_start(
                        out=va_d.ap()[t0_:t0_ + 128, :], in_=vout[vs][:, 0:4, :].rearrange("p h c -> p (h c)")),
                        reads=["vout%d" % vs], multi=["d_va"], chan="vout%d" % vs)
                    P.op("pool", lambda e, vs=vs, t0_=t0_: e.dma_start(
                        out=vc_d.ap()[t0_:t0_ + 128, :], in_=vout[vs][:, 4:8, :].rearrange("p h c -> p (h c)")),
                        reads=["vout%d" % vs], multi=["d_vc"], chan="vout%d" % vs)
                for tt in range(4 if STEPS >= 5 else 0):
                    for k in range(8):
                        P.op("pe", lambda e, tt=tt, k=k, s=s: e.matmul(
                            ps[5][:, tt * 4: tt * 4 + 4], lhsT=hT[s][:, k, tt * 128:(tt + 1) * 128],
                            rhs=wf[:, k, :], start=(k == 0), stop=(k == 7)),
                            reads=["wf", "hT%d" % s], multi=["ps5"])
                for tt in range(4 if STEPS >= 5 else 0):
                    blk = g * 4 + tt
                    P.op("dve", lambda e, tt=tt, blk=blk: e.tensor_tensor(
                        fsb[:, :, blk], ps[5][:, tt * 4: tt * 4 + 4], bfbc[:], ALU.add),
                        reads=["ps5", "bfbc"], multi=["fsb"], osz=4)
            P.op("act", lambda e: e.activation(out=fsb[:], in_=fsb[:], func=AF.Exp, scale=-1.0), reads=["fsb"], writes=["fsb"], osz=255)
            P.op("act", lambda e: e.activation(out=fsb[:], in_=fsb[:], func=AF.Ln, bias=1.0), reads=["fsb"], writes=["fsb"], osz=255)
            for h in range(int(os.environ.get('K_P1_TAIL', '4'))):
                P.op("pe", lambda e, h=h: e.matmul(ps[2][0:64, 0:1], lhsT=fsb[:, h, :], rhs=ones32[:, 0:1], start=True, stop=True),
                     reads=["fsb", "ones32"], writes=["ps2"])
                P.op("dve", lambda e: e.tensor_copy(tot[:], ps[2][0:64, 0:1]), reads=["ps2"], writes=["tot"], osz=1)
                P.op("dve", lambda e: e.tensor_scalar(totrep[:], ones32[0:64, :], tot[:, 0:1], None, ALU.mult),
                     reads=["tot", "ones32"], writes=["totrep"], osz=128)
                P.op("pe", lambda e, h=h: e.matmul(ps[3][:, 0:64], lhsT=tri[:], rhs=fsb[:, h, :], start=True, stop=False),
                     reads=["fsb", "tri"], writes=["ps3"])
                P.op("pe", lambda e: e.matmul(ps[3][:, 0:64], lhsT=totrep[:], rhs=tri64s[:], start=False, stop=True),
                     reads=["totrep", "tri64s"], multi=["ps3"])
                P.op("dve", lambda e, h=h: e.tensor_copy(G_sb[:, h, :], ps[3][:, 0:64]), reads=["ps3"], multi=["G_sb"], osz=64)
                P.op("pe", lambda e: e.matmul(ps[4][0:64, 0:1], lhsT=tri64i[:], rhs=tot[:], start=True, stop=True),
                     reads=["tot", "tri64i"], writes=["ps4"])
                P.op("dve", lambda e: e.tensor_scalar(r0[:, 0:1], ps[4][0:64, 0:1], -1.0, None, ALU.mult),
                     reads=["ps4"], writes=["r0"], osz=1)
                for i in range(3):
                    P.op("dve", lambda e, i=i: e.tensor_copy(vals[:, i:i + 1], r0[:, i:i + 1]), reads=["r0"], multi=["vals"], osz=1)
                    if i < 2:
                        P.op("dve", lambda e, i=i: e.tensor_copy(v32[:], vals[:, i:i + 1]), reads=["vals"], writes=["v32"], osz=1)
                        P.op("dve", lambda e, i=i: e.tensor_tensor(r0[:, i + 1:i + 2], r0[:, i:i + 1], v32[:], ALU.subtract),
                             reads=["v32", "r0"], multi=["r0"], osz=1)
                for j in range(16):
                    bank = 5 + j % 2
                    gs = j % 2
                    P.op("pe", lambda e, j=j, bank=bank: e.matmul(
                        ps[bank][0:3, :], lhsT=vals[:, :], rhs=Emat[:, j * 512:(j + 1) * 512], start=True, stop=True),
                        reads=["vals", "Emat"], writes=["ps%d" % bank])
                    P.op("act", lambda e, bank=bank, gs=gs: e.activation(out=grow[gs][:], in_=ps[bank][0:3, :], func=AF.Copy),
                         reads=["ps%d" % bank], writes=["grow%d" % gs])
                    P.op("pool", lambda e, h=h, j=j, gs=gs: e.dma_start(
                        out=gref_d.ap()[h, :, j * 512:(j + 1) * 512], in_=grow[gs][:]),
                        reads=["grow%d" % gs], multi=["d_gref"], chan="grow%d" % gs)
            P.emit_phase()
        if stop_after == ("p1", l):
            break

        with contextlib.ExitStack() as ph:
            kT = [sb(ph, "kT%d" % i, [128, S], BF16) for i in range(2)]
            qT = [sb(ph, "qT%d" % i, [128, S], BF16) for i in range(2)]
            vx = [sb(ph, "vx%d" % i, [128, 64, 260], BF16) for i in range(2)]
            pT = [sb(ph, "pT%d" % i, [128, 512], BF16) for i in range(4)]
            pf = [sb(ph, "pf%d" % i, [128, 512], F32) for i in range(2)]
            osb = [sb(ph, "osb%d" % i, [96, 512], BF16) for i in range(2)]
            otmp = sb(ph, "otmp", [96, 512], F32)
            rl = sb(ph, "rl", [128, 512], F32)
            expT = sb(ph, "expT", [128, TW], F32)
            vis = sb(ph, "vis", [128, TW], F32)

            ld(vis[:], vis_d.ap(), "vis")
            for i in range(2):
                P.op("pool", lambda e, i=i: e.memset(kT[i][64:96, :], 1.0), writes=["kT%d" % i])
            for i_, (vsrc, vsr) in enumerate(((va_d, "d_va"), (vc_d, "d_vc"))):
                v3 = vsrc.ap().rearrange("(b p) c -> p b c", p=128)
                for q_ in range(4):
                    P.op("sp", lambda e, i_=i_, v3=v3, q_=q_: e.dma_start(out=vx[i_][:, q_ * 16:(q_ + 1) * 16, :], in_=v3[:, q_ * 16:(q_ + 1) * 16, :]),
                         reads=[vsr], multi=["vx%d" % i_], chan="vx%d" % i_)

            def finish_group(acc_bank, h_row0, g, gi):
                bcb = 6 + gi % 2
                o = gi % 2
                P.op("dve", lambda e: e.reciprocal(rl[0:1, :], ps[acc_bank][0:1, :]),
                     reads=["ps%d" % acc_bank], writes=["rl"])
                P.op("pe", lambda e: e.matmul(ps[bcb][0:96, :], lhsT=ones32[0:1, 0:96], rhs=rl[0:1, :], start=True, stop=True),
                     reads=["rl", "ones32"], writes=["ps%d" % bcb])
                P.op("dve", lambda e: e.tensor_copy(otmp[:], ps[acc_bank][0:96, :]),
                     reads=["ps%d" % acc_bank], writes=["otmp"])
                P.op("dve", lambda e: e.tensor_tensor(osb[o][:], otmp[:], ps[bcb][0:96, :], ALU.mult),
                     reads=["otmp", "ps%d" % bcb], writes=["osb%d" % o])
                P.op("pool", lambda e: e.dma_start(out=osend.ap()[h_row0:h_row0 + 64, g * 512:(g + 1) * 512], in_=osb[o][1:65, :]),
                     reads=["osb%d" % o], multi=["d_osend"], chan="osb%d" % o)

            def do_head(typ, h, tile_i, gi):
                if True:
                    hs = (typ * 4 + h) % 2
                    ksrc, qsrc = (kaT, qaT) if typ == 0 else (kcT, qcT)
                    kres, qres = "kT%d" % hs, "qT%d" % hs
                    P.op("sp", lambda e, hs=hs, ksrc=ksrc, h=h: e.dma_start(out=kT[hs][0:64, :], in_=ksrc.ap()[h * 64:(h + 1) * 64, :]),
                         reads=["d_" + ksrc.name], multi=[kres], chan=kres + "a")
                    P.op("sp", lambda e, hs=hs, qsrc=qsrc, h=h: e.dma_start(out=qT[hs][0:64, :], in_=qsrc.ap()[h * 64:(h + 1) * 64, :]),
                         reads=["d_" + qsrc.name], multi=[qres], chan=qres + "a")
                    if typ == 0:
                        P.op("sp", lambda e, hs=hs, h=h: e.dma_start(out=qT[hs][64:67, :], in_=gref_d.ap()[h]),
                             reads=["d_gref"], multi=[qres], chan=qres + "b")
                    else:
                        P.op("sp", lambda e, h=h: e.dma_start(out=expT[:], in_=ttab_d.ap()[l, h]), writes=["expT"], chan="expT")
                        P.op("act", lambda e: e.activation(out=expT[:], in_=expT[:], func=AF.Exp), writes=["expT"])
                        P.op("dve", lambda e: e.tensor_tensor(expT[:], expT[:], vis[:], ALU.mult), reads=["vis"], writes=["expT"])
                    KR = 67 if typ == 0 else 64
                    vxt = vx[typ]
                    vres = "vx%d" % typ
                    tiles = []
                    for g in range(16):
                        if typ == 0:
                            blks = list(range(4 * g + 4))
                            for j in blks:
                                d = j - 4 * g
                                c_lo = 0 if d < 0 else d * 128
                                tiles.append((g, j, c_lo, 512, j == 0, j == blks[-1], d))
                        else:
                            rs = [r for r in (0, -1, -2, -3, -4, 1, 2, 3) if 0 <= 4 * g + r <= 63]
                            for ri, r in enumerate(rs):
                                c_lo = 128 * max(0, r)
                                c_hi = 128 * (min(3, r + 4) + 1)
                                tiles.append((g, 4 * g + r, c_lo, c_hi, ri == 0, ri == len(rs) - 1, r))
                    nt = len(tiles)
                    LOOK = 3

                    def emit_S(ti):
                        g, j, c_lo, c_hi, first, last, d = tiles[ti]
                        sbank = (tile_i + ti) % 4
                        P.op("pe", lambda e: e.matmul(
                            ps[sbank][:, c_lo:c_hi], lhsT=kT[hs][0:KR, j * 128:(j + 1) * 128],
                            rhs=qT[hs][0:KR, g * 512 + c_lo:g * 512 + c_hi], start=True, stop=True),
                            reads=[kres, qres], writes=["ps%d" % sbank])

                    def emit_rest(ti):
                        g, j, c_lo, c_hi, first, last, d = tiles[ti]
                        sbank = (tile_i + ti) % 4
                        pslot = (tile_i + ti) % 4
                        acc = 4 + (gi + g) % 2
                        w = c_hi - c_lo
                        if typ == 0:
                            P.op("act", lambda e: e.activation(
                                out=pT[pslot][:, c_lo:c_hi], in_=ps[sbank][:, c_lo:c_hi], func=AF.Exp,
                                bias=G_sb[:, h, j:j + 1], scale=1.0),
                                reads=["ps%d" % sbank, "G_sb"], writes=["pT%d" % pslot])
                            if d >= 0:
                                P.op("dve", lambda e: e.tensor_tensor(
                                    pT[pslot][:, c_lo:c_lo + 128], pT[pslot][:, c_lo:c_lo + 128], mask01[:], ALU.mult),
                                    reads=["mask01", "pT%d" % pslot], writes=["pT%d" % pslot], osz=128)
                        else:
                            fs = (tile_i + ti) % 2
                            P.op("act", lambda e: e.activation(
                                out=pf[fs][:, c_lo:c_hi], in_=ps[sbank][:, c_lo:c_hi], func=AF.Exp),
                                reads=["ps%d" % sbank], writes=["pf%d" % fs])
                            t0c = c_lo - 128 * d + 384
                            P.op("dve", lambda e: e.tensor_tensor(
                                pT[pslot][:, c_lo:c_hi], pf[fs][:, c_lo:c_hi], expT[:, t0c:t0c + w], ALU.mult),
                                reads=["pf%d" % fs, "expT"], writes=["pT%d" % pslot])
                        P.op("pe", lambda e: e.matmul(
                            ps[acc][0:65, c_lo:c_hi], lhsT=vxt[:, j, h * 65:(h + 1) * 65], rhs=pT[pslot][:, c_lo:c_hi],
                            start=first, stop=last), reads=[vres, "pT%d" % pslot],
                            **({"writes": ["ps%d" % acc]} if first else {"multi": ["ps%d" % acc]}))
                        if last and os.environ.get('K_P2_FIN', '1') == '1':
                            finish_group(acc, typ * 256 + h * 64, g, gi + g)

                    for ti in range(min(LOOK, nt)):
                        emit_S(ti)
                    for ti in range(nt):
                        if ti + LOOK < nt:
                            emit_S(ti + LOOK)
                        emit_rest(ti)
                    return nt

            gi = 0
            tile_i = 0
            for typ in range(2):
                for h in range(4):
                    if typ * 4 + h >= int(os.environ.get('K_P2_HEADS', '8')):
                        continue
                    tile_i += do_head(typ, h, tile_i, gi)
                    gi += 16
            if os.environ.get('K_P2_CC', '1') == '1':
              for j_ in range(4):
                P.op("pool", lambda e, j_=j_: e.collective_compute(
                    "AllGather", ALU.bypass, replica_groups=PAIRS,
                    ins=[osend.ap()[j_ * 128:(j_ + 1) * 128, :]], outs=[og.ap()[j_ * 256:(j_ + 1) * 256, :]]),
                    reads=["d_osend"], multi=["d_og"], chan="cc_og", inc=1)
            P.emit_phase()
        if stop_after == ("p2", l):
            break

        with contextlib.ExitStack() as ph:
            stage = [sb(ph, "wstage%d" % i, [128, 2048], F32) for i in range(2)]
            wg = sb(ph, "wg", [128, 8, 2048], BF16)
            wbr = sb(ph, "wbr", [128, 8, D], BF16)
            wout = sb(ph, "wout", [128, 8, D], BF16)
            xin = [sb(ph, "xin%d" % i, [128, 4, D], F32) for i in range(2)]
            xn = [sb(ph, "xn%d" % i, [128, 4, D], F32) for i in range(1)]
            hT = [sb(ph, "hT%d" % i, [128, 8, 512], BF16) for i in range(1)]
            oT = [sb(ph, "oT%d" % i, [128, 8, 512], BF16) for i in range(2)]
            mT = sb(ph, "mT", [128, 8, 512], BF16)
            sa = [sb(ph, "sa%d" % i, [128, 512], F32) for i in range(2)]
            sc = [sb(ph, "sc%d" % i, [128, 512], F32) for i in range(2)]
            z = [sb(ph, "z%d" % i, [128, D], F32) for i in range(2)]
            gb1 = sb(ph, "gb1", [128, D], F32)
            lng = sb(ph, "lng", [128, D], F32)
            lnb = sb(ph, "lnb", [128, D], F32)
            st = sb(ph, "st", [128, 4, 3], F32)
            mv = sb(ph, "mv", [128, 4], F32)

            load_weight_bf16(stage, wg, "wg", wg_d.ap()[l].rearrange("(k p) n -> p k n", p=128), 8, 2048)
            load_weight_bf16(stage, wbr, "wbr", wbr_d.ap()[l].rearrange("(k p) n -> p k n", p=128), 8, D)
            load_weight_bf16(stage, wout, "wout", wout_d.ap()[l].rearrange("(k p) n -> p k n", p=128), 8, D)
            ld(gb1[:], gbc_d.ap()[l, 0], "gb1", reads=["d_gbc"])
            ld(lng[:], lnp_d.ap()[l, 0], "lnp")
            P.op("sp", lambda e: e.dma_start(out=lnb[:], in_=lnp_d.ap()[l, 1]), multi=["lnp"], chan="lnpb")

            FOXK = (0, 1, 2, 3)
            CHK = (4, 5, 6, 7)
            groups = [("halo", 2)] + [("main", gq) for gq in range(8)]
            def p3_group(gidx_, kind, gq):
                s = gidx_ % 2
                if kind == "halo":
                    ntok = 2
                    if l == 0:
                        xsrc = xmine.ap()[0:2, :]
                        xres = []
                    else:
                        xsrc = xg.ap()[14 * 512 + 510:14 * 512 + 512, :]
                        xres = ["d_xg"]
                    dst_row = 0
                else:
                    ntok = 512
                    if l == 0:
                        xsrc = xmine.ap()[2 + gq * 512: 2 + (gq + 1) * 512, :]
                        xres = []
                    else:
                        xsrc = xs.ap()[gq * 512:(gq + 1) * 512, :]
                        xres = ["d_xs"]
                    dst_row = 2 + gq * 512
                ntt = (ntok + 127) // 128
                if kind == "halo":
                    ld(xin[s][0:2, 0, :], xsrc, "xin%d" % s, reads=xres)
                else:
                    ld(xin[s][:], xsrc.rearrange("(t p) d -> p t d", p=128), "xin%d" % s, reads=xres)

                def og_load(e, s=s, kind=kind, gq=gq, ntok=ntok):
                    og3 = og.ap().rearrange("(k p) t -> p k t", p=128)
                    if kind == "halo":
                        src = og3[:, :, bass.ds(P.c1, 2)]
                    else:
                        src = og3[:, :, bass.ds(P.c0, HALF)][:, :, gq * 512:(gq + 1) * 512]
                    return e.dma_start(out=oT[s][:, :, 0:ntok], in_=src)
                P.op("pool", og_load, reads=["d_og"], writes=["oT%d" % s], chan="oT%d" % s)
                ln_to_hT(xin[s], "xin%d" % s, ntok, xn[0], "xn0", hT[0], "hT0", st, mv, l, 8, 0, [0], "p3")
                for fc in range(8):
                    b0 = 1 + (fc % 2) * 3
                    bga, bgc, bba = b0, b0 + 1, b0 + 2
                    bbc = 7
                    for k in range(8):
                        P.op("pe", lambda e, k=k, fc=fc, bga=bga: e.matmul(
                            ps[bga][:, 0:ntok], lhsT=wg[:, k, fc * 128:(fc + 1) * 128], rhs=hT[0][:, k, 0:ntok],
                            start=(k == 0), stop=(k == 7)), reads=["wg", "hT0"],
                            **({"writes": ["ps%d" % bga]} if k == 0 else {"multi": ["ps%d" % bga]}))
                    for k in range(8):
                        P.op("pe", lambda e, k=k, fc=fc, bgc=bgc: e.matmul(
                            ps[bgc][:, 0:ntok], lhsT=wg[:, k, D + fc * 128:D + (fc + 1) * 128], rhs=hT[0][:, k, 0:ntok],
                            start=(k == 0), stop=(k == 7)), reads=["wg", "hT0"],
                            **({"writes": ["ps%d" % bgc]} if k == 0 else {"multi": ["ps%d" % bgc]}))
                    for ki, kk in enumerate(FOXK):
                        P.op("pe", lambda e, kk=kk, ki=ki, fc=fc, bba=bba: e.matmul(
                            ps[bba][:, 0:ntok], lhsT=wbr[:, kk, fc * 128:(fc + 1) * 128], rhs=oT[s][:, kk, 0:ntok],
                            start=(ki == 0), stop=(ki == 3)), reads=["wbr", "oT%d" % s],
                            **({"writes": ["ps%d" % bba]} if ki == 0 else {"multi": ["ps%d" % bba]}))
                    for ki, kk in enumerate(CHK):
                        P.op("pe", lambda e, kk=kk, ki=ki, fc=fc: e.matmul(
                            ps[bbc][:, 0:ntok], lhsT=wbr[:, kk, fc * 128:(fc + 1) * 128], rhs=oT[s][:, kk, 0:ntok],
                            start=(ki == 0), stop=(ki == 3)), reads=["wbr", "oT%d" % s],
                            **({"writes": ["ps7"]} if ki == 0 else {"multi": ["ps7"]}))
                    q = fc % 2
                    P.op("act", lambda e, q=q, bga=bga: e.activation(out=sa[q][:, 0:ntok], in_=ps[bga][:, 0:ntok], func=AF.Sigmoid),
                         reads=["ps%d" % bga], writes=["sa%d" % q], osz=ntok)
                    P.op("act", lambda e, q=q, bgc=bgc: e.activation(out=sc[q][:, 0:ntok], in_=ps[bgc][:, 0:ntok], func=AF.Sigmoid),
                         reads=["ps%d" % bgc], writes=["sc%d" % q], osz=ntok)
                    P.op("dve", lambda e, q=q, bba=bba: e.tensor_tensor(sa[q][:, 0:ntok], sa[q][:, 0:ntok], ps[bba][:, 0:ntok], ALU.mult),
                         reads=["ps%d" % bba, "sa%d" % q], writes=["sa%d" % q], osz=ntok)
                    P.op("dve", lambda e, q=q: e.tensor_tensor(sc[q][:, 0:ntok], sc[q][:, 0:ntok], ps[bbc][:, 0:ntok], ALU.mult),
                         reads=["ps7", "sc%d" % q], writes=["sc%d" % q], osz=ntok)
                    P.op("dve", lambda e, q=q, fc=fc: e.tensor_tensor(mT[:, fc, 0:ntok], sa[q][:, 0:ntok], sc[q][:, 0:ntok], ALU.add),
                         reads=["sa%d" % q, "sc%d" % q], multi=["mT"], osz=ntok)
                for tt in range(ntt):
                    n = min(128, ntok - tt * 128)
                    zs = tt % 2
                    zt = z[zs]
                    for hf in range(2):
                        bank = 1 + (tt * 2 + hf) % 6
                        for k in range(8):
                            P.op("pe", lambda e, k=k, tt=tt, n=n, hf=hf, bank=bank: e.matmul(
                                ps[bank][0:n, :], lhsT=mT[:, k, tt * 128:tt * 128 + n], rhs=wout[:, k, hf * 512:(hf + 1) * 512],
                                start=(k == 0), stop=(k == 7)), reads=["wout", "mT"],
                                **({"writes": ["ps%d" % bank]} if k == 0 else {"multi": ["ps%d" % bank]}))
                        P.op("dve", lambda e, n=n, hf=hf, bank=bank, zt=zt: e.tensor_tensor(
                            zt[0:n, hf * 512:(hf + 1) * 512], ps[bank][0:n, :], gb1[0:n, hf * 512:(hf + 1) * 512], ALU.mult),
                            reads=["ps%d" % bank, "gb1"], multi=["z%d" % zs])
                    P.op("dve", lambda e, n=n, tt=tt, zt=zt, s=s: e.scalar_tensor_tensor(
                        zt[0:n, :], xin[s][0:n, tt, :], ALPHA, zt[0:n, :], ALU.mult, ALU.add),
                        reads=["xin%d" % s], multi=["z%d" % zs])
                    final_ln(zt[0:n, :], n, st, mv, lng, lnb, "z%d" % zs, "p3z")
                    r_ = dst_row + tt * 128
                    P.op("pool", lambda e, zt=zt, n=n, r_=r_: e.dma_start(out=x1buf.ap()[r_:r_ + n, :], in_=zt[0:n, :]),
                         reads=["z%d" % zs], multi=["d_x1"], chan="z%d" % zs)

            for gidx_, (kind, gq) in enumerate(groups):
                p3_group(gidx_, kind, gq)
            P.emit_phase()
        if stop_after == ("p3", l):
            break

        with contextlib.ExitStack() as ph:
            GT = G4
            NTT = (GT + 127) // 128
            stage = [sb(ph, "wstage%d" % i, [128, 512], F32) for i in range(2)]
            wup = sb(ph, "wup", [128, 8, 2 * DFF], BF16)
            wdn = sb(ph, "wdn", [128, 22, D], BF16)
            xin = [sb(ph, "xin%d" % i, [128, NTT, D], F32) for i in range(2)]
            xn = [sb(ph, "xn%d" % i, [128, NTT, D], F32) for i in range(1)]
            hT = [sb(ph, "hT%d" % i, [128, 8, GT], BF16) for i in range(1)]
            gT = sb(ph, "gT", [128, 22, GT], BF16)
            t0 = [sb(ph, "t0%d" % i, [128, GT], F32) for i in range(4)]
            sact = [sb(ph, "sact%d" % i, [128, GT], BF16) for i in range(2)]
            z = [sb(ph, "z%d" % i, [128, D], F32) for i in range(2)]
            gb2 = sb(ph, "gb2", [128, D], F32)
            lng = sb(ph, "lng", [128, D], F32)
            lnb = sb(ph, "lnb", [128, D], F32)
            carry = sb(ph, "carry", [128, NFC, 2], F32)
            cw = sb(ph, "cw", [128, NFC, 3], F32)
            cb = sb(ph, "cb", [128, NFC], F32)
            st = sb(ph, "st", [128, 4, 3], F32)
            mv = sb(ph, "mv", [128, 4], F32)

            load_weight_bf16(stage, wup, "wup", wup_d.ap()[l].rearrange("(k p) n -> p k n", p=128), 8, 2 * DFF)
            load_weight_bf16(stage, wdn, "wdn", wdn_d.ap()[l].rearrange("(k p) n -> p k n", p=128), 22, D)
            ld(gb2[:], gbc_d.ap()[l, 1], "gb2", reads=["d_gbc"])
            ld(lng[:], lnp_d.ap()[l, 2], "lnp")
            P.op("sp", lambda e: e.dma_start(out=lnb[:], in_=lnp_d.ap()[l, 3]), multi=["lnp"], chan="lnpb")
            ld(cw[:], cw_d.ap()[l], "cw")
            ld(cb[:], cb_d.ap()[l], "cb")

            groups = [(0, 2)]
            r_ = 2
            while r_ < HALF + 2:
                n_ = min(GT, HALF + 2 - r_)
                groups.append((r_, n_))
                r_ += n_
            def p4_group(gidx_, row0, ntok):
                halo = gidx_ == 0
                s = gidx_ % 2
                ntt = (ntok + 127) // 128
                if halo:
                    ld(xin[s][0:2, 0, :], x1buf.ap()[0:2, :], "xin%d" % s, reads=["d_x1"])
                else:
                    nfull = ntok // 128
                    ld(xin[s][:, 0:nfull, :], x1buf.ap()[row0:row0 + nfull * 128, :].rearrange("(t p) d -> p t d", p=128),
                       "xin%d" % s, reads=["d_x1"])
                    if ntok % 128:
                        rem = ntok % 128
                        P.op("sp", lambda e, s=s, nfull=nfull, rem=rem, row0=row0: e.dma_start(
                            out=xin[s][0:rem, nfull, :], in_=x1buf.ap()[row0 + nfull * 128: row0 + nfull * 128 + rem, :]),
                            reads=["d_x1"], multi=["xin%d" % s], chan="xin%db" % s)
                ln_to_hT(xin[s], "xin%d" % s, ntok, xn[0], "xn0", hT[0], "hT0", st, mv, l, 32, 24, [0], "p4")
                for j in range(22):
                    for half_i, f in enumerate((j, 22 + j)):
                        bank = 1 + (j * 2 + half_i) % 4
                        pres = "ps%d" % bank
                        for k in range(8):
                            P.op("pe", lambda e, k=k, f=f, bank=bank: e.matmul(
                                ps[bank][:, 0:ntok], lhsT=wup[:, k, f * 128:(f + 1) * 128], rhs=hT[0][:, k, 0:ntok],
                                start=(k == 0), stop=(k == 7)), reads=["wup", "hT0"],
                                **({"writes": [pres]} if k == 0 else {"multi": [pres]}))
                        u = ps[bank]
                        if halo:
                            P.op("dve", lambda e, f=f, u=u: e.tensor_scalar(carry[:, f, :], u[:, 0:2], hv[:, 0:1], None, ALU.mult),
                                 reads=[pres, "hv"], multi=["carry"], osz=2)
                            continue
                        ts = (j * 2 + half_i) % 4
                        tt_ = t0[ts]
                        tres = "t0%d" % ts
                        n = ntok
                        P.op("act", lambda e, f=f, u=u, tt_=tt_, n=n: e.activation(
                            out=tt_[:, 0:n], in_=u[:, 0:n], func=AF.Identity, scale=cw[:, f, 2:3], bias=cb[:, f:f + 1]),
                            reads=[pres, "cw", "cb"], writes=[tres], osz=n)
                        P.op("dve", lambda e, f=f, u=u, tt_=tt_, n=n: e.scalar_tensor_tensor(
                            tt_[:, 1:n], u[:, 0:n - 1], cw[:, f, 1:2], tt_[:, 1:n], ALU.mult, ALU.add),
                            reads=[pres, "cw", tres], writes=[tres], osz=n)
                        P.op("dve", lambda e, f=f, tt_=tt_: e.scalar_tensor_tensor(
                            tt_[:, 0:1], carry[:, f, 1:2], cw[:, f, 1:2], tt_[:, 0:1], ALU.mult, ALU.add),
                            reads=["carry", tres], writes=[tres], osz=1)
                        P.op("dve", lambda e, f=f, u=u, tt_=tt_, n=n: e.scalar_tensor_tensor(
                            tt_[:, 2:n], u[:, 0:n - 2], cw[:, f, 0:1], tt_[:, 2:n], ALU.mult, ALU.add),
                            reads=[pres, tres], writes=[tres], osz=n)
                        P.op("dve", lambda e, f=f, tt_=tt_: e.scalar_tensor_tensor(
                            tt_[:, 0:2], carry[:, f, 0:2], cw[:, f, 0:1], tt_[:, 0:2], ALU.mult, ALU.add),
                            reads=["carry", tres], writes=[tres], osz=2)
                        P.op("dve", lambda e, f=f, u=u, n=n: e.tensor_copy(carry[:, f, :], u[:, n - 2:n]),
                             reads=[pres], multi=["carry"], osz=2)
                        if half_i == 0:
                            q = j % 2
                            P.op("act", lambda e, q=q, tt_=tt_, n=n: e.activation(out=sact[q][:, 0:n], in_=tt_[:, 0:n], func=AF.Silu),
                                 reads=[tres], writes=["sact%d" % q], osz=n)
                        else:
                            q = j % 2
                            P.op("pool", lambda e, q=q, tt_=tt_, n=n, j=j: e.tensor_tensor(
                                gT[:, j, 0:n], sact[q][:, 0:n], tt_[:, 0:n], ALU.mult),
                                reads=[tres, "sact%d" % q], multi=["gT"], osz=n)
                if halo:
                    return
                for tt in range(ntt):
                    n = min(128, ntok - tt * 128)
                    zs = tt % 2
                    zt = z[zs]
                    for hf in range(2):
                        bank = 5 + (tt * 2 + hf) % 3
                        for j in range(22):
                            P.op("pe", lambda e, j=j, tt=tt, n=n, hf=hf, bank=bank: e.matmul(
                                ps[bank][0:n, :], lhsT=gT[:, j, tt * 128:tt * 128 + n], rhs=wdn[:, j, hf * 512:(hf + 1) * 512],
                                start=(j == 0), stop=(j == 21)), reads=["wdn", "gT"],
                                **({"writes": ["ps%d" % bank]} if j == 0 else {"multi": ["ps%d" % bank]}))
                        P.op("dve", lambda e, n=n, hf=hf, bank=bank, zt=zt: e.tensor_tensor(
                            zt[0:n, hf * 512:(hf + 1) * 512], ps[bank][0:n, :], gb2[0:n, hf * 512:(hf + 1) * 512], ALU.mult),
                            reads=["ps%d" % bank, "gb2"], multi=["z%d" % zs])
                    P.op("dve", lambda e, n=n, tt=tt, zt=zt, s=s: e.scalar_tensor_tensor(
                        zt[0:n, :], xin[s][0:n, tt, :], ALPHA, zt[0:n, :], ALU.mult, ALU.add),
                        reads=["xin%d" % s], multi=["z%d" % zs])
                    final_ln(zt[0:n, :], n, st, mv, lng, lnb, "z%d" % zs, "p4z")
                    rr_ = row0 - 2 + tt * 128
                    if l == n_layers - 1:
                        P.op("pool", lambda e, zt=zt, n=n, rr_=rr_: e.dma_start(out=out_d.ap()[rr_:rr_ + n, :], in_=zt[0:n, :]),
                             reads=["z%d" % zs], multi=["d_out"], chan="z%d" % zs)
                    else:
                        P.op("pool", lambda e, zt=zt, n=n, rr_=rr_: e.dma_start(out=xs.ap()[rr_:rr_ + n, :], in_=zt[0:n, :]),
                             reads=["z%d" % zs], multi=["d_xs"], chan="z%d" % zs)
            for gidx_, (row0, ntok) in enumerate(groups):
                p4_group(gidx_, row0, ntok)
            if l < n_layers - 1:
                for j_ in range(8):
                    P.op("pool", lambda e, j_=j_: e.collective_compute(
                        "AllGather", ALU.bypass, replica_groups=PAIRS,
                        ins=[xs.ap()[j_ * 512:(j_ + 1) * 512, :]], outs=[xg.ap()[j_ * 1024:(j_ + 1) * 1024, :]]),
                        reads=["d_xs"], multi=["d_xg"], chan="cc_xg", inc=1)
            P.emit_phase(final_wait=True)

    if debug:
        with contextlib.ExitStack() as ph:
            for name in debug:
                src = {"qaT": qaT, "kaT": kaT, "qcT": qcT, "kcT": kcT, "va": va_d, "vc": vc_d, "gref": gref_d,
                       "osend": osend, "og": og, "x1buf": x1buf, "xs": xs, "xg": xg, "gbc": gbc_d}[name]
                o = nc.dram_tensor("dbg_" + name, list(src.shape), src.dtype, kind="ExternalOutput")
                dbg[name] = o
                P.op("sp", lambda e, o=o, src=src: e.dma_start(out=o.ap(), in_=src.ap()),
                     reads=["d_" + name, "d_" + src.name], multi=["dbgout"], chan="dbg_" + name)
            if "G_sb" in debug or True:
                o = nc.dram_tensor("dbg_G", [128, 256], F32, kind="ExternalOutput")
                P.op("sp", lambda e, o=o: e.dma_start(out=o.ap(), in_=G_sb[:].rearrange("p h b -> p (h b)")),
                     reads=["G_sb"], multi=["dbgout"], chan="dbg_G")
                o2 = nc.dram_tensor("dbg_modc", [128, NL * 48], F32, kind="ExternalOutput")
                P.op("sp", lambda e, o2=o2: e.dma_start(out=o2.ap(), in_=modc[:].rearrange("p l c -> p (l c)")),
                     reads=["modc"], multi=["dbgout"], chan="dbg_modc")
            P.emit_phase(final_wait=True)
    es.close()
    _NC_CACHE["trace"] = P.trace
    return nc


def _consts():
    bf = ml_dtypes.bfloat16
    p = np.arange(128)
    ident = np.eye(128, dtype=np.float32)
    tri = (p[:, None] <= p[None, :]).astype(np.float32)
    mask01 = tri.astype(bf)
    q = np.arange(64)
    tri64s = (q[:, None] < q[None, :]).astype(np.float32)
    tri64i = (q[:, None] <= q[None, :]).astype(np.float32)
    E = (np.arange(S)[None, :] // 128 == q[:, None]).astype(np.float32).astype(bf)
    m = np.arange(TW) - 384
    qc = np.floor_divide(m, 64)[None, :]
    kc = (p // 64)[:, None]
    vis = ((qc - kc >= 0) & (qc - kc <= 8)).astype(np.float32)
    relidx = np.clip(m[None, :] - p[:, None], -128, 128) + 128
    return dict(ident=ident, mask01=mask01, tri=tri, tri64s=tri64s, tri64i=tri64i, Emat=E, vis=vis), relidx


def prep_inputs(x, c, w_in, b_f, rel_bias, w_br_fox, w_br_chunk, w_out, w_up, conv_w, conv_b, w_down,
                w_ada, b_ada, ln1_g, ln1_b, ln2_g, ln2_b):
    f32 = np.float32
    A = lambda a: np.ascontiguousarray(np.asarray(a), dtype=f32)
    x, c, w_in, b_f, rel_bias = A(x), A(c), A(w_in), A(b_f), A(rel_bias)
    w_br_fox, w_br_chunk, w_out, w_up = A(w_br_fox), A(w_br_chunk), A(w_out), A(w_up)
    conv_w, conv_b, w_down, w_ada, b_ada = A(conv_w), A(conv_b), A(w_down), A(w_ada), A(b_ada)
    ln1_g, ln1_b, ln2_g, ln2_b = A(ln1_g), A(ln1_b), A(ln2_g), A(ln2_b)
    consts, relidx = _consts()
    shared = dict(consts)
    shared["wg"] = np.ascontiguousarray(w_in[:, :, 3080:5128])
    shared["wout"] = w_out
    shared["wup"] = w_up
    shared["wdn"] = w_down
    shared["wada"] = w_ada
    shared["cw"] = np.ascontiguousarray(conv_w.reshape(NL, 3, NFC, 128).transpose(0, 3, 2, 1))
    shared["cb"] = np.ascontiguousarray(conv_b.reshape(NL, NFC, 128).transpose(0, 2, 1))
    shared["badac"] = np.ascontiguousarray(b_ada.reshape(NL, 48, 128).transpose(0, 2, 1))
    shared["badar"] = np.ascontiguousarray(np.stack([b_ada[:, 2 * D:3 * D], b_ada[:, 5 * D:6 * D]], axis=1))
    lnp = np.stack([ln1_g, ln1_b, ln2_g, ln2_b], axis=1)
    shared["lnp"] = np.ascontiguousarray(np.broadcast_to(lnp[:, :, None, :], (NL, 4, 128, D)))
    brorder = [("f", 0), ("f", 2), ("f", 1), ("f", 3), ("c", 0), ("c", 2), ("c", 1), ("c", 3)]
    shared["wbr"] = np.ascontiguousarray(np.concatenate(
        [(w_br_fox if t == "f" else w_br_chunk)[:, i * 128:(i + 1) * 128, :] for t, i in brorder], axis=1))
    maps = []
    for core in range(8):
        b, r = core // 2, core % 2
        m = dict(shared)
        m["xfull"] = x[b]
        xm = np.zeros((HALF + 2, D), f32)
        xm[2:] = x[b, r * HALF:(r + 1) * HALF]
        if r == 1:
            xm[0:2] = x[b, HALF - 2:HALF]
        m["xmine"] = xm
        m["hv"] = np.full((128, 1), float(r), f32)
        m["ccol"] = np.ascontiguousarray(c[b].reshape(8, 128).T)
        cs = lambda base: slice(base + 256 * r, base + 256 * r + 256)
        m["wqk"] = np.ascontiguousarray(np.concatenate(
            [w_in[:, :, cs(0)], w_in[:, :, cs(512)], w_in[:, :, cs(1544)], w_in[:, :, cs(2056)]], axis=2))
        m["wv"] = np.ascontiguousarray(np.concatenate([w_in[:, :, cs(1024)], w_in[:, :, cs(2568)]], axis=2))
        m["wf"] = np.ascontiguousarray(w_in[:, :, 1536 + 4 * r:1536 + 4 * r + 4])
        m["bfbc"] = np.ascontiguousarray(np.broadcast_to(b_f[:, None, 4 * r:4 * r + 4], (NL, 128, 4)))
        m["ttab"] = np.ascontiguousarray(rel_bias[:, 4 * r:4 * r + 4][:, :, relidx])
        maps.append(m)
    return maps


_NC_CACHE = {}


def kernel(**inputs):
    maps = prep_inputs(**inputs)
    if "nc" not in _NC_CACHE:
        _NC_CACHE["nc"] = build_program()
    nc = _NC_CACHE["nc"]
    res = run_bass_kernel_spmd(nc, maps, core_ids=list(range(8)))
    out = np.empty((4, S, D), np.float32)
    for core in range(8):
        b, r = core // 2, core % 2
        out[b, r * HALF:(r + 1) * HALF] = np.asarray(res.results[core]["out"], dtype=np.float32)
    return out
```
